# Optimizing a Trainium2 kernel written in Bass

```python
import math
import jax, jax.numpy as jnp
from jax import lax
import numpy as np

D_MODEL = 1024
BATCH = 4
SEQ = 4096
DEPTH = 4
DEC_BATCH = 32
DEC_SEQ = 16
PAST_LEN = 4096

CHUNK = 64
N_META = 16
Q_BLOCK = 128
EPS = 1e-6
D_FF = 2816

MLA_HEADS = 4
MLA_NOPE = 64
MLA_ROPE = 32
MLA_QK = MLA_NOPE + MLA_ROPE
MLA_V = 64
MLA_Q_LORA = 256
MLA_KV_LORA = 128
ROPE_THETA = 10000.0
LRU_HEADS = 4
LRU_WIDTH = 256
LRU_HEAD_DIM = LRU_WIDTH // LRU_HEADS
LRU_C = 8.0
CONV_W = 4
GDN_HEADS = 4
GDN_DK = 64
GDN_DV = 64
GDN_CHUNK = 64
GDN_QKV = GDN_HEADS * (2 * GDN_DK + GDN_DV)
RWKV_HEADS = 4
RWKV_HD = 64
RWKV_WIDTH = RWKV_HEADS * RWKV_HD
RWKV_DECAY_LORA = 64
RWKV_A_LORA = 64
RWKV_G_LORA = 128
RWKV_IN = 3 * RWKV_WIDTH + RWKV_DECAY_LORA + RWKV_A_LORA + RWKV_G_LORA
RWKV_GN_EPS = 64e-5

A_IN = MLA_Q_LORA + MLA_KV_LORA + MLA_ROPE
B_IN = 2 * LRU_WIDTH
C_IN = GDN_QKV + GDN_HEADS * GDN_DV + 2 * GDN_HEADS
IN_WIDTH = A_IN + B_IN + C_IN + RWKV_IN
MIX_WIDTH = MLA_HEADS * MLA_V + LRU_WIDTH + GDN_HEADS * GDN_DV + RWKV_WIDTH

kernel_name = "hybrid_streaming_encoder_step"

F32 = jnp.float32


def rms_norm(x, g):
    xf = x.astype(F32)
    y = xf * lax.rsqrt(jnp.mean(xf * xf, axis=-1, keepdims=True) + EPS)
    return (y * g.astype(F32)).astype(x.dtype)


def l2_normalize(x):
    xf = x.astype(F32)
    return xf * lax.rsqrt(jnp.sum(xf * xf, axis=-1, keepdims=True) + EPS)


def swiglu(x, w1, w2):
    gate, up = jnp.split(x @ w1, 2, axis=-1)
    return (jax.nn.silu(gate) * up) @ w2


def causal_conv(x, buf, w):
    T = x.shape[1]
    xp = jnp.concatenate([buf.astype(x.dtype), x], axis=1)
    y = sum(xp[:, j:j + T] * w[j] for j in range(CONV_W))
    return y, xp[:, -(CONV_W - 1):]


def rope_cos_sin(pos):
    inv = ROPE_THETA ** (-jnp.arange(0, MLA_ROPE, 2, dtype=F32) / MLA_ROPE)
    ang = pos.astype(F32)[:, None] * inv[None, :]
    return jnp.cos(ang), jnp.sin(ang)


def apply_rope(x, cos, sin):
    x1, x2 = jnp.split(x.astype(F32), 2, axis=-1)
    return jnp.concatenate([x1 * cos - x2 * sin, x1 * sin + x2 * cos], axis=-1).astype(x.dtype)


def mla_qkv(cols, pos, lp):
    B, T, _ = cols.shape
    c_q, c_kv, k_pe = jnp.split(cols, [MLA_Q_LORA, MLA_Q_LORA + MLA_KV_LORA], axis=-1)
    cos, sin = rope_cos_sin(pos)
    q = (rms_norm(c_q, lp["mla_q_a_norm"]) @ lp["mla_w_uq"]).reshape(B, T, MLA_HEADS, MLA_QK)
    q = jnp.concatenate([q[..., :MLA_NOPE], apply_rope(q[..., MLA_NOPE:], cos[:, None], sin[:, None])], axis=-1)
    q = rms_norm(q, lp["mla_q_norm"])
    ckv = rms_norm(c_kv, lp["mla_kv_a_norm"])
    kpe = apply_rope(k_pe, cos, sin)
    return q, ckv, kpe


def mla_keys(ckv, kpe, lp):
    B, S, _ = ckv.shape
    kv = (ckv @ lp["mla_w_ukv"]).reshape(B, S, MLA_HEADS, MLA_NOPE + MLA_V)
    k = jnp.concatenate([kv[..., :MLA_NOPE], jnp.broadcast_to(kpe[:, :, None, :], (B, S, MLA_HEADS, MLA_ROPE))], axis=-1)
    return rms_norm(k, lp["mla_k_norm"]), kv[..., MLA_NOPE:]


def attend(q, k, v, mask):
    s = jnp.einsum("bqhe,bshe->bhqs", q.astype(F32), k.astype(F32)) * (1.0 / math.sqrt(MLA_QK))
    if mask is not None:
        s = jnp.where(mask, s, -1e30)
    p = jax.nn.softmax(s, axis=-1)
    return jnp.einsum("bhqs,bshd->bqhd", p, v.astype(F32)).astype(v.dtype)


def mla_prompt(cols, lp):
    B, L, _ = cols.shape
    q, ckv, kpe = mla_qkv(cols, jnp.arange(L), lp)
    k, v = mla_keys(ckv, kpe, lp)
    o_meta = attend(q[:, :N_META], k[:, :N_META], v[:, :N_META], None)
    n_frames = L - N_META
    nb = n_frames // Q_BLOCK
    frame_chunk = (jnp.arange(n_frames) // CHUNK).astype(jnp.int32)
    key_chunk = jnp.concatenate([jnp.full((N_META,), -1, jnp.int32), frame_chunk])
    qf = jnp.moveaxis(q[:, N_META:].reshape(B, nb, Q_BLOCK, MLA_HEADS, MLA_QK), 1, 0)
    cq = frame_chunk.reshape(nb, Q_BLOCK)

    def block(args):
        qb, cqb = args
        return attend(qb, k, v, key_chunk[None, :] <= cqb[:, None])

    o = lax.map(block, (qf, cq))
    o = jnp.moveaxis(o, 0, 1).reshape(B, n_frames, MLA_HEADS * MLA_V)
    return jnp.concatenate([o_meta.reshape(B, N_META, MLA_HEADS * MLA_V), o], axis=1), ckv, kpe


def mla_sample(cols, cache_ckv, cache_kpe, lp):
    B, T, _ = cols.shape
    past = cache_ckv.shape[1]
    q, ckv, kpe = mla_qkv(cols, past + jnp.arange(T), lp)
    k, v = mla_keys(jnp.concatenate([cache_ckv.astype(ckv.dtype), ckv], axis=1),
                    jnp.concatenate([cache_kpe.astype(kpe.dtype), kpe], axis=1), lp)
    o = attend(q, k, v, None)
    return o.reshape(B, T, MLA_HEADS * MLA_V), ckv, kpe


def linear_scan(a, b, h0):
    b = b.at[:, 0].add(a[:, 0] * h0)

    def comb(l, r):
        return l[0] * r[0], r[0] * l[1] + r[1]

    _, h = lax.associative_scan(comb, (a, b), axis=1)
    return h


def rglru(cols, conv_buf, h0, lp):
    B, T, _ = cols.shape
    xb, gb = jnp.split(cols, 2, axis=-1)
    xc, new_buf = causal_conv(xb, conv_buf, lp["lru_conv_w"])
    xc = xc + lp["lru_conv_b"]
    xh = xc.reshape(B, T, LRU_HEADS, LRU_HEAD_DIM)
    r = jax.nn.sigmoid(jnp.einsum("bthi,hij->bthj", xh, lp["lru_wa"]).reshape(B, T, LRU_WIDTH) + lp["lru_ba"])
    i = jax.nn.sigmoid(jnp.einsum("bthi,hij->bthj", xh, lp["lru_wx"]).reshape(B, T, LRU_WIDTH) + lp["lru_bx"])
    log_a = -LRU_C * r.astype(F32) * jax.nn.softplus(-lp["lru_lambda"].astype(F32))
    a = jnp.exp(log_a)
    b = jnp.sqrt(-jnp.expm1(2.0 * log_a)) * (i * xc).astype(F32)
    h = linear_scan(a, b, h0.astype(F32))
    y = h.astype(cols.dtype) * jax.nn.gelu(gb)
    return y, new_buf, h[:, -1]


def chunk_gated_delta(q, k, v, g, beta, S0):
    B, T, H, K = k.shape
    C = GDN_CHUNK
    N = -(-T // C)
    pad = N * C - T

    def prep(x):
        x = jnp.pad(x, [(0, 0), (0, pad)] + [(0, 0)] * (x.ndim - 2))
        x = jnp.moveaxis(x, 2, 1)
        return x.reshape((B, H, N, C) + x.shape[3:])

    q, k, v, g, beta = (prep(t) for t in (q, k, v, g, beta))
    gc = jnp.cumsum(g, axis=-1)
    tril = jnp.tril(jnp.ones((C, C), bool))
    strict = jnp.tril(jnp.ones((C, C), bool), -1)
    diff = gc[..., :, None] - gc[..., None, :]
    decay = jnp.where(tril, jnp.exp(jnp.where(tril, diff, 0.0)), 0.0)
    kb = k * beta[..., None]
    lmat = jnp.where(strict, jnp.einsum("bhnik,bhnjk->bhnij", kb, k) * decay, 0.0)
    eye = jnp.eye(C, dtype=F32)
    tinv = lax.linalg.triangular_solve(eye + lmat, jnp.broadcast_to(eye, lmat.shape),
                                       left_side=True, lower=True, unit_diagonal=True)
    u = tinv @ (v * beta[..., None])
    w = tinv @ (kb * jnp.exp(gc)[..., None])
    attn = jnp.einsum("bhnik,bhnjk->bhnij", q, k) * decay

    def step(S, xs):
        qc, kc, gcc, uc, wc, ac = xs
        v_new = uc - wc @ S
        o = (qc * jnp.exp(gcc)[..., None]) @ S + ac @ v_new
        glast = gcc[..., -1:]
        S = S * jnp.exp(glast)[..., None] + jnp.swapaxes(kc * jnp.exp(glast - gcc)[..., None], -1, -2) @ v_new
        return S, o

    xs = tuple(jnp.moveaxis(t, 2, 0) for t in (q, k, gc, u, w, attn))
    S, o = lax.scan(step, S0, xs)
    o = jnp.moveaxis(o, 0, 2).reshape(B, H, N * C, -1)[:, :, :T]
    return jnp.moveaxis(o, 1, 2), S


def gated_delta(cols, conv_buf, S0, lp):
    B, T, _ = cols.shape
    qkv, z, a_in, b_in = jnp.split(cols, [GDN_QKV, GDN_QKV + GDN_HEADS * GDN_DV,
                                          GDN_QKV + GDN_HEADS * GDN_DV + GDN_HEADS], axis=-1)
    qkv, new_buf = causal_conv(qkv, conv_buf, lp["gdn_conv_w"])
    qkv = jax.nn.silu(qkv)
    q, k, v = jnp.split(qkv, [GDN_HEADS * GDN_DK, 2 * GDN_HEADS * GDN_DK], axis=-1)
    q = l2_normalize(q.reshape(B, T, GDN_HEADS, GDN_DK)) * (GDN_DK ** -0.5)
    k = l2_normalize(k.reshape(B, T, GDN_HEADS, GDN_DK))
    v = v.reshape(B, T, GDN_HEADS, GDN_DV).astype(F32)
    beta = jax.nn.sigmoid(b_in.astype(F32))
    g = -jnp.exp(lp["gdn_a_log"].astype(F32)) * jax.nn.softplus(a_in.astype(F32) + lp["gdn_dt_bias"].astype(F32))
    o, S = chunk_gated_delta(q, k, v, g, beta, S0.astype(F32))
    o = rms_norm(o, lp["gdn_o_norm"]) * jax.nn.silu(z.reshape(B, T, GDN_HEADS, GDN_DV).astype(F32))
    return o.reshape(B, T, GDN_HEADS * GDN_DV).astype(cols.dtype), new_buf, S


def rwkv7(cols, prev, S0, lp):
    B, T, _ = cols.shape
    shifted = jnp.concatenate([prev[:, None].astype(cols.dtype), cols[:, :-1]], axis=1)
    xm = cols + (shifted - cols) * lp["rwkv_mu"]
    W = RWKV_WIDTH
    r, k, v, w_lo, a_lo, g_lo = jnp.split(
        xm, [W, 2 * W, 3 * W, 3 * W + RWKV_DECAY_LORA, 3 * W + RWKV_DECAY_LORA + RWKV_A_LORA], axis=-1)
    hs = lambda t: t.reshape(B, T, RWKV_HEADS, RWKV_HD)
    hp = lambda p: p.reshape(RWKV_HEADS, RWKV_HD).astype(F32)
    w = jnp.exp(-0.606531 * jax.nn.sigmoid((lp["rwkv_w0"] + jnp.tanh(w_lo) @ lp["rwkv_w_b"]).astype(F32)))
    a = hs(jax.nn.sigmoid((lp["rwkv_a0"] + a_lo @ lp["rwkv_a_b"]).astype(F32)))
    g = jax.nn.sigmoid(g_lo) @ lp["rwkv_g_b"]
    kk = l2_normalize(hs(k * lp["rwkv_k_k"]))
    k = hs(k).astype(F32) * (1.0 + (a - 1.0) * hp(lp["rwkv_k_a"]))
    r = hs(r).astype(F32)
    v = hs(v).astype(F32)
    w = hs(w)

    def step(S, xs):
        r_t, w_t, k_t, v_t, kk_t, a_t = xs
        sa = jnp.einsum("bhvk,bhk->bhv", S, -kk_t)
        S = S * w_t[:, :, None, :] + sa[..., None] * (kk_t * a_t)[:, :, None, :] + v_t[..., None] * k_t[:, :, None, :]
        return S, jnp.einsum("bhvk,bhk->bhv", S, r_t)

    xs = tuple(jnp.moveaxis(t, 1, 0) for t in (r, w, k, v, kk, a))
    S, o = lax.scan(step, S0.astype(F32), xs)
    o = jnp.moveaxis(o, 0, 1)
    mu = jnp.mean(o, axis=-1, keepdims=True)
    var = jnp.mean(jnp.square(o - mu), axis=-1, keepdims=True)
    o = (o - mu) * lax.rsqrt(var + RWKV_GN_EPS) * hp(lp["rwkv_ln_w"]) + hp(lp["rwkv_ln_b"])
    o = o + jnp.sum(r * k * lp["rwkv_r_k"].astype(F32), axis=-1, keepdims=True) * v
    y = o.reshape(B, T, W) * g.astype(F32)
    return y.astype(cols.dtype), cols[:, -1], S


def token_mixing(h, lp, st, sample):
    cols = h @ lp["w_in"]
    ca, cb, cc, cd = jnp.split(cols, [A_IN, A_IN + B_IN, A_IN + B_IN + C_IN], axis=-1)
    if sample:
        ya, ckv, kpe = mla_sample(ca, st["mla_ckv"], st["mla_kpe"], lp)
    else:
        ya, ckv, kpe = mla_prompt(ca, lp)
    yb, lru_conv, lru_h = rglru(cb, st["lru_conv"], st["lru_h"], lp)
    yc, gdn_conv, gdn_s = gated_delta(cc, st["gdn_conv"], st["gdn_s"], lp)
    yd, shift, rwkv_s = rwkv7(cd, st["rwkv_shift"], st["rwkv_s"], lp)
    y = jnp.concatenate([ya, yb, yc, yd], axis=-1) @ lp["w_out"]
    return y, (ckv, kpe, lru_conv, lru_h, gdn_conv, gdn_s, shift, rwkv_s)


def run_trunk(x, layers, states, sample, final_norm):
    new = []
    for l in range(DEPTH):
        lp = {name: w[l] for name, w in layers.items()}
        st = {name: s[l] for name, s in states.items()}
        x = x + 0.5 * swiglu(rms_norm(x, lp["norm_ffn1"]), lp["ffn1_w1"], lp["ffn1_w2"])
        y, ns = token_mixing(rms_norm(x, lp["norm_mix"]), lp, st, sample)
        x = x + y
        x = x + 0.5 * swiglu(rms_norm(x, lp["norm_ffn2"]), lp["ffn2_w1"], lp["ffn2_w2"])
        new.append(ns)
    stacked = [jnp.stack(t) for t in zip(*new)]
    return rms_norm(x, final_norm), stacked


def setup_inputs(seed: int = 0) -> dict:
    key = jax.random.key(seed)
    ks = iter(jax.random.split(key, 64))

    def nrm(shape, scale=1.0):
        return scale * jax.random.normal(next(ks), shape, F32)

    def gain(shape):
        return 1.0 + 0.05 * jax.random.normal(next(ks), shape, F32)

    def unif(shape, lo, hi):
        return jax.random.uniform(next(ks), shape, F32, lo, hi)

    L = DEPTH
    inp = {}
    inp["x_prompt"] = nrm((BATCH, SEQ, D_MODEL))
    inp["x_sample"] = nrm((DEC_BATCH, DEC_SEQ, D_MODEL))
    inp["cache_mla_ckv"] = nrm((L, DEC_BATCH, PAST_LEN, MLA_KV_LORA))
    inp["cache_mla_kpe"] = nrm((L, DEC_BATCH, PAST_LEN, MLA_ROPE))
    inp["state_lru_conv"] = nrm((L, DEC_BATCH, CONV_W - 1, LRU_WIDTH))
    inp["state_lru_h"] = nrm((L, DEC_BATCH, LRU_WIDTH), 0.5)
    inp["state_gdn_conv"] = nrm((L, DEC_BATCH, CONV_W - 1, GDN_QKV))
    inp["state_gdn_s"] = nrm((L, DEC_BATCH, GDN_HEADS, GDN_DK, GDN_DV), 0.3)
    inp["state_rwkv_shift"] = nrm((L, DEC_BATCH, RWKV_IN))
    inp["state_rwkv_s"] = nrm((L, DEC_BATCH, RWKV_HEADS, RWKV_HD, RWKV_HD), 0.3)
    inp["meta_tokens"] = nrm((N_META, D_MODEL))
    inp["norm_ffn1"] = gain((L, D_MODEL))
    inp["ffn1_w1"] = nrm((L, D_MODEL, 2 * D_FF), D_MODEL ** -0.5)
    inp["ffn1_w2"] = nrm((L, D_FF, D_MODEL), D_FF ** -0.5)
    inp["norm_mix"] = gain((L, D_MODEL))
    inp["w_in"] = nrm((L, D_MODEL, IN_WIDTH), D_MODEL ** -0.5)
    inp["mla_q_a_norm"] = gain((L, MLA_Q_LORA))
    inp["mla_w_uq"] = nrm((L, MLA_Q_LORA, MLA_HEADS * MLA_QK), MLA_Q_LORA ** -0.5)
    inp["mla_kv_a_norm"] = gain((L, MLA_KV_LORA))
    inp["mla_w_ukv"] = nrm((L, MLA_KV_LORA, MLA_HEADS * (MLA_NOPE + MLA_V)), MLA_KV_LORA ** -0.5)
    inp["mla_q_norm"] = gain((L, MLA_QK))
    inp["mla_k_norm"] = gain((L, MLA_QK))
    inp["lru_conv_w"] = nrm((L, CONV_W, LRU_WIDTH), CONV_W ** -0.5)
    inp["lru_conv_b"] = nrm((L, LRU_WIDTH), 0.05)
    inp["lru_wa"] = nrm((L, LRU_HEADS, LRU_HEAD_DIM, LRU_HEAD_DIM), LRU_HEAD_DIM ** -0.5)
    inp["lru_ba"] = nrm((L, LRU_WIDTH), 0.1)
    inp["lru_wx"] = nrm((L, LRU_HEADS, LRU_HEAD_DIM, LRU_HEAD_DIM), LRU_HEAD_DIM ** -0.5)
    inp["lru_bx"] = nrm((L, LRU_WIDTH), 0.1)
    s = unif((L, LRU_WIDTH), 0.9, 0.999) ** (1.0 / LRU_C)
    inp["lru_lambda"] = jnp.log(s) - jnp.log1p(-s)
    inp["gdn_conv_w"] = nrm((L, CONV_W, GDN_QKV), CONV_W ** -0.5)
    inp["gdn_a_log"] = jnp.log(unif((L, GDN_HEADS), 1.0, 16.0))
    dt = jnp.exp(unif((L, GDN_HEADS), math.log(1e-3), math.log(1e-1)))
    inp["gdn_dt_bias"] = dt + jnp.log(-jnp.expm1(-dt))
    inp["gdn_o_norm"] = gain((L, GDN_DV))
    inp["rwkv_mu"] = unif((L, RWKV_IN), 0.0, 1.0)
    inp["rwkv_w0"] = nrm((L, RWKV_WIDTH), 0.5)
    inp["rwkv_w_b"] = nrm((L, RWKV_DECAY_LORA, RWKV_WIDTH), 0.5 * RWKV_DECAY_LORA ** -0.5)
    inp["rwkv_a0"] = nrm((L, RWKV_WIDTH), 0.1)
    inp["rwkv_a_b"] = nrm((L, RWKV_A_LORA, RWKV_WIDTH), 0.5 * RWKV_A_LORA ** -0.5)
    inp["rwkv_g_b"] = nrm((L, RWKV_G_LORA, RWKV_WIDTH), RWKV_G_LORA ** -0.5)
    inp["rwkv_k_k"] = gain((L, RWKV_WIDTH))
    inp["rwkv_k_a"] = gain((L, RWKV_WIDTH))
    inp["rwkv_r_k"] = nrm((L, RWKV_HEADS, RWKV_HD), 0.1)
    inp["rwkv_ln_w"] = gain((L, RWKV_WIDTH))
    inp["rwkv_ln_b"] = nrm((L, RWKV_WIDTH), 0.02)
    inp["w_out"] = nrm((L, MIX_WIDTH, D_MODEL), MIX_WIDTH ** -0.5)
    inp["norm_ffn2"] = gain((L, D_MODEL))
    inp["ffn2_w1"] = nrm((L, D_MODEL, 2 * D_FF), D_MODEL ** -0.5)
    inp["ffn2_w2"] = nrm((L, D_FF, D_MODEL), D_FF ** -0.5)
    inp["final_norm"] = gain((D_MODEL,))
    return inp


def reference(x_prompt, x_sample, cache_mla_ckv, cache_mla_kpe, state_lru_conv, state_lru_h,
              state_gdn_conv, state_gdn_s, state_rwkv_shift, state_rwkv_s, meta_tokens,
              norm_ffn1, ffn1_w1, ffn1_w2, norm_mix, w_in, mla_q_a_norm, mla_w_uq, mla_kv_a_norm,
              mla_w_ukv, mla_q_norm, mla_k_norm, lru_conv_w, lru_conv_b, lru_wa, lru_ba, lru_wx, lru_bx,
              lru_lambda, gdn_conv_w, gdn_a_log, gdn_dt_bias, gdn_o_norm, rwkv_mu, rwkv_w0, rwkv_w_b,
              rwkv_a0, rwkv_a_b, rwkv_g_b, rwkv_k_k, rwkv_k_a, rwkv_r_k, rwkv_ln_w, rwkv_ln_b, w_out,
              norm_ffn2, ffn2_w1, ffn2_w2, final_norm):
    layers = dict(norm_ffn1=norm_ffn1, ffn1_w1=ffn1_w1, ffn1_w2=ffn1_w2, norm_mix=norm_mix, w_in=w_in,
                  mla_q_a_norm=mla_q_a_norm, mla_w_uq=mla_w_uq, mla_kv_a_norm=mla_kv_a_norm,
                  mla_w_ukv=mla_w_ukv, mla_q_norm=mla_q_norm, mla_k_norm=mla_k_norm,
                  lru_conv_w=lru_conv_w, lru_conv_b=lru_conv_b, lru_wa=lru_wa, lru_ba=lru_ba,
                  lru_wx=lru_wx, lru_bx=lru_bx, lru_lambda=lru_lambda, gdn_conv_w=gdn_conv_w,
                  gdn_a_log=gdn_a_log, gdn_dt_bias=gdn_dt_bias, gdn_o_norm=gdn_o_norm, rwkv_mu=rwkv_mu,
                  rwkv_w0=rwkv_w0, rwkv_w_b=rwkv_w_b, rwkv_a0=rwkv_a0, rwkv_a_b=rwkv_a_b, rwkv_g_b=rwkv_g_b,
                  rwkv_k_k=rwkv_k_k, rwkv_k_a=rwkv_k_a, rwkv_r_k=rwkv_r_k, rwkv_ln_w=rwkv_ln_w,
                  rwkv_ln_b=rwkv_ln_b, w_out=w_out, norm_ffn2=norm_ffn2, ffn2_w1=ffn2_w1, ffn2_w2=ffn2_w2)
    B = x_prompt.shape[0]
    dt = x_prompt.dtype
    zero_states = dict(
        lru_conv=jnp.zeros((DEPTH, B, CONV_W - 1, LRU_WIDTH), dt),
        lru_h=jnp.zeros((DEPTH, B, LRU_WIDTH), F32),
        gdn_conv=jnp.zeros((DEPTH, B, CONV_W - 1, GDN_QKV), dt),
        gdn_s=jnp.zeros((DEPTH, B, GDN_HEADS, GDN_DK, GDN_DV), F32),
        rwkv_shift=jnp.zeros((DEPTH, B, RWKV_IN), dt),
        rwkv_s=jnp.zeros((DEPTH, B, RWKV_HEADS, RWKV_HD, RWKV_HD), F32))
    x0 = jnp.concatenate([jnp.broadcast_to(meta_tokens.astype(dt)[None], (B, N_META, D_MODEL)), x_prompt], axis=1)
    yp, p_new = run_trunk(x0, layers, zero_states, False, final_norm)
    sample_states = dict(mla_ckv=cache_mla_ckv, mla_kpe=cache_mla_kpe, lru_conv=state_lru_conv,
                         lru_h=state_lru_h, gdn_conv=state_gdn_conv, gdn_s=state_gdn_s,
                         rwkv_shift=state_rwkv_shift, rwkv_s=state_rwkv_s)
    ys, s_new = run_trunk(x_sample, layers, sample_states, True, final_norm)
    p_ckv, p_kpe, p_lru_conv, p_lru_h, p_gdn_conv, p_gdn_s, p_rwkv_shift, p_rwkv_s = p_new
    s_ckv, s_kpe, s_lru_conv, s_lru_h, s_gdn_conv, s_gdn_s, s_rwkv_shift, s_rwkv_s = s_new
    return (yp[:, N_META:], ys, p_ckv, p_kpe, p_lru_conv, p_lru_h, p_gdn_conv, p_gdn_s, p_rwkv_shift, p_rwkv_s,
            s_ckv, s_kpe, s_lru_conv, s_lru_h, s_gdn_conv, s_gdn_s, s_rwkv_shift, s_rwkv_s)
```

```python
import contextlib
import math
import numpy as np
import concourse.bass as bass
import concourse.mybir as mybir
from concourse.bass_utils import run_bass_kernel_spmd

F32 = mybir.dt.float32
BF16 = mybir.dt.bfloat16
AF = mybir.ActivationFunctionType
ALU = mybir.AluOpType
AX = mybir.AxisListType

D = 1024
DFF = 2816
DEPTH = 4
IN_W = 2984
EPS = 1e-6
NCOL = 72
MIXW = 1024

STREAM = {"pe": "pe", "act": "act", "dve": "dve", "pool": "pool", "sp": "sp", "actq": "act", "poolq": "pool"}


class _Cut(Exception):
    pass


class Op:
    __slots__ = ("eng", "stream", "fn", "deps", "is_dma", "semkey", "dma_cnt", "sig", "sigcnt", "sidx", "gidx")


class Prog:
    def __init__(self, nc):
        self.nc = nc
        self.ops = []
        self.streams = {s: [] for s in ("pe", "act", "dve", "pool", "sp")}
        self.last_write = {}
        self.readers = {}
        self.dma_count = {}
        self.last_dma = {}
        self.seen = {s: {} for s in self.streams}
        self.bar = None
        self.excl = set()

    def op(self, eng, fn, reads=(), writes=(), dma=False, semkey=None, extra_deps=()):
        o = Op()
        o.eng = eng
        o.stream = STREAM[eng]
        o.fn = fn
        o.is_dma = dma
        o.sig = False
        o.gidx = len(self.ops)
        deps = [(d, "raw") for d in extra_deps]
        if self.bar is not None:
            deps.append((self.bar, "raw"))
        for k in reads:
            w = self.last_write.get(k)
            if w is not None:
                deps.append((w, "raw"))
            if k in self.excl:
                for r in self.readers.get(k, ()):
                    if r.stream != STREAM[eng]:
                        deps.append((r, "rar"))
        for k in writes:
            w = self.last_write.get(k)
            if w is not None:
                deps.append((w, "waw"))
            for r in self.readers.get(k, ()):
                deps.append((r, "war"))
        deps = [((self.last_dma[d.semkey], kind) if d.is_dma else (d, kind)) for d, kind in deps]
        if dma:
            assert semkey is not None
            o.semkey = semkey
            self.dma_count[semkey] = self.dma_count.get(semkey, 0) + 1
            o.dma_cnt = self.dma_count[semkey]
            self.last_dma[semkey] = o
        st = o.stream
        seen = self.seen[st]
        final = []
        for d, kind in deps:
            if d.is_dma:
                if seen.get(d.semkey, 0) >= d.dma_cnt:
                    continue
                if dma and d.semkey == o.semkey and kind == "waw":
                    continue
                seen[d.semkey] = d.dma_cnt
                final.append(d)
            else:
                if d.stream == st and not dma:
                    if st == "pe":
                        continue
                if seen.get(d.stream, -1) >= d.sidx:
                    continue
                seen[d.stream] = d.sidx
                d.sig = True
                final.append(d)
        o.deps = final
        o.sidx = len(self.streams[st])
        self.streams[st].append(o)
        self.ops.append(o)
        for k in writes:
            self.last_write[k] = o
            self.readers[k] = []
        for k in reads:
            self.readers.setdefault(k, []).append(o)
        return o

    def barrier(self, fn):
        deps = []
        for st, lst in self.streams.items():
            if lst and st != "sp":
                last = None
                for o in reversed(lst):
                    if not o.is_dma:
                        last = o
                        break
                if last is not None:
                    deps.append(last)
        deps += list(self.last_dma.values())
        self.bar = None
        b = self.op("dve", fn, extra_deps=deps)
        b.sig = True
        self.bar = b
        self.last_write = {}
        self.readers = {}

    def emit(self):
        nc = self.nc
        for st, lst in self.streams.items():
            c = 0
            for o in lst:
                if o.sig and not o.is_dma:
                    c += 1
                o.sigcnt = c
        with contextlib.ExitStack() as es:
            EP = 30000
            csem = {}
            for st in ("pe", "act", "dve", "pool"):
                lst = self.streams[st]
                nep = (lst[-1].sigcnt if lst else 0) // EP + 1
                csem[st] = [es.enter_context(nc.semaphore("c_%s%d" % (st, i))) for i in range(nep)]
            dsem = {}
            for i, k in enumerate(self.dma_count):
                dsem[k] = es.enter_context(nc.semaphore("d%d" % i))
            es.enter_context(nc.allow_non_contiguous_dma(reason="small state / parameter IO"))
            block = es.enter_context(nc.Block())

            def run(st):
                def body(e):
                    for o in self.streams[st]:
                        for d in o.deps:
                            if d.is_dma:
                                e.wait_ge(dsem[d.semkey], 16 * d.dma_cnt)
                            else:
                                k_ = (d.sigcnt - 1) // EP
                                e.wait_ge(csem[d.stream][k_], d.sigcnt - k_ * EP)
                        ins = o.fn(e)
                        if o.is_dma:
                            ins.then_inc(dsem[o.semkey], 16)
                        elif o.sig:
                            ins.then_inc(csem[st][(o.sigcnt - 1) // EP], 1)
                    if st == "sp":
                        for k, c in self.dma_count.items():
                            e.wait_ge(dsem[k], 16 * c)
                return body

            block.tensor(run("pe"))
            block.scalar(run("act"))
            block.vector(run("dve"))
            block.gpsimd(run("pool"))
            block.sync(run("sp"))


def _isap(v):
    return hasattr(v, "ap") and hasattr(v, "name") and hasattr(v, "shape") and not isinstance(v, (int, float))


class KB:
    def __init__(self, nc):
        self.nc = nc
        self.P = Prog(nc)
        self.cnt = 0
        self.sbnames = {}
        self.flip = 0
        self.stop = False

    def _gen(self, eng, method, kw, rk=None, wk=None):
        if self.stop:
            return
        reads, writes = [], []
        for k, v in kw.items():
            if _isap(v):
                (writes if k in ("out", "accum_out") else reads).append(v.name)
        if rk is not None:
            reads = list(rk)
        if wk is not None:
            writes = list(wk)
        self.P.op(eng, lambda e: getattr(e, method)(**kw), reads=reads, writes=writes)

    def V(self, method, rk=None, wk=None, **kw):
        self._gen("dve", method, kw, rk, wk)

    def A(self, method, rk=None, wk=None, **kw):
        self._gen("act", method, kw, rk, wk)

    def G(self, method, rk=None, wk=None, **kw):
        self._gen("pool", method, kw, rk, wk)

    def cp(self, out, in_):
        self.flip ^= 1
        if self.flip:
            self.A("activation", out=out, in_=in_, func=AF.Copy)
        else:
            self.V("tensor_copy", out=out, in_=in_)

    def mm(self, out, lhsT, rhs, start=True, stop=True):
        if self.stop:
            return
        self.P.op("pe", lambda e: e.matmul(out, lhsT=lhsT, rhs=rhs, start=start, stop=stop),
                  reads=[lhsT.name, rhs.name], writes=[out.name])

    def tr(self, out, in_, ident):
        if self.stop:
            return
        self.P.op("pe", lambda e: e.transpose(out, in_, ident), reads=[in_.name, ident.name], writes=[out.name])

    def dma(self, q, out, in_, rk=None, wk=None, semkey=None):
        if self.stop:
            return
        r = [in_.name] if rk is None else list(rk)
        w = [out.name] if wk is None else list(wk)
        if semkey is None:
            if out.name in self.sbnames:
                semkey = self.sbnames[out.name]
            elif in_.name in self.sbnames:
                semkey = self.sbnames[in_.name]
            else:
                self.cnt += 1
                semkey = "dd%d" % (self.cnt % 4)
        self.P.op(q, lambda e: e.dma_start(out=out, in_=in_), reads=r, writes=w, dma=True, semkey=semkey)


def build(cfg):
    NFT, NS, PAST, NTS = cfg["NFT"], cfg["NS"], cfg["PAST"], cfg["NTS"]
    NL = cfg.get("DEPTH", DEPTH)
    NT = NFT + 1
    T = NT * 128
    NSEQ = 1 + NS
    NKB = PAST // 128
    nc = bass.Bass("TRN2", target_bir_lowering=False)
    kb = KB(nc)
    glob = contextlib.ExitStack()
    uid = {"n": 0}

    pidx = {"n": 0}

    def sb(es, name, shape, dt=F32):
        uid["n"] += 1
        nm = "%s_%d" % (name, uid["n"])
        if es is glob:
            kb.sbnames[nm] = "g:" + nm
        else:
            pidx["n"] += 1
            kb.sbnames[nm] = "p%d" % pidx["n"]
        return es.enter_context(nc.sbuf_tensor(nm, list(shape), dt))

    def din(name, shape):
        return nc.dram_tensor(name, list(shape), F32, kind="ExternalInput").ap()

    def dout(name, shape):
        return nc.dram_tensor(name, list(shape), F32, kind="ExternalOutput").ap()

    def dscr(name, shape):
        return nc.dram_tensor(name, list(shape), F32, kind="Internal").ap()

    x0 = din("x0", [T, D])
    cs_d = din("cs", [T, 32])
    identf_d = din("identf", [128, 128])
    mstrict_d = din("mstrict", [128, 128])
    mstrictT_d = din("mstrictT", [128, 128])
    mincl_d = din("mincl16", [16, 128])
    bones_d = din("blockones", [128, 128])
    attm_d = din("attmask", [128, 128])
    cache_ckv = din("cache_ckv", [DEPTH, NS, PAST, 128])
    cache_kpe = din("cache_kpe", [DEPTH, NS, PAST, 32])
    st_lru_conv = din("st_lru_conv", [DEPTH, NS, 256, 3])
    st_lru_h = din("st_lru_h", [DEPTH, NS, 256, 1])
    st_gdn_conv = din("st_gdn_conv", [DEPTH, NS, 768, 3])
    st_gdn_s = din("st_gdn_s", [DEPTH, NS, 256, 64])
    st_rwkv_shift = din("st_rwkv_shift", [DEPTH, NS, 1024, 1])
    st_rwkv_s = din("st_rwkv_s", [DEPTH, NS, 256, 64])
    w = {}
    for nm, shp in (("norm_ffn1", [DEPTH, D]), ("ffn1_w1", [DEPTH, D, 2 * DFF]), ("ffn1_w2", [DEPTH, DFF, D]),
                    ("norm_mix", [DEPTH, D]), ("w_in", [DEPTH, D, IN_W]), ("w_out", [DEPTH, MIXW, D]),
                    ("norm_ffn2", [DEPTH, D]), ("ffn2_w1", [DEPTH, D, 2 * DFF]), ("ffn2_w2", [DEPTH, DFF, D]),
                    ("final_norm", [1, D]), ("mla_rows", [DEPTH, 576]), ("mla_w_uq", [DEPTH, 256, 384]),
                    ("mla_w_ukv", [DEPTH, 128, 512]), ("colp", [DEPTH, 128, NCOL]), ("lru_wbd", [DEPTH, 4, 128, 128]),
                    ("rwkv_wab", [DEPTH, 128, 256]), ("rwkv_gb", [DEPTH, 128, 256]), ("gdn_ab", [DEPTH, D, 8])):
        w[nm] = din(nm, shp)
    yout = dout("yout", [T, D])
    o_ckv = dout("o_ckv", [DEPTH, T, 128])
    o_kpe = dout("o_kpe", [DEPTH, T, 32])
    o_lru_conv = dout("o_lru_conv", [DEPTH, NSEQ, 256, 3])
    o_lru_h = dout("o_lru_h", [DEPTH, NSEQ, 256, 1])
    o_gdn_conv = dout("o_gdn_conv", [DEPTH, NSEQ, 768, 3])
    o_gdn_s = dout("o_gdn_s", [DEPTH, NSEQ, 256, 64])
    o_rwkv_shift = dout("o_rwkv_shift", [DEPTH, NSEQ, 1024, 1])
    o_rwkv_s = dout("o_rwkv_s", [DEPTH, NSEQ, 256, 64])
    X = dscr("Xres", [T, D])
    X1 = dscr("X1res", [T, D])
    CA = dscr("CA", [T, 416])
    NFC = 24
    CF = dscr("CF", [NFC * 128, T])
    YMT = dscr("YMT", [MIXW, T])

    identf = sb(glob, "identf", [128, 128])
    identb = sb(glob, "identb", [128, 128], BF16)
    mstrict = sb(glob, "mstrict", [128, 128])
    mstrictT = sb(glob, "mstrictT", [128, 128])
    mincl = sb(glob, "mincl", [16, 128])
    bones = sb(glob, "bones", [128, 128])
    attm = sb(glob, "attm", [128, 128], BF16)
    epsc = sb(glob, "epsc", [128, 1])
    eps_gn = sb(glob, "eps_gn", [128, 1])
    zeros = sb(glob, "zeros", [128, 512])
    ones = sb(glob, "ones", [128, 512])
    bar_t = sb(glob, "bar_t", [128, 1])
    pb = [glob.enter_context(nc.psum_tensor("pb%d" % i, [128, 512], F32)) for i in range(7)]
    ptb = glob.enter_context(nc.psum_tensor("ptb", [128, 1024], BF16))
    kb.P.excl = set(["pb%d" % i for i in range(7)] + ["ptb"])
    kb.dma("sp", identf[:], identf_d)
    kb.dma("poolq", identb[:], identf_d)
    kb.dma("sp", mstrict[:], mstrict_d)
    kb.dma("sp", mstrictT[:], mstrictT_d)
    kb.dma("sp", mincl[:], mincl_d)
    kb.dma("sp", bones[:], bones_d)
    kb.dma("poolq", attm[:], attm_d)
    kb.V("memset", ap=epsc[:], constant=EPS, wk=[epsc[:].name])
    kb.V("memset", ap=eps_gn[:], constant=64e-5, wk=[eps_gn[:].name])
    kb.V("memset", ap=zeros[:], constant=0.0, wk=[zeros[:].name])
    kb.V("memset", ap=ones[:], constant=1.0, wk=[ones[:].name])
    npad0 = 16 + 16 * NS
    for r in range(MIXW // 128):
        kb.dma("sp", YMT[r * 128:(r + 1) * 128, npad0:128], zeros[:, 0:128 - npad0], wk=[("YMT", "pad", r)])

    def cut(n_):
        if cfg.get('DBG', 99) == n_:
            kb.stop = True

    def barrier():
        kb.P.barrier(lambda e: e.memset(bar_t[:], 0.0))
        pidx["n"] = 0

    nst = (NT + NTS - 1) // NTS
    NTOK = NTS * 128
    pieces = []
    o_ = 0
    while o_ < NTOK:
        n_ = min(512, NTOK - o_)
        pieces.append((o_, n_))
        o_ += n_
    cnt = {"n": 0}

    def alloc_rowlocal(es):
        B = {}
        B["xs"] = sb(es, "xs", [128, NTS, D])
        B["xnT"] = sb(es, "xnT", [128, 8, NTOK], BF16)
        B["xn"] = [sb(es, "xn", [128, D], BF16) for _ in range(2)]
        B["gbc"] = sb(es, "gbc", [128, D])
        B["junk"] = sb(es, "junk", [128, D], BF16)
        B["ss"] = [sb(es, "ss", [128, 1]) for _ in range(2)]
        B["s2"] = [sb(es, "s2", [128, 1]) for _ in range(2)]
        B["rstd"] = [sb(es, "rstd", [128, 1]) for _ in range(2)]
        B["actT"] = [sb(es, "actT", [128, 2, NTOK], BF16) for _ in range(2)]
        B["W1g"] = [sb(es, "W1g", [128, 8, 512], BF16) for _ in range(2)]
        B["W2g"] = [sb(es, "W2g", [128, 2, D], BF16) for _ in range(2)]
        B["sg"] = [sb(es, "sg", [128, 512]) for _ in range(2)]
        return B

    def rms_rows(B, src_ap, n_free, i):
        kb.A("activation", out=B["junk"][:, 0:n_free], in_=src_ap, func=AF.Square, accum_out=B["ss"][i][:])
        kb.A("activation", out=B["s2"][i][:], in_=B["ss"][i][:], func=AF.Sqrt, scale=1.0 / n_free, bias=epsc[:])
        kb.V("reciprocal", out=B["rstd"][i][:], in_=B["s2"][i][:])

    def rmsnorm_T(B, ntl, gvec_ap):
        kb.dma("sp", B["gbc"][:], gvec_ap.broadcast_to([128, D]))
        for t in range(ntl):
            i = cnt["n"] % 2
            cnt["n"] += 1
            rms_rows(B, B["xs"][:, t, :], D, i)
            kb.V("scalar_tensor_tensor", out=B["xn"][i][:], in0=B["xs"][:, t, :], scalar=B["rstd"][i][:, 0:1],
                 in1=B["gbc"][:], op0=ALU.mult, op1=ALU.mult)
            for k in range(8):
                kb.tr(ptb[:, k * 128:(k + 1) * 128], B["xn"][i][:, k * 128:(k + 1) * 128], identb[:])
            kb.cp(B["xnT"][:, :, t * 128:(t + 1) * 128], ptb[:].rearrange("p (k t) -> p k t", k=8))

    def ffn(B, ntl, w1, w2):
        w1v = w1.rearrange("(k p) n -> p k n", p=128)
        w2v = w2.rearrange("(j p) n -> p j n", p=128)
        ntok = ntl * 128
        xs, xnT, actT, W1g, W2g, sg = B["xs"], B["xnT"], B["actT"], B["W1g"], B["W2g"], B["sg"]
        for hg in range(DFF // 256):
            b = hg % 2
            kb.dma("poolq", W1g[b][:, :, 0:256], w1v[:, :, hg * 256:(hg + 1) * 256])
            kb.dma("poolq", W1g[b][:, :, 256:512], w1v[:, :, DFF + hg * 256:DFF + (hg + 1) * 256])
            kb.dma("poolq", W2g[b][:], w2v[:, 2 * hg:2 * hg + 2, :])
            for (o, n) in pieces:
                if o >= ntok:
                    break
                n = min(n, ntok - o)
                for j in range(2):
                    gp, up = pb[2 * j], pb[2 * j + 1]
                    for gu, pp in ((0, gp), (1, up)):
                        for k in range(8):
                            kb.mm(pp[:, 0:n], W1g[b][:, k, gu * 256 + j * 128: gu * 256 + (j + 1) * 128],
                                  xnT[:, k, o:o + n], start=(k == 0), stop=(k == 7))
                    kb.A("activation", out=sg[j][:, 0:n], in_=gp[:, 0:n], func=AF.Silu)
                    kb.V("tensor_tensor", out=actT[b][:, j, o:o + n], in0=sg[j][:, 0:n], in1=up[:, 0:n], op=ALU.mult)
            for t in range(ntl):
                for dh in range(2):
                    pp = pb[4 + (t * 2 + dh) % 2]
                    for j in range(2):
                        kb.mm(pp[:], actT[b][:, j, t * 128:(t + 1) * 128], W2g[b][:, j, dh * 512:(dh + 1) * 512],
                              start=(j == 0), stop=(j == 1))
                    kb.V("scalar_tensor_tensor", out=xs[:, t, dh * 512:(dh + 1) * 512], in0=pp[:], scalar=0.5,
                         in1=xs[:, t, dh * 512:(dh + 1) * 512], op0=ALU.mult, op1=ALU.add)

    FCH = []
    for c in range(4):
        FCH.append(("col", 416 + c * 128))
    for c in range(8):
        FCH.append(("col", 928 + c * 128))
    for c in range(4):
        FCH.append(("rep", c))
    for c in range(8):
        FCH.append(("col", 1960 + c * 128))

    def phaseA(l):
        with contextlib.ExitStack() as es:
            B = alloc_rowlocal(es)
            winA = sb(es, "winA", [128, 8, 416], BF16)
            winF = [sb(es, "winF", [128, 8, 128], BF16) for _ in range(2)]
            wab = sb(es, "wab", [128, 8, 8], BF16)
            stageF = [sb(es, "stageF", [128, NTOK]) for _ in range(2)]
            stageA = [sb(es, "stageA", [128, 416]) for _ in range(2)]
            winv = w["w_in"][l].rearrange("(k p) n -> p k n", p=128)
            kb.dma("poolq", winA[:], winv[:, :, 0:416])
            kb.dma("poolq", wab[:], w["gdn_ab"][l].rearrange("(k p) n -> p k n", p=128))
            src = x0 if l == 0 else X
            for st in range(nst):
                t0 = st * NTS
                ntl = min(NTS, NT - t0)
                ntok = ntl * 128
                kb.dma("sp", B["xs"][:, 0:ntl, :], src[t0 * 128:(t0 + ntl) * 128, :].rearrange("(t p) d -> p t d", p=128),
                       rk=[("X", st)] if l > 0 else [])
                rmsnorm_T(B, ntl, w["norm_ffn1"][l:l + 1, :])
                ffn(B, ntl, w["ffn1_w1"][l], w["ffn1_w2"][l])
                kb.dma("sp", X1[t0 * 128:(t0 + ntl) * 128, :].rearrange("(t p) d -> p t d", p=128), B["xs"][:, 0:ntl, :],
                       wk=[("X1", st)])
                rmsnorm_T(B, ntl, w["norm_mix"][l:l + 1, :])
                xnT = B["xnT"]
                for t in range(ntl):
                    sa = stageA[t % 2]
                    for k in range(8):
                        kb.mm(pb[6][:, 0:416], xnT[:, k, t * 128:(t + 1) * 128], winA[:, k, :], start=(k == 0), stop=(k == 7))
                    kb.cp(sa[:], pb[6][:, 0:416])
                    kb.dma("sp", CA[(t0 + t) * 128:(t0 + t + 1) * 128, :], sa[:], wk=[("CA", t0 + t)])
                for fc, (kind, c0) in enumerate(FCH):
                    wf = winF[fc % 2]
                    if kind == "col":
                        kb.dma("poolq", wf[:], winv[:, :, c0:c0 + 128])
                    else:
                        for hh in range(2):
                            col = (c0 // 2) * 4 + (c0 % 2) * 2 + hh
                            kb.V("tensor_copy", out=wf[:, :, hh * 64:(hh + 1) * 64],
                                 in_=wab[:, :, col:col + 1].to_broadcast([128, 8, 64]))
                    sf = stageF[fc % 2]
                    for pi, (o, n) in enumerate(pieces):
                        if o >= ntok:
                            break
                        n = min(n, ntok - o)
                        pp = pb[pi % 4]
                        for k in range(8):
                            kb.mm(pp[:, 0:n], wf[:, k, :], xnT[:, k, o:o + n], start=(k == 0), stop=(k == 7))
                        kb.cp(sf[:, o:o + n], pp[:, 0:n])
                    kb.dma("sp", CF[fc * 128:(fc + 1) * 128, t0 * 128:t0 * 128 + ntok], sf[:, 0:ntok], wk=[("CF", fc, st)])

    def phaseB(l, last):
        with contextlib.ExitStack() as es:
            B = alloc_rowlocal(es)
            wout = sb(es, "wout", [128, 8, D], BF16)
            ymT = sb(es, "ymT", [128, 8, NTOK], BF16)
            kb.dma("poolq", wout[:], w["w_out"][l].rearrange("(k p) n -> p k n", p=128))
            if last:
                gfin = sb(es, "gfin", [128, D])
                kb.dma("sp", gfin[:], w["final_norm"].broadcast_to([128, D]))
            for st in range(nst):
                t0 = st * NTS
                ntl = min(NTS, NT - t0)
                ntok = ntl * 128
                xs = B["xs"]
                kb.dma("sp", xs[:, 0:ntl, :], X1[t0 * 128:(t0 + ntl) * 128, :].rearrange("(t p) d -> p t d", p=128),
                       rk=[("X1", st)])
                kb.dma("poolq", ymT[:, :, 0:ntok], YMT[:, t0 * 128:t0 * 128 + ntok].rearrange("(k p) t -> p k t", p=128),
                       rk=[("YMT", "all")])
                for t in range(ntl):
                    for dh in range(2):
                        pp = pb[4 + (t * 2 + dh) % 2]
                        for k in range(8):
                            kb.mm(pp[:], ymT[:, k, t * 128:(t + 1) * 128], wout[:, k, dh * 512:(dh + 1) * 512],
                                  start=(k == 0), stop=(k == 7))
                        kb.V("tensor_tensor", out=xs[:, t, dh * 512:(dh + 1) * 512], in0=pp[:],
                             in1=xs[:, t, dh * 512:(dh + 1) * 512], op=ALU.add)
                rmsnorm_T(B, ntl, w["norm_ffn2"][l:l + 1, :])
                ffn(B, ntl, w["ffn2_w1"][l], w["ffn2_w2"][l])
                if not last:
                    kb.dma("sp", X[t0 * 128:(t0 + ntl) * 128, :].rearrange("(t p) d -> p t d", p=128), xs[:, 0:ntl, :],
                           wk=[("X", st)])
                else:
                    for t in range(ntl):
                        i = cnt["n"] % 2
                        cnt["n"] += 1
                        rms_rows(B, xs[:, t, :], D, i)
                        kb.V("scalar_tensor_tensor", out=xs[:, t, :], in0=xs[:, t, :], scalar=B["rstd"][i][:, 0:1],
                             in1=gfin[:], op0=ALU.mult, op1=ALU.mult)
                    kb.dma("sp", yout[t0 * 128:(t0 + ntl) * 128, :].rearrange("(t p) d -> p t d", p=128), xs[:, 0:ntl, :])

    def phaseMLA(l):
        with contextlib.ExitStack() as es:
            LK = 16 + NFT * 128
            KT = sb(es, "KT", [96, 4, LK], BF16)
            VA = sb(es, "VA", [128, NT, 4, 72], BF16)
            QT0 = sb(es, "QT0", [96, 4, 128], BF16)
            KT0 = sb(es, "KT0", [96, 4, 128], BF16)
            ckvT0 = sb(es, "ckvT0", [128, 128], BF16)
            rows = sb(es, "mrows", [128, 576])
            wuq = sb(es, "wuq", [128, 2, 384], BF16)
            wukv = sb(es, "wukv", [128, 512], BF16)
            ca = [sb(es, "ca", [128, 416]) for _ in range(2)]
            cst = [sb(es, "cst", [128, 32]) for _ in range(2)]
            junk = sb(es, "mjunk", [128, 512])
            ssq = [sb(es, "ssq", [128, 4]) for _ in range(4)]
            cqn = sb(es, "cqn", [128, 256], BF16)
            cqT = sb(es, "cqT", [128, 2, 128], BF16)
            qf = sb(es, "qf", [128, 4, 96])
            qb = sb(es, "qb", [128, 4, 96], BF16)
            kf = sb(es, "kf", [128, 4, 96])
            kbb = sb(es, "kbb", [128, 4, 96], BF16)
            ckv = [sb(es, "ckv", [128, 128]) for _ in range(2)]
            kpe = [sb(es, "kpe", [128, 32]) for _ in range(2)]
            ckvb = sb(es, "ckvb", [128, 128], BF16)
            ckvT = sb(es, "ckvT", [128, 128], BF16)
            rtmp = sb(es, "rtmp", [128, 4, 16])
            rtmp2 = sb(es, "rtmp2", [128, 4, 16])
            qT = [sb(es, "qT", [96, 4, 128], BF16) for _ in range(2)]
            pT = [sb(es, "pT", [128, 128], BF16) for _ in range(3)]
            osb = sb(es, "osb", [128, 4, 64])
            rden = sb(es, "rden", [128, 4, 1])
            ost = [sb(es, "ost", [128, 128]) for _ in range(2)]
            vs = sb(es, "vs", [16, 4, 72], BF16)
            kTc = [sb(es, "kTc", [96, 4, 128], BF16) for _ in range(2)]
            vac = [sb(es, "vac", [128, 4, 72], BF16) for _ in range(2)]
            gq = rows[:, 384:480]
            gk = rows[:, 480:576]
            kb.dma("sp", rows[:], w["mla_rows"][l:l + 1, :].broadcast_to([128, 576]))
            kb.dma("poolq", wuq[:], w["mla_w_uq"][l].rearrange("(k p) n -> p k n", p=128))
            kb.dma("poolq", wukv[:], w["mla_w_ukv"][l])
            kb.V("tensor_scalar", out=gq, in0=gq, scalar1=1.0 / math.sqrt(96.0), scalar2=None, op0=ALU.mult)
            mc = {"n": 0}

            def headnorm(xf, xbf, gain, i):
                s = ssq[i % 4]
                kb.V("tensor_tensor", out=junk[:, 0:384].rearrange("p (h e) -> p h e", h=4), in0=xf[:], in1=xf[:], op=ALU.mult)
                kb.V("tensor_reduce", out=s[:], in_=junk[:, 0:384].rearrange("p (h e) -> p h e", h=4), axis=AX.X, op=ALU.add)
                kb.A("activation", out=s[:], in_=s[:], func=AF.Sqrt, scale=1.0 / 96.0, bias=epsc[:])
                kb.V("reciprocal", out=s[:], in_=s[:])
                kb.V("tensor_tensor", out=xf[:], in0=xf[:], in1=s[:].unsqueeze(2).to_broadcast([128, 4, 96]), op=ALU.mult)
                kb.V("tensor_tensor", out=xbf[:], in0=xf[:], in1=gain.unsqueeze(1).to_broadcast([128, 4, 96]), op=ALU.mult)

            def rope(dst_lo, dst_hi, x_lo, x_hi, cs_t, nh):
                cosb = cs_t[:, 0:16].unsqueeze(1).to_broadcast([128, nh, 16])
                sinb = cs_t[:, 16:32].unsqueeze(1).to_broadcast([128, nh, 16])
                a, b = rtmp[:, 0:nh, :], rtmp2[:, 0:nh, :]
                kb.V("tensor_tensor", out=a, in0=x_lo, in1=cosb, op=ALU.mult)
                kb.V("tensor_tensor", out=b, in0=x_hi, in1=sinb, op=ALU.mult)
                kb.V("tensor_tensor", out=a, in0=a, in1=b, op=ALU.subtract)
                kb.V("tensor_tensor", out=b, in0=x_lo, in1=sinb, op=ALU.mult)
                kb.V("tensor_tensor", out=dst_hi, in0=x_hi, in1=cosb, op=ALU.mult)
                kb.V("tensor_tensor", out=dst_hi, in0=dst_hi, in1=b, op=ALU.add)
                kb.V("tensor_copy", out=dst_lo, in_=a)

            def keys_from(ckv_ap, kpe_ap, kT_dst, va_dst, ckvT_keep=None):
                kb.cp(ckvb[:], ckv_ap)
                kb.tr(ptb[:, 0:128], ckvb[:], identb[:])
                tgt = ckvT if ckvT_keep is None else ckvT_keep
                kb.cp(tgt[:], ptb[:, 0:128])
                cut(10)
                kb.mm(pb[0][:], tgt[:], wukv[:])
                cut(13)
                kvv = pb[0][:].rearrange("p (h e) -> p h e", h=4)
                kb.cp(kf[:, :, 0:64], kvv[:, :, 0:64])
                cut(14)
                kb.V("tensor_copy", out=kf[:, :, 64:96], in_=kpe_ap.unsqueeze(1).to_broadcast([128, 4, 32]))
                cut(15)
                for h_ in range(4):
                    kb.V("tensor_copy", out=va_dst[:, h_, 0:64], in_=pb[0][:, h_ * 128 + 64:h_ * 128 + 128])
                cut(16)
                cut(11)
                mc["n"] += 1
                headnorm(kf, kbb, gk, mc["n"])
                cut(12)
                for h in range(4):
                    kb.tr(ptb[0:96, h * 128:(h + 1) * 128], kbb[:, h, :], identb[:])
                kb.cp(kT_dst, ptb[0:96, 0:512].rearrange("p (h t) -> p h t", h=4))

            def tile_pre(t):
                c = ca[t % 2]
                ct = cst[t % 2]
                kb.dma("sp", c[:], CA[t * 128:(t + 1) * 128, :], rk=[("CA", t)])
                kb.dma("sp", ct[:], cs_d[t * 128:(t + 1) * 128, :])
                cut(1)
                s = ssq[0]
                kb.A("activation", out=junk[:, 0:256], in_=c[:, 0:256], func=AF.Square, accum_out=s[:, 0:1])
                kb.A("activation", out=s[:, 1:2], in_=s[:, 0:1], func=AF.Sqrt, scale=1.0 / 256.0, bias=epsc[:])
                kb.V("reciprocal", out=s[:, 2:3], in_=s[:, 1:2])
                kb.V("scalar_tensor_tensor", out=cqn[:], in0=c[:, 0:256], scalar=s[:, 2:3], in1=rows[:, 0:256],
                     op0=ALU.mult, op1=ALU.mult)
                for k in range(2):
                    kb.tr(ptb[:, k * 128:(k + 1) * 128], cqn[:, k * 128:(k + 1) * 128], identb[:])
                kb.cp(cqT[:], ptb[:, 0:256].rearrange("p (k t) -> p k t", k=2))
                cut(2)
                for k in range(2):
                    kb.mm(pb[0][:, 0:384], cqT[:, k, :], wuq[:, k, :], start=(k == 0), stop=(k == 1))
                kb.cp(qf[:], pb[0][:, 0:384].rearrange("p (h e) -> p h e", h=4))
                cut(3)
                rope(qf[:, :, 64:80], qf[:, :, 80:96], qf[:, :, 64:80], qf[:, :, 80:96], ct, 4)
                cut(4)
                mc["n"] += 1
                headnorm(qf, qb, gq, mc["n"])
                cut(5)
                s = ssq[1]
                kb.A("activation", out=junk[:, 0:128], in_=c[:, 256:384], func=AF.Square, accum_out=s[:, 0:1])
                kb.A("activation", out=s[:, 1:2], in_=s[:, 0:1], func=AF.Sqrt, scale=1.0 / 128.0, bias=epsc[:])
                kb.V("reciprocal", out=s[:, 2:3], in_=s[:, 1:2])
                cv = ckv[t % 2]
                kp = kpe[t % 2]
                kb.V("scalar_tensor_tensor", out=cv[:], in0=c[:, 256:384], scalar=s[:, 2:3], in1=rows[:, 256:384],
                     op0=ALU.mult, op1=ALU.mult)
                rope(kp[:, 0:16].unsqueeze(1), kp[:, 16:32].unsqueeze(1), c[:, 384:400].unsqueeze(1),
                     c[:, 400:416].unsqueeze(1), ct, 1)
                cut(7)
                kb.dma("sp", o_ckv[l, t * 128:(t + 1) * 128, :], cv[:])
                kb.dma("sp", o_kpe[l, t * 128:(t + 1) * 128, :], kp[:])
                cut(8)
                return cv, kp

            def q_transpose(dst):
                for h in range(4):
                    kb.tr(ptb[0:96, h * 128:(h + 1) * 128], qb[:, h, :], identb[:])
                kb.cp(dst, ptb[0:96, 0:512].rearrange("p (h t) -> p h t", h=4))
                cut(9)

            def finish(nq):
                for h in range(4):
                    kb.V("reciprocal", out=rden[0:nq, h, :], in_=pb[3 + h][0:nq, 64:65])
                    kb.V("tensor_scalar", out=osb[0:nq, h, :], in0=pb[3 + h][0:nq, 0:64], scalar1=rden[0:nq, h, :],
                         scalar2=None, op0=ALU.mult)

            def attend(q_ap_fn, nq, blocks, out_rows_fn):
                nb = len(blocks)
                for bi, (kfn, vfn, nk, masked) in enumerate(blocks):
                    for h in range(4):
                        mc["n"] += 1
                        ps_ = pb[1 + mc["n"] % 2]
                        pt_ = pT[mc["n"] % 3]
                        kb.mm(ps_[0:nk, 0:nq], kfn(h), q_ap_fn(h))
                        kb.A("activation", out=pt_[0:nk, 0:nq], in_=ps_[0:nk, 0:nq], func=AF.Exp)
                        if masked:
                            kb.V("tensor_tensor", out=pt_[0:nk, 0:nq], in0=pt_[0:nk, 0:nq], in1=attm[0:nk, 0:nq], op=ALU.mult)
                        kb.mm(pb[3 + h][0:nq, 0:65], pt_[0:nk, 0:nq], vfn(h), start=(bi == 0), stop=(bi == nb - 1))
                finish(nq)
                for k in range(2):
                    kb.tr(pb[0][:, k * 128:k * 128 + nq], osb[0:nq, 2 * k:2 * k + 2, :].rearrange("p h e -> p (h e)"),
                          identf[0:nq, 0:nq])
                    so = ost[k]
                    kb.cp(so[:, 0:nq], pb[0][:, k * 128:k * 128 + nq])
                    out_rows_fn(k, so)

            kb.V("memset", ap=VA[:].rearrange("p a b c -> p (a b c)"), constant=1.0, wk=[VA[:].name])
            kb.V("memset", ap=vs[:].rearrange("p b c -> p (b c)"), constant=1.0, wk=[vs[:].name])
            for v_ in vac:
                kb.V("memset", ap=v_[:].rearrange("p b c -> p (b c)"), constant=1.0, wk=[v_[:].name])
            mcut = cfg.get('MCUT', 99)
            if mcut <= 0:
                return
            cv0 = kp0 = None
            for t in range(NT):
                cv, kp = tile_pre(t)
                if t == 0:
                    q_transpose(QT0[:])
                    keys_from(cv[:], kp[:], KT0[:], VA[:, 0, :, :], ckvT_keep=ckvT0)
                    kb.cp(KT[:, :, 0:16], KT0[:, :, 0:16])
                else:
                    keys_from(cv[:], kp[:], KT[:, :, 16 + (t - 1) * 128: 16 + t * 128], VA[:, t, :, :])
            if mcut <= 1:
                return
            attend(lambda h: QT0[:, h, 0:16], 16,
                   [(lambda h: KT[:, h, 0:16], lambda h: VA[0:16, 0, h, 0:65], 16, False)],
                   lambda k, so: kb.dma("sp", YMT[k * 128:(k + 1) * 128, 0:16], so[:, 0:16], wk=[("YMT", "a0", k)]))
            if mcut <= 2:
                return
            for s_ in range(NS):
                r0 = 16 + 16 * s_
                kb.mm(pb[0][0:16, :], ckvT0[:, r0:r0 + 16], wukv[:])
                for h_ in range(4):
                    kb.V("tensor_copy", out=vs[:, h_, 0:64], in_=pb[0][0:16, h_ * 128 + 64:h_ * 128 + 128])
                blocks = []
                for b_ in range(NKB):
                    cc = ckv[b_ % 2]
                    kk = kpe[b_ % 2]
                    kb.dma("sp", cc[:], cache_ckv[l, s_, b_ * 128:(b_ + 1) * 128, :])
                    kb.dma("sp", kk[:], cache_kpe[l, s_, b_ * 128:(b_ + 1) * 128, :])
                    ktc = kTc[b_ % 2]
                    vc_ = vac[b_ % 2]
                    keys_from(cc[:], kk[:], ktc[:], vc_[:])
                    blocks.append((ktc, vc_))
                    for h in range(4):
                        mc["n"] += 1
                        ps_ = pb[1 + mc["n"] % 2]
                        pt_ = pT[mc["n"] % 3]
                        kb.mm(ps_[:, 0:16], ktc[:, h, :], QT0[:, h, r0:r0 + 16])
                        kb.A("activation", out=pt_[:, 0:16], in_=ps_[:, 0:16], func=AF.Exp)
                        kb.mm(pb[3 + h][0:16, 0:65], pt_[:, 0:16], vc_[:, h, 0:65], start=(b_ == 0), stop=False)
                for h in range(4):
                    mc["n"] += 1
                    ps_ = pb[1 + mc["n"] % 2]
                    pt_ = pT[mc["n"] % 3]
                    kb.mm(ps_[0:16, 0:16], KT0[:, h, r0:r0 + 16], QT0[:, h, r0:r0 + 16])
                    kb.A("activation", out=pt_[0:16, 0:16], in_=ps_[0:16, 0:16], func=AF.Exp)
                    kb.mm(pb[3 + h][0:16, 0:65], pt_[0:16, 0:16], vs[:, h, 0:65], start=(NKB == 0), stop=True)
                finish(16)
                for k in range(2):
                    kb.tr(pb[0][:, k * 128:k * 128 + 16], osb[0:16, 2 * k:2 * k + 2, :].rearrange("p h e -> p (h e)"),
                          identf[0:16, 0:16])
                    so = ost[k]
                    kb.cp(so[:, 0:16], pb[0][:, k * 128:k * 128 + 16])
                    kb.dma("sp", YMT[k * 128:(k + 1) * 128, r0:r0 + 16], so[:, 0:16], wk=[("YMT", "as", k, s_)])
            if mcut <= 3:
                return
            for t in range(1, NT):
                c = ca[t % 2]
                ct = cst[t % 2]
                kb.dma("sp", c[:], CA[t * 128:(t + 1) * 128, :], rk=[("CA", t)])
                kb.dma("sp", ct[:], cs_d[t * 128:(t + 1) * 128, :])
                s = ssq[0]
                kb.A("activation", out=junk[:, 0:256], in_=c[:, 0:256], func=AF.Square, accum_out=s[:, 0:1])
                kb.A("activation", out=s[:, 1:2], in_=s[:, 0:1], func=AF.Sqrt, scale=1.0 / 256.0, bias=epsc[:])
                kb.V("reciprocal", out=s[:, 2:3], in_=s[:, 1:2])
                kb.V("scalar_tensor_tensor", out=cqn[:], in0=c[:, 0:256], scalar=s[:, 2:3], in1=rows[:, 0:256],
                     op0=ALU.mult, op1=ALU.mult)
                for k in range(2):
                    kb.tr(ptb[:, k * 128:(k + 1) * 128], cqn[:, k * 128:(k + 1) * 128], identb[:])
                kb.cp(cqT[:], ptb[:, 0:256].rearrange("p (k t) -> p k t", k=2))
                for k in range(2):
                    kb.mm(pb[0][:, 0:384], cqT[:, k, :], wuq[:, k, :], start=(k == 0), stop=(k == 1))
                kb.cp(qf[:], pb[0][:, 0:384].rearrange("p (h e) -> p h e", h=4))
                rope(qf[:, :, 64:80], qf[:, :, 80:96], qf[:, :, 64:80], qf[:, :, 80:96], ct, 4)
                mc["n"] += 1
                headnorm(qf, qb, gq, mc["n"])
                qTt = qT[t % 2]
                q_transpose(qTt[:])
                blocks = [(lambda h: KT[:, h, 0:16], lambda h: VA[0:16, 0, h, 0:65], 16, False)]
                for kt in range(1, t + 1):
                    blocks.append((lambda h, kt=kt: KT[:, h, 16 + (kt - 1) * 128:16 + kt * 128],
                                   lambda h, kt=kt: VA[:, kt, h, 0:65], 128, kt == t))
                attend(lambda h: qTt[:, h, :], 128, blocks,
                       lambda k, so, t=t: kb.dma("sp", YMT[k * 128:(k + 1) * 128, t * 128:(t + 1) * 128], so[:],
                                                 wk=[("YMT", "af", k, t)]))

    SEG = 256

    def phaseREC(l):
        with contextlib.ExitStack() as es:
            colp = sb(es, "colp", [128, NCOL])
            wbd = sb(es, "wbd", [128, 4, 128])
            wabr = sb(es, "wabr", [128, 256])
            gbr = sb(es, "gbr", [128, 256])
            negc = sb(es, "negc", [128, 2])
            negA = sb(es, "negA", [128, 2])
            kb.dma("sp", colp[:], w["colp"][l])
            kb.dma("sp", wbd[:], w["lru_wbd"][l].rearrange("c p m -> p c m"))
            kb.dma("sp", wabr[:], w["rwkv_wab"][l])
            kb.dma("sp", gbr[:], w["rwkv_gb"][l])
            C_LRU = lambda cb, j: colp[:, cb * 8 + j: cb * 8 + j + 1]
            C_GCW = lambda ci, j: colp[:, 16 + ci * 4 + j: 16 + ci * 4 + j + 1]
            C_GAL = lambda p: colp[:, 40 + p: 41 + p]
            C_GDT = lambda p: colp[:, 42 + p: 43 + p]
            C_GON = lambda p: colp[:, 44 + p: 45 + p]
            C_MU = lambda c: colp[:, 46 + c: 47 + c]
            C_W0 = lambda p: colp[:, 54 + p: 55 + p]
            C_A0 = lambda p: colp[:, 56 + p: 57 + p]
            C_KK = lambda p: colp[:, 58 + p: 59 + p]
            C_KA = lambda p: colp[:, 60 + p: 61 + p]
            C_RK = lambda p: colp[:, 62 + p: 63 + p]
            C_LNW = lambda p: colp[:, 64 + p: 65 + p]
            C_LNB = lambda p: colp[:, 66 + p: 67 + p]
            for cb in range(2):
                kb.A("activation", out=negc[:, cb:cb + 1], in_=C_LRU(cb, 7), func=AF.Exp, scale=-1.0)
                kb.A("activation", out=negc[:, cb:cb + 1], in_=negc[:, cb:cb + 1], func=AF.Ln, bias=ones[:, 0:1])
                kb.V("tensor_scalar", out=negc[:, cb:cb + 1], in0=negc[:, cb:cb + 1], scalar1=-8.0, scalar2=None, op0=ALU.mult)
            for p in range(2):
                kb.A("activation", out=negA[:, p:p + 1], in_=C_GAL(p), func=AF.Exp)
                kb.V("tensor_scalar", out=negA[:, p:p + 1], in0=negA[:, p:p + 1], scalar1=-1.0, scalar2=None, op0=ALU.mult)

            lru_x = [sb(es, "lru_x", [128, 3 + SEG]) for _ in range(2)]
            lru_h = [sb(es, "lru_h", [128, 1]) for _ in range(2)]
            gdn_x = [sb(es, "gdn_x", [128, 3 + SEG]) for _ in range(6)]
            rw_x = [sb(es, "rw_x", [128, 1 + SEG]) for _ in range(8)]
            Hs = [sb(es, "Hs", [128, 4, 64]) for _ in range(2)]
            def W(name):
                return sb(es, name, [128, SEG])
            t_ = [W("t%d" % i) for i in range(8)]
            lr_xc, lr_r, lr_i, lr_h, lr_g = W("lr_xc"), W("lr_r"), W("lr_i"), W("lr_h"), W("lr_g")
            yst = [W("yst") for _ in range(2)]
            xm = [W("xm%d" % i) for i in range(8)]
            agate = W("agate")
            NP = 4
            R_ = [W("R") for _ in range(NP)]
            A_ = [W("A") for _ in range(NP)]
            Bv = [W("Bv") for _ in range(NP)]
            Kv = [None, None, W("Kv2"), W("Kv3")]
            Vv = [W("V") for _ in range(NP)]
            logw = [W("logw") for _ in range(NP)]
            zg = [W("zg") for _ in range(NP)]
            Rt = [W("Rt") for _ in range(NP)]
            At = [W("At") for _ in range(NP)]
            Bt = [W("Bt") for _ in range(NP)]
            Kt = [None, None, W("Kt2"), W("Kt3")]
            Bp = [W("Bp") for _ in range(NP)]
            Kp = [None, None, W("Kp2"), W("Kp3")]
            cum = W("cum")
            crel = W("crel")
            WC = [sb(es, "WC", [128, SEG // 16]) for _ in range(NP)]
            OT = sb(es, "OT", [128, NP, SEG])
            Xs, XTs, Aaks = [sb(es, nm_, [128, 128]) for nm_ in ("Xs", "XTs", "Aaks")]
            Y = [sb(es, "Y", [128, 128]) for _ in range(2)]
            YT = [sb(es, "YT", [128, 128]) for _ in range(2)]
            PT = [sb(es, "PT", [128, 128]) for _ in range(2)]
            TAT = [[sb(es, "TAT", [128, 128]) for _ in range(2)] for _ in range(NP)]
            WtT = [sb(es, "WtT", [128, 128]) for _ in range(NP)]
            Atok = sb(es, "Atok", [128, 128])
            Vtok = [sb(es, "Vtok", [128, 128]) for _ in range(NP)]
            RBT = [[sb(es, "RBT", [16, 128]) for _ in range(2)] for _ in range(NP)]
            RKT = [[None, None], [None, None]] + [[sb(es, "RKT", [16, 128]) for _ in range(2)] for _ in range(2)]
            CM = [sb(es, "CM", [16, 8, 384]) for _ in range(NP)]
            Usb = [sb(es, "Usb", [16, 512]) for _ in range(2)]
            UV = [sb(es, "UV", [16, 256]) for _ in range(2)]

            def blocksum(out_ps, src, n):
                kb.mm(out_ps[:, 0:n], bones[:], src[:, 0:n])

            def rsqrt_ps(dst, ps_ap, scale, bias_tile):
                kb.A("activation", out=dst, in_=ps_ap, func=AF.Sqrt, scale=scale, bias=bias_tile[:])
                kb.V("reciprocal", out=dst, in_=dst)

            def conv4(dst, xbuf, n, wcol, bias_col=None):
                if bias_col is None:
                    kb.V("tensor_scalar", out=dst[:, 0:n], in0=xbuf[:, 0:n], scalar1=wcol(0), scalar2=None, op0=ALU.mult)
                else:
                    kb.V("tensor_scalar", out=dst[:, 0:n], in0=xbuf[:, 0:n], scalar1=wcol(0), scalar2=bias_col,
                         op0=ALU.mult, op1=ALU.add)
                for j in range(1, 4):
                    kb.V("scalar_tensor_tensor", out=dst[:, 0:n], in0=xbuf[:, j:j + n], scalar=wcol(j), in1=dst[:, 0:n],
                         op0=ALU.mult, op1=ALU.add)

            def store_ymt(row0, src, n, col0, tag):
                kb.dma("sp", YMT[row0:row0 + 128, col0:col0 + n], src[:, 0:n], wk=[("YMT", tag, row0, col0)])

            def run_sequence(si, segs):
                if si == 0:
                    for cb in range(2):
                        kb.V("memset", ap=lru_x[cb][:, 0:3], constant=0.0, wk=[lru_x[cb][:].name])
                        kb.V("memset", ap=lru_h[cb][:], constant=0.0, wk=[lru_h[cb][:].name])
                    for ci in range(6):
                        kb.V("memset", ap=gdn_x[ci][:, 0:3], constant=0.0, wk=[gdn_x[ci][:].name])
                    for c in range(8):
                        kb.V("memset", ap=rw_x[c][:, 0:1], constant=0.0, wk=[rw_x[c][:].name])
                    kb.V("memset", ap=Hs[0][:], constant=0.0, wk=[Hs[0][:].name])
                else:
                    s_ = si - 1
                    for cb in range(2):
                        kb.dma("sp", lru_x[cb][:, 0:3], st_lru_conv[l, s_, cb * 128:(cb + 1) * 128, :])
                        kb.dma("sp", lru_h[cb][:], st_lru_h[l, s_, cb * 128:(cb + 1) * 128, :])
                    for ci in range(6):
                        kb.dma("sp", gdn_x[ci][:, 0:3], st_gdn_conv[l, s_, ci * 128:(ci + 1) * 128, :])
                    for c in range(8):
                        kb.dma("sp", rw_x[c][:, 0:1], st_rwkv_shift[l, s_, c * 128:(c + 1) * 128, :])
                    for p in range(2):
                        kb.dma("sp", Hs[0][:, p, :], st_gdn_s[l, s_, p * 128:(p + 1) * 128, :])
                        kb.dma("sp", Hs[0][:, 2 + p, :], st_rwkv_s[l, s_, p * 128:(p + 1) * 128, :])
                hcur = 0
                for (col0, n) in segs:
                    nch = n // 16
                    for cb in range(2):
                        xb_ = lru_x[cb]
                        kb.dma("sp", xb_[:, 3:3 + n], CF[cb * 128:(cb + 1) * 128, col0:col0 + n], rk=[("CF", "all")])
                        kb.dma("sp", lr_g[:, 0:n], CF[(2 + cb) * 128:(3 + cb) * 128, col0:col0 + n], rk=[("CF", "all")])
                        conv4(lr_xc, xb_, n, lambda j: C_LRU(cb, j), C_LRU(cb, 4))
                        kb.mm(pb[0][:, 0:n], wbd[:, cb, :], lr_xc[:, 0:n])
                        kb.mm(pb[1][:, 0:n], wbd[:, 2 + cb, :], lr_xc[:, 0:n])
                        kb.A("activation", out=lr_r[:, 0:n], in_=pb[0][:, 0:n], func=AF.Sigmoid, bias=C_LRU(cb, 5))
                        kb.A("activation", out=lr_i[:, 0:n], in_=pb[1][:, 0:n], func=AF.Sigmoid, bias=C_LRU(cb, 6))
                        kb.A("activation", out=lr_r[:, 0:n], in_=lr_r[:, 0:n], func=AF.Exp, scale=negc[:, cb:cb + 1])
                        kb.V("tensor_tensor", out=t_[0][:, 0:n], in0=lr_r[:, 0:n], in1=lr_r[:, 0:n], op=ALU.mult)
                        kb.V("tensor_scalar", out=t_[0][:, 0:n], in0=t_[0][:, 0:n], scalar1=-1.0, scalar2=1.0,
                             op0=ALU.mult, op1=ALU.add)
                        kb.A("activation", out=t_[0][:, 0:n], in_=t_[0][:, 0:n], func=AF.Sqrt)
                        kb.V("tensor_tensor", out=lr_i[:, 0:n], in0=lr_i[:, 0:n], in1=lr_xc[:, 0:n], op=ALU.mult)
                        kb.V("tensor_tensor", out=lr_i[:, 0:n], in0=lr_i[:, 0:n], in1=t_[0][:, 0:n], op=ALU.mult)
                        kb.V("tensor_tensor_scan", out=lr_h[:, 0:n], data0=lr_r[:, 0:n], data1=lr_i[:, 0:n],
                             initial=lru_h[cb][:, 0:1], op0=ALU.mult, op1=ALU.add)
                        kb.V("tensor_copy", out=lru_h[cb][:], in_=lr_h[:, n - 1:n])
                        kb.V("tensor_tensor", out=t_[1][:, 0:n], in0=lr_g[:, 0:n], in1=lr_g[:, 0:n], op=ALU.mult)
                        kb.V("tensor_scalar", out=t_[1][:, 0:n], in0=t_[1][:, 0:n], scalar1=0.044715, scalar2=1.0,
                             op0=ALU.mult, op1=ALU.add)
                        kb.V("tensor_tensor", out=t_[1][:, 0:n], in0=t_[1][:, 0:n], in1=lr_g[:, 0:n], op=ALU.mult)
                        kb.A("activation", out=t_[1][:, 0:n], in_=t_[1][:, 0:n], func=AF.Sigmoid, scale=1.5957691216057308)
                        kb.V("tensor_tensor", out=t_[1][:, 0:n], in0=t_[1][:, 0:n], in1=lr_g[:, 0:n], op=ALU.mult)
                        ys = yst[cb]
                        kb.V("tensor_tensor", out=ys[:, 0:n], in0=t_[1][:, 0:n], in1=lr_h[:, 0:n], op=ALU.mult)
                        store_ymt(256 + cb * 128, ys, n, col0, "b")
                        kb.V("tensor_copy", out=t_[2][:, 0:3], in_=xb_[:, n:n + 3])
                        kb.V("tensor_copy", out=xb_[:, 0:3], in_=t_[2][:, 0:3])
                    cut(20)
                    for ci in range(6):
                        kb.dma("sp", gdn_x[ci][:, 3:3 + n], CF[(4 + ci) * 128:(5 + ci) * 128, col0:col0 + n], rk=[("CF", "all")])
                    for p in range(2):
                        qc, kc, vc = t_[0], t_[1], Vv[p]
                        for dst, ci in ((qc, p), (kc, 2 + p), (vc, 4 + p)):
                            conv4(dst, gdn_x[ci], n, lambda j, ci=ci: C_GCW(ci, j))
                            kb.A("activation", out=dst[:, 0:n], in_=dst[:, 0:n], func=AF.Silu)
                        for src, dst, sc in ((qc, R_[p], 64 ** -0.5), (kc, t_[2], 1.0)):
                            kb.V("tensor_tensor", out=t_[3][:, 0:n], in0=src[:, 0:n], in1=src[:, 0:n], op=ALU.mult)
                            blocksum(pb[0], t_[3], n)
                            rsqrt_ps(t_[3][:, 0:n], pb[0][:, 0:n], 1.0, epsc)
                            kb.V("scalar_tensor_tensor", out=dst[:, 0:n], in0=src[:, 0:n], scalar=sc, in1=t_[3][:, 0:n],
                                 op0=ALU.mult, op1=ALU.mult)
                        kn = t_[2]
                        kb.dma("sp", t_[4][:, 0:n], CF[(12 + p) * 128:(13 + p) * 128, col0:col0 + n], rk=[("CF", "all")])
                        kb.dma("sp", t_[5][:, 0:n], CF[(14 + p) * 128:(15 + p) * 128, col0:col0 + n], rk=[("CF", "all")])
                        kb.dma("sp", zg[p][:, 0:n], CF[(10 + p) * 128:(11 + p) * 128, col0:col0 + n], rk=[("CF", "all")])
                        kb.A("activation", out=t_[4][:, 0:n], in_=t_[4][:, 0:n], func=AF.Exp, bias=C_GDT(p))
                        kb.A("activation", out=t_[4][:, 0:n], in_=t_[4][:, 0:n], func=AF.Ln, bias=ones[:, 0:1])
                        kb.V("tensor_scalar", out=logw[p][:, 0:n], in0=t_[4][:, 0:n], scalar1=negA[:, p:p + 1], scalar2=None,
                             op0=ALU.mult)
                        kb.A("activation", out=t_[5][:, 0:n], in_=t_[5][:, 0:n], func=AF.Sigmoid)
                        kb.V("tensor_tensor", out=Bv[p][:, 0:n], in0=t_[5][:, 0:n], in1=kn[:, 0:n], op=ALU.mult)
                        kb.V("tensor_scalar", out=A_[p][:, 0:n], in0=kn[:, 0:n], scalar1=-1.0, scalar2=None, op0=ALU.mult)
                    for ci in range(6):
                        kb.V("tensor_copy", out=t_[6][:, 0:3], in_=gdn_x[ci][:, n:n + 3])
                        kb.V("tensor_copy", out=gdn_x[ci][:, 0:3], in_=t_[6][:, 0:3])
                    cut(21)
                    for c in range(8):
                        kb.dma("sp", rw_x[c][:, 1:1 + n], CF[(16 + c) * 128:(17 + c) * 128, col0:col0 + n], rk=[("CF", "all")])
                        kb.V("tensor_tensor", out=xm[c][:, 0:n], in0=rw_x[c][:, 0:n], in1=rw_x[c][:, 1:1 + n], op=ALU.subtract)
                        kb.V("scalar_tensor_tensor", out=xm[c][:, 0:n], in0=xm[c][:, 0:n], scalar=C_MU(c), in1=rw_x[c][:, 1:1 + n],
                             op0=ALU.mult, op1=ALU.add)
                    kb.A("activation", out=t_[0][0:64, 0:n], in_=xm[6][0:64, 0:n], func=AF.Tanh)
                    kb.A("activation", out=t_[1][:, 0:n], in_=xm[7][:, 0:n], func=AF.Sigmoid)
                    for p in range(2):
                        P_ = 2 + p
                        kb.mm(pb[0][:, 0:n], wabr[0:64, p * 128:(p + 1) * 128], t_[0][0:64, 0:n])
                        kb.mm(pb[1][:, 0:n], wabr[64:128, p * 128:(p + 1) * 128], xm[6][64:128, 0:n])
                        kb.mm(pb[2][:, 0:n], gbr[:, p * 128:(p + 1) * 128], t_[1][:, 0:n])
                        kb.A("activation", out=logw[P_][:, 0:n], in_=pb[0][:, 0:n], func=AF.Sigmoid, bias=C_W0(p))
                        kb.V("tensor_scalar", out=logw[P_][:, 0:n], in0=logw[P_][:, 0:n], scalar1=-0.606531, scalar2=None,
                             op0=ALU.mult)
                        kb.A("activation", out=agate[:, 0:n], in_=pb[1][:, 0:n], func=AF.Sigmoid, bias=C_A0(p))
                        kb.cp(zg[P_][:, 0:n], pb[2][:, 0:n])
                        kb.V("tensor_scalar", out=t_[2][:, 0:n], in0=xm[2 + p][:, 0:n], scalar1=C_KK(p), scalar2=None, op0=ALU.mult)
                        kb.V("tensor_tensor", out=t_[3][:, 0:n], in0=t_[2][:, 0:n], in1=t_[2][:, 0:n], op=ALU.mult)
                        blocksum(pb[3], t_[3], n)
                        rsqrt_ps(t_[3][:, 0:n], pb[3][:, 0:n], 1.0, epsc)
                        kb.V("tensor_tensor", out=t_[2][:, 0:n], in0=t_[2][:, 0:n], in1=t_[3][:, 0:n], op=ALU.mult)
                        kb.V("tensor_scalar", out=A_[P_][:, 0:n], in0=t_[2][:, 0:n], scalar1=-1.0, scalar2=None, op0=ALU.mult)
                        kb.V("tensor_tensor", out=Bv[P_][:, 0:n], in0=t_[2][:, 0:n], in1=agate[:, 0:n], op=ALU.mult)
                        kb.V("tensor_scalar", out=t_[4][:, 0:n], in0=agate[:, 0:n], scalar1=-1.0, scalar2=C_KA(p),
                             op0=ALU.add, op1=ALU.mult)
                        kb.V("scalar_tensor_tensor", out=Kv[P_][:, 0:n], in0=t_[4][:, 0:n], scalar=1.0, in1=xm[2 + p][:, 0:n],
                             op0=ALU.add, op1=ALU.mult)
                        kb.V("tensor_copy", out=R_[P_][:, 0:n], in_=xm[p][:, 0:n])
                        kb.V("tensor_copy", out=Vv[P_][:, 0:n], in_=xm[4 + p][:, 0:n])
                    for c in range(8):
                        kb.V("tensor_copy", out=t_[6][:, 0:1], in_=rw_x[c][:, n:n + 1])
                        kb.V("tensor_copy", out=rw_x[c][:, 0:1], in_=t_[6][:, 0:1])
                    cut(22)
                    for P_ in range(NP):
                        lw = logw[P_]
                        kb.V("tensor_tensor_scan", out=cum[:, 0:n], data0=ones[:, 0:n], data1=lw[:, 0:n], initial=0.0,
                             op0=ALU.mult, op1=ALU.add)
                        c3 = cum[:, 0:n].rearrange("p (c i) -> p c i", i=16)
                        r3 = crel[:, 0:n].rearrange("p (c i) -> p c i", i=16)
                        kb.V("tensor_copy", out=r3[:, 0:1, :], in_=c3[:, 0:1, :])
                        if nch > 1:
                            kb.V("tensor_tensor", out=r3[:, 1:nch, :], in0=c3[:, 1:nch, :],
                                 in1=c3[:, 0:nch - 1, 15:16].to_broadcast([128, nch - 1, 16]), op=ALU.subtract)
                        kb.A("activation", out=t_[0][:, 0:n], in_=crel[:, 0:n], func=AF.Exp)
                        kb.V("tensor_tensor", out=Rt[P_][:, 0:n], in0=R_[P_][:, 0:n], in1=t_[0][:, 0:n], op=ALU.mult)
                        kb.V("tensor_copy", out=WC[P_][:, 0:nch], in_=t_[0][:, 0:n].rearrange("p (c i) -> p c i", i=16)[:, :, 15])
                        if P_ < 2:
                            kb.V("tensor_tensor", out=At[P_][:, 0:n], in0=A_[P_][:, 0:n], in1=t_[0][:, 0:n], op=ALU.mult)
                        else:
                            kb.V("tensor_tensor", out=t_[1][:, 0:n], in0=crel[:, 0:n], in1=lw[:, 0:n], op=ALU.subtract)
                            kb.A("activation", out=t_[1][:, 0:n], in_=t_[1][:, 0:n], func=AF.Exp)
                            kb.V("tensor_tensor", out=At[P_][:, 0:n], in0=A_[P_][:, 0:n], in1=t_[1][:, 0:n], op=ALU.mult)
                        kb.V("tensor_scalar", out=t_[2][:, 0:n], in0=crel[:, 0:n], scalar1=-1.0, scalar2=80.0,
                             op0=ALU.mult, op1=ALU.min)
                        kb.A("activation", out=t_[2][:, 0:n], in_=t_[2][:, 0:n], func=AF.Exp)
                        kb.V("tensor_tensor", out=Bt[P_][:, 0:n], in0=Bv[P_][:, 0:n], in1=t_[2][:, 0:n], op=ALU.mult)
                        if P_ >= 2:
                            kb.V("tensor_tensor", out=Kt[P_][:, 0:n], in0=Kv[P_][:, 0:n], in1=t_[2][:, 0:n], op=ALU.mult)
                        t33 = t_[3][:, 0:n].rearrange("p (c i) -> p c i", i=16)
                        kb.V("tensor_tensor", out=t33, in0=r3[:, :, 15:16].to_broadcast([128, nch, 16]), in1=r3, op=ALU.subtract)
                        kb.A("activation", out=t_[3][:, 0:n], in_=t_[3][:, 0:n], func=AF.Exp)
                        kb.V("tensor_tensor", out=Bp[P_][:, 0:n], in0=Bv[P_][:, 0:n], in1=t_[3][:, 0:n], op=ALU.mult)
                        if P_ >= 2:
                            kb.V("tensor_tensor", out=Kp[P_][:, 0:n], in0=Kv[P_][:, 0:n], in1=t_[3][:, 0:n], op=ALU.mult)
                    cut(23)
                    nsc = (n + 127) // 128
                    for sc in range(nsc):
                        o0 = sc * 128
                        m = min(128, n - o0)
                        ncs = m // 16
                        sl = slice(o0, o0 + m)
                        for P_ in range(NP):
                            rw = P_ >= 2
                            Ktp = Kt[P_] if rw else Bt[P_]
                            Kpp = Kp[P_] if rw else Bp[P_]
                            for hh in range(2):
                                ps_ = slice(64 * hh, 64 * hh + 64)
                                kb.mm(pb[0][0:m, 0:m], At[P_][ps_, sl], Bt[P_][ps_, sl])
                                kb.mm(pb[1][0:m, 0:m], Bt[P_][ps_, sl], At[P_][ps_, sl])
                                kb.V("tensor_tensor", out=Xs[0:m, 0:m], in0=pb[0][0:m, 0:m], in1=mstrict[0:m, 0:m], op=ALU.mult)
                                kb.V("tensor_tensor", out=XTs[0:m, 0:m], in0=pb[1][0:m, 0:m], in1=mstrictT[0:m, 0:m], op=ALU.mult)
                                if rw:
                                    kb.mm(pb[2][0:m, 0:m], At[P_][ps_, sl], Ktp[ps_, sl])
                                    kb.V("tensor_tensor", out=Aaks[0:m, 0:m], in0=pb[2][0:m, 0:m], in1=mstrict[0:m, 0:m], op=ALU.mult)
                                    aak = Aaks
                                else:
                                    aak = Xs
                                kb.V("tensor_tensor", out=PT[0][0:m, 0:m], in0=XTs[0:m, 0:m], in1=identf[0:m, 0:m], op=ALU.add)
                                yc, ytc, pc = Xs, XTs, 0
                                for lev in range(3):
                                    yn, ytn = Y[lev % 2], YT[lev % 2]
                                    kb.mm(pb[2][0:m, 0:m], ytc[0:m, 0:m], yc[0:m, 0:m])
                                    kb.cp(yn[0:m, 0:m], pb[2][0:m, 0:m])
                                    if lev < 2:
                                        kb.mm(pb[1][0:m, 0:m], yc[0:m, 0:m], ytc[0:m, 0:m])
                                        kb.cp(ytn[0:m, 0:m], pb[1][0:m, 0:m])
                                    kb.mm(pb[0][0:m, 0:m], yn[0:m, 0:m], PT[pc][0:m, 0:m])
                                    kb.V("tensor_tensor", out=PT[1 - pc][0:m, 0:m], in0=pb[0][0:m, 0:m], in1=PT[pc][0:m, 0:m], op=ALU.add)
                                    pc = 1 - pc
                                    yc, ytc = yn, ytn
                                TT = PT[pc]
                                kb.mm(pb[1][0:m, 0:m], aak[0:m, 0:m], TT[0:m, 0:m])
                                kb.cp(TAT[P_][hh][0:m, 0:m], pb[1][0:m, 0:m])
                                if hh == 0:
                                    kb.tr(pb[4][0:m, 0:128], At[P_][:, sl], identf[:])
                                    kb.cp(Atok[0:m, :], pb[4][0:m, 0:128])
                                    kb.tr(pb[4][0:m, 128:256], Vv[P_][:, sl], identf[:])
                                    kb.cp(Vtok[P_][0:m, :], pb[4][0:m, 128:256])
                                kb.mm(pb[2][ps_, 0:m], Atok[0:m, ps_], TT[0:m, 0:m])
                                kb.cp(WtT[P_][ps_, 0:m], pb[2][ps_, 0:m])
                                for c in range(ncs):
                                    cs_ = slice(o0 + c * 16, o0 + c * 16 + 16)
                                    kb.mm(pb[3][0:16, c * 16:(c + 1) * 16], Bt[P_][ps_, cs_], Rt[P_][ps_, cs_])
                                    if rw:
                                        kb.mm(pb[3][0:16, 128 + c * 16:128 + (c + 1) * 16], Ktp[ps_, cs_], Rt[P_][ps_, cs_])
                                kb.V("tensor_tensor", out=RBT[P_][hh][:, 0:m], in0=pb[3][0:16, 0:m], in1=mincl[:, 0:m], op=ALU.mult)
                                if rw:
                                    kb.V("tensor_tensor", out=RKT[P_][hh][:, 0:m], in0=pb[3][0:16, 128:128 + m], in1=mincl[:, 0:m], op=ALU.mult)
                            for c in range(ncs):
                                cs_ = slice(o0 + c * 16, o0 + c * 16 + 16)
                                kb.tr(pb[4][0:16, 0:128], Bp[P_][:, cs_], identf[:])
                                kb.tr(pb[4][0:16, 128:256], Kpp[:, cs_], identf[:])
                                kb.tr(pb[4][0:16, 256:384], Vv[P_][:, cs_], identf[:])
                                kb.cp(CM[P_][:, c, :], pb[4][0:16, 0:384])
                        cut(24)
                        for c in range(ncs):
                            cg = sc * 8 + c
                            cs_ = slice(o0 + c * 16, o0 + c * 16 + 16)
                            cl = slice(c * 16, c * 16 + 16)
                            Hc, Hn = Hs[hcur], Hs[1 - hcur]
                            U = Usb[cg % 2]
                            uv = UV[cg % 2]
                            pu, pH, pO = pb[5], pb[6], pb[0]
                            for P_ in range(NP):
                                for hh in range(2):
                                    ps_ = slice(64 * hh, 64 * hh + 64)
                                    oc = slice(P_ * 128 + hh * 64, P_ * 128 + hh * 64 + 64)
                                    kb.mm(pu[0:16, oc], WtT[P_][ps_, cl], Hc[ps_, P_, :], start=True, stop=False)
                                    kb.mm(pu[0:16, oc], TAT[P_][hh][0:m, cl], Vtok[P_][0:m, ps_], start=False, stop=True)
                            cut(30)
                            kb.A("activation", out=U[:], in_=pu[0:16, :], func=AF.Copy)
                            for P_ in range(2):
                                kb.V("tensor_tensor", out=uv[:, P_ * 128:(P_ + 1) * 128], in0=pu[0:16, P_ * 128:(P_ + 1) * 128],
                                     in1=CM[P_][:, c, 256:384], op=ALU.add)
                            cut(31)
                            for P_ in range(NP):
                                rw = P_ >= 2
                                for hh in range(2):
                                    ps_ = slice(64 * hh, 64 * hh + 64)
                                    oc = slice(P_ * 128 + hh * 64, P_ * 128 + hh * 64 + 64)
                                    if rw:
                                        kb.mm(pH[ps_, P_ * 64:(P_ + 1) * 64], CM[P_][:, c, 128 + 64 * hh:128 + 64 * hh + 64],
                                              CM[P_][:, c, 256 + 64 * hh:256 + 64 * hh + 64], start=True, stop=False)
                                        kb.mm(pH[ps_, P_ * 64:(P_ + 1) * 64], CM[P_][:, c, 64 * hh:64 * hh + 64], U[:, oc],
                                              start=False, stop=True)
                                    else:
                                        kb.mm(pH[ps_, P_ * 64:(P_ + 1) * 64], CM[P_][:, c, 64 * hh:64 * hh + 64],
                                              uv[:, P_ * 128 + hh * 64:P_ * 128 + hh * 64 + 64], start=True, stop=True)
                                    kb.mm(pO[ps_, P_ * 16:(P_ + 1) * 16], Hc[ps_, P_, :], Rt[P_][ps_, cs_], start=True, stop=False)
                                    if rw:
                                        kb.mm(pO[ps_, P_ * 16:(P_ + 1) * 16], U[:, oc], RBT[P_][hh][:, cl], start=False, stop=False)
                                        kb.mm(pO[ps_, P_ * 16:(P_ + 1) * 16], CM[P_][:, c, 256 + 64 * hh:256 + 64 * hh + 64],
                                              RKT[P_][hh][:, cl], start=False, stop=True)
                                    else:
                                        kb.mm(pO[ps_, P_ * 16:(P_ + 1) * 16], uv[:, P_ * 128 + hh * 64:P_ * 128 + hh * 64 + 64],
                                              RBT[P_][hh][:, cl], start=False, stop=True)
                            cut(33)
                            wcb = None
                            for P_ in range(NP):
                                kb.V("scalar_tensor_tensor", out=Hn[:, P_, :], in0=Hc[:, P_, :], scalar=WC[P_][:, cg:cg + 1],
                                     in1=pH[:, P_ * 64:(P_ + 1) * 64], op0=ALU.mult, op1=ALU.add)
                            for P_ in range(NP):
                                kb.cp(OT[:, P_, o0 + c * 16:o0 + c * 16 + 16], pO[:, P_ * 16:(P_ + 1) * 16])
                            hcur = 1 - hcur
                    cut(25)
                    for p in range(2):
                        o_ = OT[:, p, :]
                        kb.V("tensor_tensor", out=t_[0][:, 0:n], in0=o_[:, 0:n], in1=o_[:, 0:n], op=ALU.mult)
                        blocksum(pb[1], t_[0], n)
                        rsqrt_ps(t_[0][:, 0:n], pb[1][:, 0:n], 1.0 / 64.0, epsc)
                        kb.V("scalar_tensor_tensor", out=t_[0][:, 0:n], in0=o_[:, 0:n], scalar=C_GON(p), in1=t_[0][:, 0:n],
                             op0=ALU.mult, op1=ALU.mult)
                        kb.A("activation", out=t_[1][:, 0:n], in_=zg[p][:, 0:n], func=AF.Silu)
                        ys = yst[p]
                        kb.V("tensor_tensor", out=ys[:, 0:n], in0=t_[0][:, 0:n], in1=t_[1][:, 0:n], op=ALU.mult)
                        store_ymt(512 + p * 128, ys, n, col0, "c")
                    for p in range(2):
                        P_ = 2 + p
                        o_ = OT[:, P_, :]
                        blocksum(pb[1], o_, n)
                        kb.V("scalar_tensor_tensor", out=t_[0][:, 0:n], in0=pb[1][:, 0:n], scalar=-1.0 / 64.0, in1=o_[:, 0:n],
                             op0=ALU.mult, op1=ALU.add)
                        kb.V("tensor_tensor", out=t_[1][:, 0:n], in0=t_[0][:, 0:n], in1=t_[0][:, 0:n], op=ALU.mult)
                        blocksum(pb[2], t_[1], n)
                        rsqrt_ps(t_[1][:, 0:n], pb[2][:, 0:n], 1.0 / 64.0, eps_gn)
                        kb.V("tensor_tensor", out=t_[0][:, 0:n], in0=t_[0][:, 0:n], in1=t_[1][:, 0:n], op=ALU.mult)
                        kb.V("tensor_scalar", out=t_[0][:, 0:n], in0=t_[0][:, 0:n], scalar1=C_LNW(p), scalar2=C_LNB(p),
                             op0=ALU.mult, op1=ALU.add)
                        kb.V("scalar_tensor_tensor", out=t_[2][:, 0:n], in0=R_[P_][:, 0:n], scalar=C_RK(p), in1=Kv[P_][:, 0:n],
                             op0=ALU.mult, op1=ALU.mult)
                        blocksum(pb[3], t_[2], n)
                        kb.V("tensor_tensor", out=t_[2][:, 0:n], in0=pb[3][:, 0:n], in1=Vv[P_][:, 0:n], op=ALU.mult)
                        kb.V("tensor_tensor", out=t_[0][:, 0:n], in0=t_[0][:, 0:n], in1=t_[2][:, 0:n], op=ALU.add)
                        ys = yst[p]
                        kb.V("tensor_tensor", out=ys[:, 0:n], in0=t_[0][:, 0:n], in1=zg[P_][:, 0:n], op=ALU.mult)
                        store_ymt(768 + p * 128, ys, n, col0, "d")
                Hf = Hs[hcur]
                for cb in range(2):
                    kb.dma("sp", o_lru_conv[l, si, cb * 128:(cb + 1) * 128, :], lru_x[cb][:, 0:3])
                    kb.dma("sp", o_lru_h[l, si, cb * 128:(cb + 1) * 128, :], lru_h[cb][:])
                for ci in range(6):
                    kb.dma("sp", o_gdn_conv[l, si, ci * 128:(ci + 1) * 128, :], gdn_x[ci][:, 0:3])
                for c in range(8):
                    kb.dma("sp", o_rwkv_shift[l, si, c * 128:(c + 1) * 128, :], rw_x[c][:, 0:1])
                for p in range(2):
                    kb.dma("sp", o_gdn_s[l, si, p * 128:(p + 1) * 128, :], Hf[:, p, :])
                    kb.dma("sp", o_rwkv_s[l, si, p * 128:(p + 1) * 128, :], Hf[:, 2 + p, :])
                if hcur == 1:
                    pass
                return hcur

            segs_p = [(0, 16)] + [(128 + j * SEG, min(SEG, NFT * 128 - j * SEG)) for j in range((NFT * 128 + SEG - 1) // SEG)]
            run_sequence(0, segs_p)
            for s_ in range(NS):
                run_sequence(1 + s_, [(16 + 16 * s_, 16)])

    only = cfg.get("ONLY", "AMRB")
    for l in range(NL):
        if "A" in only:
            phaseA(l)
            barrier()
        if "M" in only:
            phaseMLA(l)
            kb.stop = False
            barrier()
        if "R" in only:
            phaseREC(l)
            kb.stop = False
            barrier()
        if "B" in only:
            phaseB(l, l == NL - 1)
            barrier()
    kb.P.emit()
    glob.close()
    global _LAST_KB
    _LAST_KB = kb
    return nc


def _consts(T, NS, PAST):
    c = {}
    c["identf"] = np.eye(128, dtype=np.float32)
    i = np.arange(128)
    same = (i[:, None] // 16) == (i[None, :] // 16)
    c["mstrict"] = (same & (i[None, :] < i[:, None])).astype(np.float32)
    c["mstrictT"] = (same & (i[:, None] < i[None, :])).astype(np.float32)
    j = np.arange(16)
    m16 = (j[:, None] <= j[None, :]).astype(np.float32)
    c["mincl16"] = np.tile(m16, (1, 8))
    c["blockones"] = ((i[:, None] // 64) == (i[None, :] // 64)).astype(np.float32)
    c["attmask"] = (~((i[:, None] >= 64) & (i[None, :] < 64))).astype(np.float32)
    pos = np.zeros(T, np.float64)
    pos[0:16] = np.arange(16)
    for s in range(NS):
        pos[16 + 16 * s:32 + 16 * s] = PAST + np.arange(16)
    pos[128:] = 16 + np.arange(T - 128)
    inv = (10000.0 ** (-np.arange(0, 32, 2, dtype=np.float32) / 32)).astype(np.float32)
    ang = pos.astype(np.float32)[:, None] * inv[None, :]
    c["cs"] = np.concatenate([np.cos(ang), np.sin(ang)], axis=1).astype(np.float32)
    return c


def _layer_tables(inp):
    L = inp["norm_mix"].shape[0]
    colp = np.zeros((L, 128, NCOL), np.float32)
    for l in range(L):
        for cb in range(2):
            sl = slice(cb * 128, (cb + 1) * 128)
            for j in range(4):
                colp[l, :, cb * 8 + j] = inp["lru_conv_w"][l, j, sl]
            colp[l, :, cb * 8 + 4] = inp["lru_conv_b"][l, sl]
            colp[l, :, cb * 8 + 5] = inp["lru_ba"][l, sl]
            colp[l, :, cb * 8 + 6] = inp["lru_bx"][l, sl]
            colp[l, :, cb * 8 + 7] = inp["lru_lambda"][l, sl]
        for ci in range(6):
            for j in range(4):
                colp[l, :, 16 + ci * 4 + j] = inp["gdn_conv_w"][l, j, ci * 128:(ci + 1) * 128]
        for p in range(2):
            hsel = np.repeat(np.arange(2 * p, 2 * p + 2), 64)
            colp[l, :, 40 + p] = inp["gdn_a_log"][l, hsel]
            colp[l, :, 42 + p] = inp["gdn_dt_bias"][l, hsel]
            colp[l, :, 44 + p] = np.tile(inp["gdn_o_norm"][l], 2)
            sl = slice(p * 128, (p + 1) * 128)
            colp[l, :, 54 + p] = inp["rwkv_w0"][l, sl]
            colp[l, :, 56 + p] = inp["rwkv_a0"][l, sl]
            colp[l, :, 58 + p] = inp["rwkv_k_k"][l, sl]
            colp[l, :, 60 + p] = inp["rwkv_k_a"][l, sl]
            colp[l, :, 62 + p] = inp["rwkv_r_k"][l].reshape(-1)[sl]
            colp[l, :, 64 + p] = inp["rwkv_ln_w"][l, sl]
            colp[l, :, 66 + p] = inp["rwkv_ln_b"][l, sl]
        for c in range(8):
            colp[l, :, 46 + c] = inp["rwkv_mu"][l, c * 128:(c + 1) * 128]
    wbd = np.zeros((L, 4, 128, 128), np.float32)
    for l in range(L):
        for wi, nm in enumerate(("lru_wa", "lru_wx")):
            for cb in range(2):
                for hh in range(2):
                    wbd[l, wi * 2 + cb, hh * 64:(hh + 1) * 64, hh * 64:(hh + 1) * 64] = inp[nm][l, cb * 2 + hh]
    t = {}
    t["colp"] = colp
    t["lru_wbd"] = wbd
    t["rwkv_wab"] = np.ascontiguousarray(np.concatenate([inp["rwkv_w_b"], inp["rwkv_a_b"]], axis=1))
    t["rwkv_gb"] = np.ascontiguousarray(inp["rwkv_g_b"])
    t["gdn_ab"] = np.ascontiguousarray(inp["w_in"][:, :, 1952:1960])
    t["mla_rows"] = np.ascontiguousarray(np.concatenate(
        [inp["mla_q_a_norm"], inp["mla_kv_a_norm"], inp["mla_q_norm"], inp["mla_k_norm"]], axis=1))
    return t


def make_in_maps(inp, n_cores, cfg):
    NFT, NS, PAST = cfg["NFT"], cfg["NS"], cfg["PAST"]
    T = (NFT + 1) * 128
    L = inp["norm_mix"].shape[0]
    consts = _consts(T, NS, PAST)
    tabs = _layer_tables(inp)
    shared = {}
    for nm in ("norm_ffn1", "ffn1_w1", "ffn1_w2", "norm_mix", "w_in", "w_out", "norm_ffn2", "ffn2_w1", "ffn2_w2",
               "mla_w_uq", "mla_w_ukv"):
        shared[nm] = np.ascontiguousarray(inp[nm])
    shared["final_norm"] = np.ascontiguousarray(inp["final_norm"].reshape(1, D))
    shared.update(tabs)
    shared.update(consts)
    maps = []
    for c in range(n_cores):
        b = c // 2
        ss = slice(c * NS, (c + 1) * NS)
        m = dict(shared)
        x0 = np.zeros((T, D), np.float32)
        x0[0:16] = inp["meta_tokens"]
        x0[16:16 + 16 * NS] = inp["x_sample"][ss].reshape(NS * 16, D)
        x0[128:] = inp["x_prompt"][b]
        m["x0"] = x0
        m["cache_ckv"] = np.ascontiguousarray(inp["cache_mla_ckv"][:, ss])
        m["cache_kpe"] = np.ascontiguousarray(inp["cache_mla_kpe"][:, ss])
        m["st_lru_conv"] = np.ascontiguousarray(inp["state_lru_conv"][:, ss].transpose(0, 1, 3, 2))
        m["st_lru_h"] = np.ascontiguousarray(inp["state_lru_h"][:, ss].reshape(L, NS, 256, 1))
        m["st_gdn_conv"] = np.ascontiguousarray(inp["state_gdn_conv"][:, ss].transpose(0, 1, 3, 2))
        m["st_gdn_s"] = np.ascontiguousarray(inp["state_gdn_s"][:, ss].reshape(L, NS, 256, 64))
        m["st_rwkv_shift"] = np.ascontiguousarray(inp["state_rwkv_shift"][:, ss].reshape(L, NS, 1024, 1))
        m["st_rwkv_s"] = np.ascontiguousarray(inp["state_rwkv_s"][:, ss].transpose(0, 1, 2, 4, 3).reshape(L, NS, 256, 64))
        maps.append(m)
    return maps


def assemble(res, n_cores, cfg, L):
    NFT, NS = cfg["NFT"], cfg["NS"]
    NB = n_cores // 2
    SEQ = NFT * 128
    f32 = np.float32
    yp = np.zeros((NB, SEQ, D), f32)
    ys = np.zeros((n_cores * NS, 16, D), f32)
    p_ckv = np.zeros((L, NB, 16 + SEQ, 128), f32)
    p_kpe = np.zeros((L, NB, 16 + SEQ, 32), f32)
    s_ckv = np.zeros((L, n_cores * NS, 16, 128), f32)
    s_kpe = np.zeros((L, n_cores * NS, 16, 32), f32)

    def mk(nb):
        return [np.zeros((L, nb, 3, 256), f32), np.zeros((L, nb, 256), f32), np.zeros((L, nb, 3, 768), f32),
                np.zeros((L, nb, 4, 64, 64), f32), np.zeros((L, nb, 1024), f32), np.zeros((L, nb, 4, 64, 64), f32)]
    pst = mk(NB)
    sst = mk(n_cores * NS)

    def put(dst, bi, r, si):
        dst[0][:, bi] = r["o_lru_conv"][:L, si].transpose(0, 2, 1)
        dst[1][:, bi] = r["o_lru_h"][:L, si, :, 0]
        dst[2][:, bi] = r["o_gdn_conv"][:L, si].transpose(0, 2, 1)
        dst[3][:, bi] = r["o_gdn_s"][:L, si].reshape(L, 4, 64, 64)
        dst[4][:, bi] = r["o_rwkv_shift"][:L, si, :, 0]
        dst[5][:, bi] = r["o_rwkv_s"][:L, si].reshape(L, 4, 64, 64).transpose(0, 1, 3, 2)

    for c in range(n_cores):
        r = res[c]
        if c % 2 == 0:
            b = c // 2
            yp[b] = r["yout"][128:]
            p_ckv[:, b, 0:16] = r["o_ckv"][:L, 0:16]
            p_ckv[:, b, 16:] = r["o_ckv"][:L, 128:]
            p_kpe[:, b, 0:16] = r["o_kpe"][:L, 0:16]
            p_kpe[:, b, 16:] = r["o_kpe"][:L, 128:]
            put(pst, b, r, 0)
        for s in range(NS):
            g = c * NS + s
            ys[g] = r["yout"][16 + 16 * s:32 + 16 * s]
            s_ckv[:, g] = r["o_ckv"][:L, 16 + 16 * s:32 + 16 * s]
            s_kpe[:, g] = r["o_kpe"][:L, 16 + 16 * s:32 + 16 * s]
            put(sst, g, r, 1 + s)
    return (yp, ys, p_ckv, p_kpe, *pst, s_ckv, s_kpe, *sst)


_CFG = dict(NFT=32, NS=4, PAST=4096, NTS=11)
_NC_CACHE = {}


def kernel(**inputs):
    inp = {k: np.asarray(v) for k, v in inputs.items()}
    cfg = _CFG
    if "nc" not in _NC_CACHE:
        _NC_CACHE["nc"] = build(cfg)
    nc = _NC_CACHE["nc"]
    maps = make_in_maps(inp, 8, cfg)
    res = run_bass_kernel_spmd(nc, maps, core_ids=list(range(8)))
    return assemble(res.results, 8, cfg, DEPTH)
```

```python
import contextlib
import math
import numpy as np
import concourse.bass as bass
import concourse.mybir as mybir
from concourse.bass_utils import run_bass_kernel_spmd

F32 = mybir.dt.float32
BF16 = mybir.dt.bfloat16
AF = mybir.ActivationFunctionType
ALU = mybir.AluOpType
AX = mybir.AxisListType

D = 1024
DFF = 2816
DEPTH = 4
IN_W = 2984
EPS = 1e-6
NCOL = 72
MIXW = 1024

STREAM = {"pe": "pe", "act": "act", "dve": "dve", "pool": "pool", "sp": "sp", "actq": "act", "poolq": "pool"}


class _Cut(Exception):
    pass


class Op:
    __slots__ = ("eng", "stream", "fn", "deps", "is_dma", "semkey", "dma_cnt", "sig", "sigcnt", "sidx", "gidx")


class Prog:
    def __init__(self, nc):
        self.nc = nc
        self.ops = []
        self.streams = {s: [] for s in ("pe", "act", "dve", "pool", "sp")}
        self.last_write = {}
        self.readers = {}
        self.dma_count = {}
        self.last_dma = {}
        self.seen = {s: {} for s in self.streams}
        self.bar = None
        self.excl = set()

    def op(self, eng, fn, reads=(), writes=(), dma=False, semkey=None, extra_deps=()):
        o = Op()
        o.eng = eng
        o.stream = STREAM[eng]
        o.fn = fn
        o.is_dma = dma
        o.sig = False
        o.gidx = len(self.ops)
        deps = [(d, "raw") for d in extra_deps]
        if self.bar is not None:
            deps.append((self.bar, "raw"))
        for k in reads:
            w = self.last_write.get(k)
            if w is not None:
                deps.append((w, "raw"))
            if k in self.excl:
                for r in self.readers.get(k, ()):
                    if r.stream != STREAM[eng]:
                        deps.append((r, "rar"))
        for k in writes:
            w = self.last_write.get(k)
            if w is not None:
                deps.append((w, "waw"))
            for r in self.readers.get(k, ()):
                deps.append((r, "war"))
        deps = [((self.last_dma[d.semkey], kind) if d.is_dma else (d, kind)) for d, kind in deps]
        if dma:
            assert semkey is not None
            o.semkey = semkey
            self.dma_count[semkey] = self.dma_count.get(semkey, 0) + 1
            o.dma_cnt = self.dma_count[semkey]
            self.last_dma[semkey] = o
        st = o.stream
        seen = self.seen[st]
        final = []
        for d, kind in deps:
            if d.is_dma:
                if seen.get(d.semkey, 0) >= d.dma_cnt:
                    continue
                if dma and d.semkey == o.semkey and kind == "waw":
                    continue
                seen[d.semkey] = d.dma_cnt
                final.append(d)
            else:
                if d.stream == st and not dma:
                    if st == "pe":
                        continue
                if seen.get(d.stream, -1) >= d.sidx:
                    continue
                seen[d.stream] = d.sidx
                d.sig = True
                final.append(d)
        o.deps = final
        o.sidx = len(self.streams[st])
        self.streams[st].append(o)
        self.ops.append(o)
        for k in writes:
            self.last_write[k] = o
            self.readers[k] = []
        for k in reads:
            self.readers.setdefault(k, []).append(o)
        return o

    def barrier(self, fn):
        deps = []
        for st, lst in self.streams.items():
            if lst and st != "sp":
                last = None
                for o in reversed(lst):
                    if not o.is_dma:
                        last = o
                        break
                if last is not None:
                    deps.append(last)
        deps += list(self.last_dma.values())
        self.bar = None
        b = self.op("dve", fn, extra_deps=deps)
        b.sig = True
        self.bar = b
        self.last_write = {}
        self.readers = {}

    def emit(self):
        nc = self.nc
        for st, lst in self.streams.items():
            c = 0
            for o in lst:
                if o.sig and not o.is_dma:
                    c += 1
                o.sigcnt = c
        with contextlib.ExitStack() as es:
            EP = 30000
            csem = {}
            for st in ("pe", "act", "dve", "pool"):
                lst = self.streams[st]
                nep = (lst[-1].sigcnt if lst else 0) // EP + 1
                csem[st] = [es.enter_context(nc.semaphore("c_%s%d" % (st, i))) for i in range(nep)]
            dsem = {}
            for i, k in enumerate(self.dma_count):
                dsem[k] = es.enter_context(nc.semaphore("d%d" % i))
            es.enter_context(nc.allow_non_contiguous_dma(reason="small state / parameter IO"))
            block = es.enter_context(nc.Block())

            def run(st):
                def body(e):
                    for o in self.streams[st]:
                        for d in o.deps:
                            if d.is_dma:
                                e.wait_ge(dsem[d.semkey], 16 * d.dma_cnt)
                            else:
                                k_ = (d.sigcnt - 1) // EP
                                e.wait_ge(csem[d.stream][k_], d.sigcnt - k_ * EP)
                        ins = o.fn(e)
                        if o.is_dma:
                            ins.then_inc(dsem[o.semkey], 16)
                        elif o.sig:
                            ins.then_inc(csem[st][(o.sigcnt - 1) // EP], 1)
                    if st == "sp":
                        for k, c in self.dma_count.items():
                            e.wait_ge(dsem[k], 16 * c)
                return body

            block.tensor(run("pe"))
            block.scalar(run("act"))
            block.vector(run("dve"))
            block.gpsimd(run("pool"))
            block.sync(run("sp"))


def _isap(v):
    return hasattr(v, "ap") and hasattr(v, "name") and hasattr(v, "shape") and not isinstance(v, (int, float))


class KB:
    def __init__(self, nc):
        self.nc = nc
        self.P = Prog(nc)
        self.cnt = 0
        self.sbnames = {}
        self.flip = 0
        self.stop = False

    def _gen(self, eng, method, kw, rk=None, wk=None):
        if self.stop:
            return
        reads, writes = [], []
        for k, v in kw.items():
            if _isap(v):
                (writes if k in ("out", "accum_out") else reads).append(v.name)
        if rk is not None:
            reads = list(rk)
        if wk is not None:
            writes = list(wk)
        self.P.op(eng, lambda e: getattr(e, method)(**kw), reads=reads, writes=writes)

    def V(self, method, rk=None, wk=None, **kw):
        self._gen("dve", method, kw, rk, wk)

    def A(self, method, rk=None, wk=None, **kw):
        self._gen("act", method, kw, rk, wk)

    def G(self, method, rk=None, wk=None, **kw):
        self._gen("pool", method, kw, rk, wk)

    def cp(self, out, in_):
        self.flip ^= 1
        if self.flip:
            self.A("activation", out=out, in_=in_, func=AF.Copy)
        else:
            self.V("tensor_copy", out=out, in_=in_)

    def mm(self, out, lhsT, rhs, start=True, stop=True):
        if self.stop:
            return
        self.P.op("pe", lambda e: e.matmul(out, lhsT=lhsT, rhs=rhs, start=start, stop=stop),
                  reads=[lhsT.name, rhs.name], writes=[out.name])

    def tr(self, out, in_, ident):
        if self.stop:
            return
        self.P.op("pe", lambda e: e.transpose(out, in_, ident), reads=[in_.name, ident.name], writes=[out.name])

    def dma(self, q, out, in_, rk=None, wk=None, semkey=None):
        if self.stop:
            return
        r = [in_.name] if rk is None else list(rk)
        w = [out.name] if wk is None else list(wk)
        if semkey is None:
            if out.name in self.sbnames:
                semkey = self.sbnames[out.name]
            elif in_.name in self.sbnames:
                semkey = self.sbnames[in_.name]
            else:
                self.cnt += 1
                semkey = "dd%d" % (self.cnt % 4)
        semkey = semkey + ("w" if q == "poolq" else "h")
        self.P.op(q, lambda e: e.dma_start(out=out, in_=in_), reads=r, writes=w, dma=True, semkey=semkey)


def build(cfg):
    NFT, NS, PAST, NTS = cfg["NFT"], cfg["NS"], cfg["PAST"], cfg["NTS"]
    NL = cfg.get("DEPTH", DEPTH)
    NT = NFT + 1
    T = NT * 128
    NSEQ = 1 + NS
    NKB = PAST // 128
    nc = bass.Bass("TRN2", target_bir_lowering=False)
    kb = KB(nc)
    glob = contextlib.ExitStack()
    uid = {"n": 0}

    pidx = {"n": 0}

    def sb(es, name, shape, dt=F32):
        uid["n"] += 1
        nm = "%s_%d" % (name, uid["n"])
        if es is glob:
            kb.sbnames[nm] = "g:" + nm
        else:
            pidx["n"] += 1
            kb.sbnames[nm] = "p%d" % pidx["n"]
        return es.enter_context(nc.sbuf_tensor(nm, list(shape), dt))

    def din(name, shape):
        return nc.dram_tensor(name, list(shape), F32, kind="ExternalInput").ap()

    def dout(name, shape):
        return nc.dram_tensor(name, list(shape), F32, kind="ExternalOutput").ap()

    def dscr(name, shape):
        return nc.dram_tensor(name, list(shape), F32, kind="Internal").ap()

    x0 = din("x0", [T, D])
    cs_d = din("cs", [T, 32])
    identf_d = din("identf", [128, 128])
    mstrict_d = din("mstrict", [128, 128])
    mstrictT_d = din("mstrictT", [128, 128])
    mincl_d = din("mincl16", [16, 128])
    bones_d = din("blockones", [128, 128])
    attm_d = din("attmask", [128, 128])
    cache_ckv = din("cache_ckv", [DEPTH, NS, PAST, 128])
    cache_kpe = din("cache_kpe", [DEPTH, NS, PAST, 32])
    st_lru_conv = din("st_lru_conv", [DEPTH, NS, 256, 3])
    st_lru_h = din("st_lru_h", [DEPTH, NS, 256, 1])
    st_gdn_conv = din("st_gdn_conv", [DEPTH, NS, 768, 3])
    st_gdn_s = din("st_gdn_s", [DEPTH, NS, 256, 64])
    st_rwkv_shift = din("st_rwkv_shift", [DEPTH, NS, 1024, 1])
    st_rwkv_s = din("st_rwkv_s", [DEPTH, NS, 256, 64])
    w = {}
    for nm, shp in (("norm_ffn1", [DEPTH, D]), ("ffn1_w1", [DEPTH, D, 2 * DFF]), ("ffn1_w2", [DEPTH, DFF, D]),
                    ("norm_mix", [DEPTH, D]), ("w_in", [DEPTH, D, IN_W]), ("w_out", [DEPTH, MIXW, D]),
                    ("norm_ffn2", [DEPTH, D]), ("ffn2_w1", [DEPTH, D, 2 * DFF]), ("ffn2_w2", [DEPTH, DFF, D]),
                    ("final_norm", [1, D]), ("mla_rows", [DEPTH, 576]), ("mla_w_uq", [DEPTH, 256, 384]),
                    ("mla_w_ukv", [DEPTH, 128, 512]), ("colp", [DEPTH, 128, NCOL]), ("lru_wbd", [DEPTH, 4, 128, 128]),
                    ("rwkv_wab", [DEPTH, 128, 256]), ("rwkv_gb", [DEPTH, 128, 256]), ("gdn_ab", [DEPTH, D, 8])):
        w[nm] = din(nm, shp)
    yout = dout("yout", [T, D])
    o_ckv = dout("o_ckv", [DEPTH, T, 128])
    o_kpe = dout("o_kpe", [DEPTH, T, 32])
    o_lru_conv = dout("o_lru_conv", [DEPTH, NSEQ, 256, 3])
    o_lru_h = dout("o_lru_h", [DEPTH, NSEQ, 256, 1])
    o_gdn_conv = dout("o_gdn_conv", [DEPTH, NSEQ, 768, 3])
    o_gdn_s = dout("o_gdn_s", [DEPTH, NSEQ, 256, 64])
    o_rwkv_shift = dout("o_rwkv_shift", [DEPTH, NSEQ, 1024, 1])
    o_rwkv_s = dout("o_rwkv_s", [DEPTH, NSEQ, 256, 64])
    X = dscr("Xres", [T, D])
    X1 = dscr("X1res", [T, D])
    CA = dscr("CA", [T, 416])
    NFC = 24
    CF = dscr("CF", [NFC * 128, T])
    YMT = dscr("YMT", [MIXW, T])

    identf = sb(glob, "identf", [128, 128])
    identb = sb(glob, "identb", [128, 128], BF16)
    mstrict = sb(glob, "mstrict", [128, 128])
    mstrictT = sb(glob, "mstrictT", [128, 128])
    mincl = sb(glob, "mincl", [16, 128])
    bones = sb(glob, "bones", [128, 128])
    attm = sb(glob, "attm", [128, 128], BF16)
    epsc = sb(glob, "epsc", [128, 1])
    eps_gn = sb(glob, "eps_gn", [128, 1])
    zeros = sb(glob, "zeros", [128, 512])
    ones = sb(glob, "ones", [128, 512])
    bar_t = sb(glob, "bar_t", [128, 1])
    pb = [glob.enter_context(nc.psum_tensor("pb%d" % i, [128, 512], F32)) for i in range(7)]
    ptb = glob.enter_context(nc.psum_tensor("ptb", [128, 1024], BF16))
    kb.P.excl = set(["pb%d" % i for i in range(7)] + ["ptb"])
    kb.dma("sp", identf[:], identf_d)
    kb.dma("poolq", identb[:], identf_d)
    kb.dma("sp", mstrict[:], mstrict_d)
    kb.dma("sp", mstrictT[:], mstrictT_d)
    kb.dma("sp", mincl[:], mincl_d)
    kb.dma("sp", bones[:], bones_d)
    kb.dma("poolq", attm[:], attm_d)
    kb.V("memset", ap=epsc[:], constant=EPS, wk=[epsc[:].name])
    kb.V("memset", ap=eps_gn[:], constant=64e-5, wk=[eps_gn[:].name])
    kb.V("memset", ap=zeros[:], constant=0.0, wk=[zeros[:].name])
    kb.V("memset", ap=ones[:], constant=1.0, wk=[ones[:].name])
    npad0 = 16 + 16 * NS
    for r in range(MIXW // 128):
        kb.dma("sp", YMT[r * 128:(r + 1) * 128, npad0:128], zeros[:, 0:128 - npad0], wk=[("YMT", "pad", r)])

    def cut(n_):
        if cfg.get('DBG', 99) == n_:
            kb.stop = True

    def barrier():
        kb.P.barrier(lambda e: e.memset(bar_t[:], 0.0))
        pidx["n"] = 0

    nst = (NT + NTS - 1) // NTS
    NTOK = NTS * 128
    pieces = []
    o_ = 0
    while o_ < NTOK:
        n_ = min(512, NTOK - o_)
        pieces.append((o_, n_))
        o_ += n_
    cnt = {"n": 0}

    def alloc_rowlocal(es):
        B = {}
        B["xs"] = sb(es, "xs", [128, NTS, D])
        B["xnT"] = sb(es, "xnT", [128, 8, NTOK], BF16)
        B["xn"] = [sb(es, "xn", [128, D], BF16) for _ in range(2)]
        B["gbc"] = sb(es, "gbc", [128, D])
        B["junk"] = sb(es, "junk", [128, D], BF16)
        B["ss"] = [sb(es, "ss", [128, 1]) for _ in range(2)]
        B["s2"] = [sb(es, "s2", [128, 1]) for _ in range(2)]
        B["rstd"] = [sb(es, "rstd", [128, 1]) for _ in range(2)]
        B["actT"] = [sb(es, "actT", [128, 2, NTOK], BF16) for _ in range(2)]
        B["W1g"] = [sb(es, "W1g", [128, 8, 512], BF16) for _ in range(2)]
        B["W2g"] = [sb(es, "W2g", [128, 2, D], BF16) for _ in range(2)]
        B["sg"] = [sb(es, "sg", [128, 512]) for _ in range(2)]
        return B

    def rms_rows(B, src_ap, n_free, i):
        kb.A("activation", out=B["junk"][:, 0:n_free], in_=src_ap, func=AF.Square, accum_out=B["ss"][i][:])
        kb.A("activation", out=B["s2"][i][:], in_=B["ss"][i][:], func=AF.Sqrt, scale=1.0 / n_free, bias=epsc[:])
        kb.V("reciprocal", out=B["rstd"][i][:], in_=B["s2"][i][:])

    def rmsnorm_T(B, ntl, gvec_ap):
        kb.dma("sp", B["gbc"][:], gvec_ap.broadcast_to([128, D]))
        for t in range(ntl):
            i = cnt["n"] % 2
            cnt["n"] += 1
            rms_rows(B, B["xs"][:, t, :], D, i)
            kb.V("scalar_tensor_tensor", out=B["xn"][i][:], in0=B["xs"][:, t, :], scalar=B["rstd"][i][:, 0:1],
                 in1=B["gbc"][:], op0=ALU.mult, op1=ALU.mult)
            for k in range(8):
                kb.tr(ptb[:, k * 128:(k + 1) * 128], B["xn"][i][:, k * 128:(k + 1) * 128], identb[:])
            kb.cp(B["xnT"][:, :, t * 128:(t + 1) * 128], ptb[:].rearrange("p (k t) -> p k t", k=8))

    def ffn(B, ntl, w1, w2):
        w1v = w1.rearrange("(k p) n -> p k n", p=128)
        w2v = w2.rearrange("(j p) n -> p j n", p=128)
        ntok = ntl * 128
        xs, xnT, actT, W1g, W2g, sg = B["xs"], B["xnT"], B["actT"], B["W1g"], B["W2g"], B["sg"]
        for hg in range(DFF // 256):
            b = hg % 2
            kb.dma("poolq", W1g[b][:, :, 0:256], w1v[:, :, hg * 256:(hg + 1) * 256])
            kb.dma("poolq", W1g[b][:, :, 256:512], w1v[:, :, DFF + hg * 256:DFF + (hg + 1) * 256])
            kb.dma("poolq", W2g[b][:], w2v[:, 2 * hg:2 * hg + 2, :])
            for (o, n) in pieces:
                if o >= ntok:
                    break
                n = min(n, ntok - o)
                for j in range(2):
                    gp, up = pb[2 * j], pb[2 * j + 1]
                    for gu, pp in ((0, gp), (1, up)):
                        for k in range(8):
                            kb.mm(pp[:, 0:n], W1g[b][:, k, gu * 256 + j * 128: gu * 256 + (j + 1) * 128],
                                  xnT[:, k, o:o + n], start=(k == 0), stop=(k == 7))
                    kb.A("activation", out=sg[j][:, 0:n], in_=gp[:, 0:n], func=AF.Silu)
                    kb.V("tensor_tensor", out=actT[b][:, j, o:o + n], in0=sg[j][:, 0:n], in1=up[:, 0:n], op=ALU.mult)
            for t in range(ntl):
                for dh in range(2):
                    pp = pb[4 + (t * 2 + dh) % 2]
                    for j in range(2):
                        kb.mm(pp[:], actT[b][:, j, t * 128:(t + 1) * 128], W2g[b][:, j, dh * 512:(dh + 1) * 512],
                              start=(j == 0), stop=(j == 1))
                    kb.V("scalar_tensor_tensor", out=xs[:, t, dh * 512:(dh + 1) * 512], in0=pp[:], scalar=0.5,
                         in1=xs[:, t, dh * 512:(dh + 1) * 512], op0=ALU.mult, op1=ALU.add)

    FCH = []
    for c in range(4):
        FCH.append(("col", 416 + c * 128))
    for c in range(8):
        FCH.append(("col", 928 + c * 128))
    for c in range(4):
        FCH.append(("rep", c))
    for c in range(8):
        FCH.append(("col", 1960 + c * 128))

    def phaseA(l):
        with contextlib.ExitStack() as es:
            B = alloc_rowlocal(es)
            winA = sb(es, "winA", [128, 8, 416], BF16)
            winF = [sb(es, "winF", [128, 8, 128], BF16) for _ in range(2)]
            wab = sb(es, "wab", [128, 8, 8], BF16)
            stageF = [sb(es, "stageF", [128, NTOK]) for _ in range(2)]
            stageA = [sb(es, "stageA", [128, 416]) for _ in range(2)]
            winv = w["w_in"][l].rearrange("(k p) n -> p k n", p=128)
            kb.dma("poolq", winA[:], winv[:, :, 0:416])
            kb.dma("poolq", wab[:], w["gdn_ab"][l].rearrange("(k p) n -> p k n", p=128))
            src = x0 if l == 0 else X
            for st in range(nst):
                t0 = st * NTS
                ntl = min(NTS, NT - t0)
                ntok = ntl * 128
                kb.dma("sp", B["xs"][:, 0:ntl, :], src[t0 * 128:(t0 + ntl) * 128, :].rearrange("(t p) d -> p t d", p=128),
                       rk=[("X", st)] if l > 0 else [])
                rmsnorm_T(B, ntl, w["norm_ffn1"][l:l + 1, :])
                ffn(B, ntl, w["ffn1_w1"][l], w["ffn1_w2"][l])
                kb.dma("sp", X1[t0 * 128:(t0 + ntl) * 128, :].rearrange("(t p) d -> p t d", p=128), B["xs"][:, 0:ntl, :],
                       wk=[("X1", st)])
                rmsnorm_T(B, ntl, w["norm_mix"][l:l + 1, :])
                xnT = B["xnT"]
                for t in range(ntl):
                    sa = stageA[t % 2]
                    for k in range(8):
                        kb.mm(pb[6][:, 0:416], xnT[:, k, t * 128:(t + 1) * 128], winA[:, k, :], start=(k == 0), stop=(k == 7))
                    kb.cp(sa[:], pb[6][:, 0:416])
                    kb.dma("sp", CA[(t0 + t) * 128:(t0 + t + 1) * 128, :], sa[:], wk=[("CA", t0 + t)])
                for fc, (kind, c0) in enumerate(FCH):
                    wf = winF[fc % 2]
                    if kind == "col":
                        kb.dma("poolq", wf[:], winv[:, :, c0:c0 + 128])
                    else:
                        for hh in range(2):
                            col = (c0 // 2) * 4 + (c0 % 2) * 2 + hh
                            kb.V("tensor_copy", out=wf[:, :, hh * 64:(hh + 1) * 64],
                                 in_=wab[:, :, col:col + 1].to_broadcast([128, 8, 64]))
                    sf = stageF[fc % 2]
                    for pi, (o, n) in enumerate(pieces):
                        if o >= ntok:
                            break
                        n = min(n, ntok - o)
                        pp = pb[pi % 4]
                        for k in range(8):
                            kb.mm(pp[:, 0:n], wf[:, k, :], xnT[:, k, o:o + n], start=(k == 0), stop=(k == 7))
                        kb.cp(sf[:, o:o + n], pp[:, 0:n])
                    kb.dma("sp", CF[fc * 128:(fc + 1) * 128, t0 * 128:t0 * 128 + ntok], sf[:, 0:ntok], wk=[("CF", fc, st)])

    def phaseB(l, last):
        with contextlib.ExitStack() as es:
            B = alloc_rowlocal(es)
            wout = sb(es, "wout", [128, 8, D], BF16)
            ymT = sb(es, "ymT", [128, 8, NTOK], BF16)
            kb.dma("poolq", wout[:], w["w_out"][l].rearrange("(k p) n -> p k n", p=128))
            if last:
                gfin = sb(es, "gfin", [128, D])
                kb.dma("sp", gfin[:], w["final_norm"].broadcast_to([128, D]))
            for st in range(nst):
                t0 = st * NTS
                ntl = min(NTS, NT - t0)
                ntok = ntl * 128
                xs = B["xs"]
                kb.dma("sp", xs[:, 0:ntl, :], X1[t0 * 128:(t0 + ntl) * 128, :].rearrange("(t p) d -> p t d", p=128),
                       rk=[("X1", st)])
                kb.dma("poolq", ymT[:, :, 0:ntok], YMT[:, t0 * 128:t0 * 128 + ntok].rearrange("(k p) t -> p k t", p=128),
                       rk=[("YMT", "all")])
                for t in range(ntl):
                    for dh in range(2):
                        pp = pb[4 + (t * 2 + dh) % 2]
                        for k in range(8):
                            kb.mm(pp[:], ymT[:, k, t * 128:(t + 1) * 128], wout[:, k, dh * 512:(dh + 1) * 512],
                                  start=(k == 0), stop=(k == 7))
                        kb.V("tensor_tensor", out=xs[:, t, dh * 512:(dh + 1) * 512], in0=pp[:],
                             in1=xs[:, t, dh * 512:(dh + 1) * 512], op=ALU.add)
                rmsnorm_T(B, ntl, w["norm_ffn2"][l:l + 1, :])
                ffn(B, ntl, w["ffn2_w1"][l], w["ffn2_w2"][l])
                if not last:
                    kb.dma("sp", X[t0 * 128:(t0 + ntl) * 128, :].rearrange("(t p) d -> p t d", p=128), xs[:, 0:ntl, :],
                           wk=[("X", st)])
                else:
                    for t in range(ntl):
                        i = cnt["n"] % 2
                        cnt["n"] += 1
                        rms_rows(B, xs[:, t, :], D, i)
                        kb.V("scalar_tensor_tensor", out=xs[:, t, :], in0=xs[:, t, :], scalar=B["rstd"][i][:, 0:1],
                             in1=gfin[:], op0=ALU.mult, op1=ALU.mult)
                    kb.dma("sp", yout[t0 * 128:(t0 + ntl) * 128, :].rearrange("(t p) d -> p t d", p=128), xs[:, 0:ntl, :])

    def phaseMLA(l):
        with contextlib.ExitStack() as es:
            LK = 16 + NFT * 128
            KT = sb(es, "KT", [96, 4, LK], BF16)
            VA = sb(es, "VA", [128, NT, 4, 72], BF16)
            QT0 = sb(es, "QT0", [96, 4, 128], BF16)
            KT0 = sb(es, "KT0", [96, 4, 128], BF16)
            ckvT0 = sb(es, "ckvT0", [128, 128], BF16)
            rows = sb(es, "mrows", [128, 576])
            wuq = sb(es, "wuq", [128, 2, 384], BF16)
            wukv = sb(es, "wukv", [128, 512], BF16)
            ca = [sb(es, "ca", [128, 416]) for _ in range(2)]
            cst = [sb(es, "cst", [128, 32]) for _ in range(2)]
            junk = sb(es, "mjunk", [128, 512])
            ssq = [sb(es, "ssq", [128, 4]) for _ in range(4)]
            cqn = sb(es, "cqn", [128, 256], BF16)
            cqT = sb(es, "cqT", [128, 2, 128], BF16)
            qf = sb(es, "qf", [128, 4, 96])
            qb = sb(es, "qb", [128, 4, 96], BF16)
            kf = sb(es, "kf", [128, 4, 96])
            kbb = sb(es, "kbb", [128, 4, 96], BF16)
            ckv = [sb(es, "ckv", [128, 128]) for _ in range(2)]
            kpe = [sb(es, "kpe", [128, 32]) for _ in range(2)]
            ckvb = sb(es, "ckvb", [128, 128], BF16)
            ckvT = sb(es, "ckvT", [128, 128], BF16)
            rtmp = sb(es, "rtmp", [128, 4, 16])
            rtmp2 = sb(es, "rtmp2", [128, 4, 16])
            qT = [sb(es, "qT", [96, 4, 128], BF16) for _ in range(2)]
            pT = [sb(es, "pT", [128, 128], BF16) for _ in range(3)]
            osb = sb(es, "osb", [128, 4, 64])
            rden = sb(es, "rden", [128, 4, 1])
            ost = [sb(es, "ost", [128, 128]) for _ in range(2)]
            vs = sb(es, "vs", [16, 4, 72], BF16)
            kTc = [sb(es, "kTc", [96, 4, 128], BF16) for _ in range(2)]
            vac = [sb(es, "vac", [128, 4, 72], BF16) for _ in range(2)]
            gq = rows[:, 384:480]
            gk = rows[:, 480:576]
            kb.dma("sp", rows[:], w["mla_rows"][l:l + 1, :].broadcast_to([128, 576]))
            kb.dma("poolq", wuq[:], w["mla_w_uq"][l].rearrange("(k p) n -> p k n", p=128))
            kb.dma("poolq", wukv[:], w["mla_w_ukv"][l])
            kb.V("tensor_scalar", out=gq, in0=gq, scalar1=1.0 / math.sqrt(96.0), scalar2=None, op0=ALU.mult)
            mc = {"n": 0}

            def headnorm(xf, xbf, gain, i):
                s = ssq[i % 4]
                kb.V("tensor_tensor", out=junk[:, 0:384].rearrange("p (h e) -> p h e", h=4), in0=xf[:], in1=xf[:], op=ALU.mult)
                kb.V("tensor_reduce", out=s[:], in_=junk[:, 0:384].rearrange("p (h e) -> p h e", h=4), axis=AX.X, op=ALU.add)
                kb.A("activation", out=s[:], in_=s[:], func=AF.Sqrt, scale=1.0 / 96.0, bias=epsc[:])
                kb.V("reciprocal", out=s[:], in_=s[:])
                kb.V("tensor_tensor", out=xf[:], in0=xf[:], in1=s[:].unsqueeze(2).to_broadcast([128, 4, 96]), op=ALU.mult)
                kb.V("tensor_tensor", out=xbf[:], in0=xf[:], in1=gain.unsqueeze(1).to_broadcast([128, 4, 96]), op=ALU.mult)

            def rope(dst_lo, dst_hi, x_lo, x_hi, cs_t, nh):
                cosb = cs_t[:, 0:16].unsqueeze(1).to_broadcast([128, nh, 16])
                sinb = cs_t[:, 16:32].unsqueeze(1).to_broadcast([128, nh, 16])
                a, b = rtmp[:, 0:nh, :], rtmp2[:, 0:nh, :]
                kb.V("tensor_tensor", out=a, in0=x_lo, in1=cosb, op=ALU.mult)
                kb.V("tensor_tensor", out=b, in0=x_hi, in1=sinb, op=ALU.mult)
                kb.V("tensor_tensor", out=a, in0=a, in1=b, op=ALU.subtract)
                kb.V("tensor_tensor", out=b, in0=x_lo, in1=sinb, op=ALU.mult)
                kb.V("tensor_tensor", out=dst_hi, in0=x_hi, in1=cosb, op=ALU.mult)
                kb.V("tensor_tensor", out=dst_hi, in0=dst_hi, in1=b, op=ALU.add)
                kb.V("tensor_copy", out=dst_lo, in_=a)

            def keys_from(ckv_ap, kpe_ap, kT_dst, va_dst, ckvT_keep=None):
                kb.cp(ckvb[:], ckv_ap)
                kb.tr(ptb[:, 0:128], ckvb[:], identb[:])
                tgt = ckvT if ckvT_keep is None else ckvT_keep
                kb.cp(tgt[:], ptb[:, 0:128])
                cut(10)
                kb.mm(pb[0][:], tgt[:], wukv[:])
                cut(13)
                kvv = pb[0][:].rearrange("p (h e) -> p h e", h=4)
                kb.cp(kf[:, :, 0:64], kvv[:, :, 0:64])
                cut(14)
                kb.V("tensor_copy", out=kf[:, :, 64:96], in_=kpe_ap.unsqueeze(1).to_broadcast([128, 4, 32]))
                cut(15)
                for h_ in range(4):
                    kb.V("tensor_copy", out=va_dst[:, h_, 0:64], in_=pb[0][:, h_ * 128 + 64:h_ * 128 + 128])
                cut(16)
                cut(11)
                mc["n"] += 1
                headnorm(kf, kbb, gk, mc["n"])
                cut(12)
                for h in range(4):
                    kb.tr(ptb[0:96, h * 128:(h + 1) * 128], kbb[:, h, :], identb[:])
                kb.cp(kT_dst, ptb[0:96, 0:512].rearrange("p (h t) -> p h t", h=4))

            def tile_pre(t):
                c = ca[t % 2]
                ct = cst[t % 2]
                kb.dma("sp", c[:], CA[t * 128:(t + 1) * 128, :], rk=[("CA", t)])
                kb.dma("sp", ct[:], cs_d[t * 128:(t + 1) * 128, :])
                cut(1)
                if t == 0:
                    s = ssq[0]
                    kb.A("activation", out=junk[:, 0:256], in_=c[:, 0:256], func=AF.Square, accum_out=s[:, 0:1])
                    kb.A("activation", out=s[:, 1:2], in_=s[:, 0:1], func=AF.Sqrt, scale=1.0 / 256.0, bias=epsc[:])
                    kb.V("reciprocal", out=s[:, 2:3], in_=s[:, 1:2])
                    kb.V("scalar_tensor_tensor", out=cqn[:], in0=c[:, 0:256], scalar=s[:, 2:3], in1=rows[:, 0:256],
                         op0=ALU.mult, op1=ALU.mult)
                    for k in range(2):
                        kb.tr(ptb[:, k * 128:(k + 1) * 128], cqn[:, k * 128:(k + 1) * 128], identb[:])
                    kb.cp(cqT[:], ptb[:, 0:256].rearrange("p (k t) -> p k t", k=2))
                    cut(2)
                    for k in range(2):
                        kb.mm(pb[0][:, 0:384], cqT[:, k, :], wuq[:, k, :], start=(k == 0), stop=(k == 1))
                    kb.cp(qf[:], pb[0][:, 0:384].rearrange("p (h e) -> p h e", h=4))
                    cut(3)
                    rope(qf[:, :, 64:80], qf[:, :, 80:96], qf[:, :, 64:80], qf[:, :, 80:96], ct, 4)
                    cut(4)
                    mc["n"] += 1
                    headnorm(qf, qb, gq, mc["n"])
                    cut(5)
                s = ssq[1]
                kb.A("activation", out=junk[:, 0:128], in_=c[:, 256:384], func=AF.Square, accum_out=s[:, 0:1])
                kb.A("activation", out=s[:, 1:2], in_=s[:, 0:1], func=AF.Sqrt, scale=1.0 / 128.0, bias=epsc[:])
                kb.V("reciprocal", out=s[:, 2:3], in_=s[:, 1:2])
                cv = ckv[t % 2]
                kp = kpe[t % 2]
                kb.V("scalar_tensor_tensor", out=cv[:], in0=c[:, 256:384], scalar=s[:, 2:3], in1=rows[:, 256:384],
                     op0=ALU.mult, op1=ALU.mult)
                rope(kp[:, 0:16].unsqueeze(1), kp[:, 16:32].unsqueeze(1), c[:, 384:400].unsqueeze(1),
                     c[:, 400:416].unsqueeze(1), ct, 1)
                cut(7)
                kb.dma("sp", o_ckv[l, t * 128:(t + 1) * 128, :], cv[:])
                kb.dma("sp", o_kpe[l, t * 128:(t + 1) * 128, :], kp[:])
                cut(8)
                return cv, kp

            def q_transpose(dst):
                for h in range(4):
                    kb.tr(ptb[0:96, h * 128:(h + 1) * 128], qb[:, h, :], identb[:])
                kb.cp(dst, ptb[0:96, 0:512].rearrange("p (h t) -> p h t", h=4))
                cut(9)

            def finish(nq):
                for h in range(4):
                    kb.V("reciprocal", out=rden[0:nq, h, :], in_=pb[3 + h][0:nq, 64:65])
                    kb.V("tensor_scalar", out=osb[0:nq, h, :], in0=pb[3 + h][0:nq, 0:64], scalar1=rden[0:nq, h, :],
                         scalar2=None, op0=ALU.mult)

            def attend(q_ap_fn, nq, blocks, out_rows_fn):
                nb = len(blocks)
                steps = [(bi, h) for bi in range(nb) for h in range(4)]

                def score(i):
                    bi, h = steps[i]
                    kfn, vfn, nk, masked = blocks[bi]
                    ps_ = pb[1 + i % 2]
                    pt_ = pT[i % 3]
                    kb.mm(ps_[0:nk, 0:nq], kfn(h), q_ap_fn(h))
                    kb.A("activation", out=pt_[0:nk, 0:nq], in_=ps_[0:nk, 0:nq], func=AF.Exp)
                    if masked:
                        kb.V("tensor_tensor", out=pt_[0:nk, 0:nq], in0=pt_[0:nk, 0:nq], in1=attm[0:nk, 0:nq], op=ALU.mult)

                def pv(i):
                    bi, h = steps[i]
                    kfn, vfn, nk, masked = blocks[bi]
                    pt_ = pT[i % 3]
                    kb.mm(pb[3 + h][0:nq, 0:65], pt_[0:nk, 0:nq], vfn(h), start=(bi == 0), stop=(bi == nb - 1))

                score(0)
                for i in range(len(steps)):
                    if i + 1 < len(steps):
                        score(i + 1)
                    pv(i)
                finish(nq)
                for k in range(2):
                    kb.tr(pb[0][:, k * 128:k * 128 + nq], osb[0:nq, 2 * k:2 * k + 2, :].rearrange("p h e -> p (h e)"),
                          identf[0:nq, 0:nq])
                    so = ost[k]
                    kb.cp(so[:, 0:nq], pb[0][:, k * 128:k * 128 + nq])
                    out_rows_fn(k, so)

            kb.V("memset", ap=VA[:].rearrange("p a b c -> p (a b c)"), constant=1.0, wk=[VA[:].name])
            kb.V("memset", ap=vs[:].rearrange("p b c -> p (b c)"), constant=1.0, wk=[vs[:].name])
            for v_ in vac:
                kb.V("memset", ap=v_[:].rearrange("p b c -> p (b c)"), constant=1.0, wk=[v_[:].name])
            mcut = cfg.get('MCUT', 99)
            if mcut <= 0:
                return
            cv0 = kp0 = None
            for t in range(NT):
                cv, kp = tile_pre(t)
                if t == 0:
                    q_transpose(QT0[:])
                    keys_from(cv[:], kp[:], KT0[:], VA[:, 0, :, :], ckvT_keep=ckvT0)
                    kb.cp(KT[:, :, 0:16], KT0[:, :, 0:16])
                else:
                    keys_from(cv[:], kp[:], KT[:, :, 16 + (t - 1) * 128: 16 + t * 128], VA[:, t, :, :])
            if mcut <= 1:
                return
            attend(lambda h: QT0[:, h, 0:16], 16,
                   [(lambda h: KT[:, h, 0:16], lambda h: VA[0:16, 0, h, 0:65], 16, False)],
                   lambda k, so: kb.dma("sp", YMT[k * 128:(k + 1) * 128, 0:16], so[:, 0:16], wk=[("YMT", "a0", k)]))
            if mcut <= 2:
                return
            for s_ in range(NS):
                r0 = 16 + 16 * s_
                kb.mm(pb[0][0:16, :], ckvT0[:, r0:r0 + 16], wukv[:])
                for h_ in range(4):
                    kb.V("tensor_copy", out=vs[:, h_, 0:64], in_=pb[0][0:16, h_ * 128 + 64:h_ * 128 + 128])
                blocks = []
                for b_ in range(NKB):
                    cc = ckv[b_ % 2]
                    kk = kpe[b_ % 2]
                    kb.dma("sp", cc[:], cache_ckv[l, s_, b_ * 128:(b_ + 1) * 128, :])
                    kb.dma("sp", kk[:], cache_kpe[l, s_, b_ * 128:(b_ + 1) * 128, :])
                    ktc = kTc[b_ % 2]
                    vc_ = vac[b_ % 2]
                    keys_from(cc[:], kk[:], ktc[:], vc_[:])
                    blocks.append((ktc, vc_))
                    for h in range(4):
                        mc["n"] += 1
                        ps_ = pb[1 + mc["n"] % 2]
                        pt_ = pT[mc["n"] % 3]
                        kb.mm(ps_[:, 0:16], ktc[:, h, :], QT0[:, h, r0:r0 + 16])
                        kb.A("activation", out=pt_[:, 0:16], in_=ps_[:, 0:16], func=AF.Exp)
                        kb.mm(pb[3 + h][0:16, 0:65], pt_[:, 0:16], vc_[:, h, 0:65], start=(b_ == 0), stop=False)
                for h in range(4):
                    mc["n"] += 1
                    ps_ = pb[1 + mc["n"] % 2]
                    pt_ = pT[mc["n"] % 3]
                    kb.mm(ps_[0:16, 0:16], KT0[:, h, r0:r0 + 16], QT0[:, h, r0:r0 + 16])
                    kb.A("activation", out=pt_[0:16, 0:16], in_=ps_[0:16, 0:16], func=AF.Exp)
                    kb.mm(pb[3 + h][0:16, 0:65], pt_[0:16, 0:16], vs[:, h, 0:65], start=(NKB == 0), stop=True)
                finish(16)
                for k in range(2):
                    kb.tr(pb[0][:, k * 128:k * 128 + 16], osb[0:16, 2 * k:2 * k + 2, :].rearrange("p h e -> p (h e)"),
                          identf[0:16, 0:16])
                    so = ost[k]
                    kb.cp(so[:, 0:16], pb[0][:, k * 128:k * 128 + 16])
                    kb.dma("sp", YMT[k * 128:(k + 1) * 128, r0:r0 + 16], so[:, 0:16], wk=[("YMT", "as", k, s_)])
            if mcut <= 3:
                return
            for t in range(1, NT):
                c = ca[t % 2]
                ct = cst[t % 2]
                kb.dma("sp", c[:], CA[t * 128:(t + 1) * 128, :], rk=[("CA", t)])
                kb.dma("sp", ct[:], cs_d[t * 128:(t + 1) * 128, :])
                s = ssq[0]
                kb.A("activation", out=junk[:, 0:256], in_=c[:, 0:256], func=AF.Square, accum_out=s[:, 0:1])
                kb.A("activation", out=s[:, 1:2], in_=s[:, 0:1], func=AF.Sqrt, scale=1.0 / 256.0, bias=epsc[:])
                kb.V("reciprocal", out=s[:, 2:3], in_=s[:, 1:2])
                kb.V("scalar_tensor_tensor", out=cqn[:], in0=c[:, 0:256], scalar=s[:, 2:3], in1=rows[:, 0:256],
                     op0=ALU.mult, op1=ALU.mult)
                for k in range(2):
                    kb.tr(ptb[:, k * 128:(k + 1) * 128], cqn[:, k * 128:(k + 1) * 128], identb[:])
                kb.cp(cqT[:], ptb[:, 0:256].rearrange("p (k t) -> p k t", k=2))
                for k in range(2):
                    kb.mm(pb[0][:, 0:384], cqT[:, k, :], wuq[:, k, :], start=(k == 0), stop=(k == 1))
                kb.cp(qf[:], pb[0][:, 0:384].rearrange("p (h e) -> p h e", h=4))
                rope(qf[:, :, 64:80], qf[:, :, 80:96], qf[:, :, 64:80], qf[:, :, 80:96], ct, 4)
                mc["n"] += 1
                headnorm(qf, qb, gq, mc["n"])
                qTt = qT[t % 2]
                q_transpose(qTt[:])
                blocks = [(lambda h: KT[:, h, 0:16], lambda h: VA[0:16, 0, h, 0:65], 16, False)]
                for kt in range(1, t + 1):
                    blocks.append((lambda h, kt=kt: KT[:, h, 16 + (kt - 1) * 128:16 + kt * 128],
                                   lambda h, kt=kt: VA[:, kt, h, 0:65], 128, kt == t))
                attend(lambda h: qTt[:, h, :], 128, blocks,
                       lambda k, so, t=t: kb.dma("sp", YMT[k * 128:(k + 1) * 128, t * 128:(t + 1) * 128], so[:],
                                                 wk=[("YMT", "af", k, t)]))

    SEG = 256

    def phaseREC(l):
        with contextlib.ExitStack() as es:
            colp = sb(es, "colp", [128, NCOL])
            wbd = sb(es, "wbd", [128, 4, 128])
            wabr = sb(es, "wabr", [128, 256])
            gbr = sb(es, "gbr", [128, 256])
            negc = sb(es, "negc", [128, 2])
            negA = sb(es, "negA", [128, 2])
            kb.dma("sp", colp[:], w["colp"][l])
            kb.dma("sp", wbd[:], w["lru_wbd"][l].rearrange("c p m -> p c m"))
            kb.dma("sp", wabr[:], w["rwkv_wab"][l])
            kb.dma("sp", gbr[:], w["rwkv_gb"][l])
            C_LRU = lambda cb, j: colp[:, cb * 8 + j: cb * 8 + j + 1]
            C_GCW = lambda ci, j: colp[:, 16 + ci * 4 + j: 16 + ci * 4 + j + 1]
            C_GAL = lambda p: colp[:, 40 + p: 41 + p]
            C_GDT = lambda p: colp[:, 42 + p: 43 + p]
            C_GON = lambda p: colp[:, 44 + p: 45 + p]
            C_MU = lambda c: colp[:, 46 + c: 47 + c]
            C_W0 = lambda p: colp[:, 54 + p: 55 + p]
            C_A0 = lambda p: colp[:, 56 + p: 57 + p]
            C_KK = lambda p: colp[:, 58 + p: 59 + p]
            C_KA = lambda p: colp[:, 60 + p: 61 + p]
            C_RK = lambda p: colp[:, 62 + p: 63 + p]
            C_LNW = lambda p: colp[:, 64 + p: 65 + p]
            C_LNB = lambda p: colp[:, 66 + p: 67 + p]
            for cb in range(2):
                kb.A("activation", out=negc[:, cb:cb + 1], in_=C_LRU(cb, 7), func=AF.Exp, scale=-1.0)
                kb.A("activation", out=negc[:, cb:cb + 1], in_=negc[:, cb:cb + 1], func=AF.Ln, bias=ones[:, 0:1])
                kb.V("tensor_scalar", out=negc[:, cb:cb + 1], in0=negc[:, cb:cb + 1], scalar1=-8.0, scalar2=None, op0=ALU.mult)
            for p in range(2):
                kb.A("activation", out=negA[:, p:p + 1], in_=C_GAL(p), func=AF.Exp)
                kb.V("tensor_scalar", out=negA[:, p:p + 1], in0=negA[:, p:p + 1], scalar1=-1.0, scalar2=None, op0=ALU.mult)

            lru_x = [sb(es, "lru_x", [128, 3 + SEG]) for _ in range(2)]
            lru_h = [sb(es, "lru_h", [128, 1]) for _ in range(2)]
            gdn_x = [sb(es, "gdn_x", [128, 3 + SEG]) for _ in range(6)]
            rw_x = [sb(es, "rw_x", [128, 1 + SEG]) for _ in range(8)]
            Hs = [sb(es, "Hs", [128, 4, 64]) for _ in range(2)]
            def W(name):
                return sb(es, name, [128, SEG])
            t_ = [W("t%d" % i) for i in range(8)]
            lr_xc, lr_r, lr_i, lr_h, lr_g = W("lr_xc"), W("lr_r"), W("lr_i"), W("lr_h"), W("lr_g")
            yst = [W("yst") for _ in range(2)]
            xm = [W("xm%d" % i) for i in range(8)]
            agate = W("agate")
            NP = 4
            R_ = [W("R") for _ in range(NP)]
            A_ = [W("A") for _ in range(NP)]
            Bv = [W("Bv") for _ in range(NP)]
            Kv = [None, None, W("Kv2"), W("Kv3")]
            Vv = [W("V") for _ in range(NP)]
            logw = [W("logw") for _ in range(NP)]
            zg = [W("zg") for _ in range(NP)]
            Rt = [W("Rt") for _ in range(NP)]
            At = [W("At") for _ in range(NP)]
            Bt = [W("Bt") for _ in range(NP)]
            Kt = [None, None, W("Kt2"), W("Kt3")]
            Bp = [W("Bp") for _ in range(NP)]
            Kp = [None, None, W("Kp2"), W("Kp3")]
            cum = W("cum")
            crel = W("crel")
            WC = [sb(es, "WC", [128, SEG // 16]) for _ in range(NP)]
            OT = sb(es, "OT", [128, NP, SEG])
            Xs, XTs, Aaks = [sb(es, nm_, [128, 128]) for nm_ in ("Xs", "XTs", "Aaks")]
            Y = [sb(es, "Y", [128, 128]) for _ in range(2)]
            YT = [sb(es, "YT", [128, 128]) for _ in range(2)]
            PT = [sb(es, "PT", [128, 128]) for _ in range(2)]
            TAT = [[sb(es, "TAT", [128, 128]) for _ in range(2)] for _ in range(NP)]
            WtT = [sb(es, "WtT", [128, 128]) for _ in range(NP)]
            Atok = sb(es, "Atok", [128, 128])
            Vtok = [sb(es, "Vtok", [128, 128]) for _ in range(NP)]
            RBT = [[sb(es, "RBT", [16, 128]) for _ in range(2)] for _ in range(NP)]
            RKT = [[None, None], [None, None]] + [[sb(es, "RKT", [16, 128]) for _ in range(2)] for _ in range(2)]
            CM = [sb(es, "CM", [16, 8, 384]) for _ in range(NP)]
            Usb = [sb(es, "Usb", [16, 512]) for _ in range(2)]
            UV = [sb(es, "UV", [16, 256]) for _ in range(2)]

            def blocksum(out_ps, src, n):
                kb.mm(out_ps[:, 0:n], bones[:], src[:, 0:n])

            def rsqrt_ps(dst, ps_ap, scale, bias_tile):
                kb.A("activation", out=dst, in_=ps_ap, func=AF.Sqrt, scale=scale, bias=bias_tile[:])
                kb.V("reciprocal", out=dst, in_=dst)

            def conv4(dst, xbuf, n, wcol, bias_col=None):
                if bias_col is None:
                    kb.V("tensor_scalar", out=dst[:, 0:n], in0=xbuf[:, 0:n], scalar1=wcol(0), scalar2=None, op0=ALU.mult)
                else:
                    kb.V("tensor_scalar", out=dst[:, 0:n], in0=xbuf[:, 0:n], scalar1=wcol(0), scalar2=bias_col,
                         op0=ALU.mult, op1=ALU.add)
                for j in range(1, 4):
                    kb.V("scalar_tensor_tensor", out=dst[:, 0:n], in0=xbuf[:, j:j + n], scalar=wcol(j), in1=dst[:, 0:n],
                         op0=ALU.mult, op1=ALU.add)

            def store_ymt(row0, src, n, col0, tag):
                kb.dma("sp", YMT[row0:row0 + 128, col0:col0 + n], src[:, 0:n], wk=[("YMT", tag, row0, col0)])

            def run_sequence(si, segs):
                if si == 0:
                    for cb in range(2):
                        kb.V("memset", ap=lru_x[cb][:, 0:3], constant=0.0, wk=[lru_x[cb][:].name])
                        kb.V("memset", ap=lru_h[cb][:], constant=0.0, wk=[lru_h[cb][:].name])
                    for ci in range(6):
                        kb.V("memset", ap=gdn_x[ci][:, 0:3], constant=0.0, wk=[gdn_x[ci][:].name])
                    for c in range(8):
                        kb.V("memset", ap=rw_x[c][:, 0:1], constant=0.0, wk=[rw_x[c][:].name])
                    kb.V("memset", ap=Hs[0][:], constant=0.0, wk=[Hs[0][:].name])
                else:
                    s_ = si - 1
                    for cb in range(2):
                        kb.dma("sp", lru_x[cb][:, 0:3], st_lru_conv[l, s_, cb * 128:(cb + 1) * 128, :])
                        kb.dma("sp", lru_h[cb][:], st_lru_h[l, s_, cb * 128:(cb + 1) * 128, :])
                    for ci in range(6):
                        kb.dma("sp", gdn_x[ci][:, 0:3], st_gdn_conv[l, s_, ci * 128:(ci + 1) * 128, :])
                    for c in range(8):
                        kb.dma("sp", rw_x[c][:, 0:1], st_rwkv_shift[l, s_, c * 128:(c + 1) * 128, :])
                    for p in range(2):
                        kb.dma("sp", Hs[0][:, p, :], st_gdn_s[l, s_, p * 128:(p + 1) * 128, :])
                        kb.dma("sp", Hs[0][:, 2 + p, :], st_rwkv_s[l, s_, p * 128:(p + 1) * 128, :])
                hcur = 0
                for (col0, n) in segs:
                    nch = n // 16
                    for cb in range(2):
                        xb_ = lru_x[cb]
                        kb.dma("sp", xb_[:, 3:3 + n], CF[cb * 128:(cb + 1) * 128, col0:col0 + n], rk=[("CF", "all")])
                        kb.dma("sp", lr_g[:, 0:n], CF[(2 + cb) * 128:(3 + cb) * 128, col0:col0 + n], rk=[("CF", "all")])
                        conv4(lr_xc, xb_, n, lambda j: C_LRU(cb, j), C_LRU(cb, 4))
                        kb.mm(pb[0][:, 0:n], wbd[:, cb, :], lr_xc[:, 0:n])
                        kb.mm(pb[1][:, 0:n], wbd[:, 2 + cb, :], lr_xc[:, 0:n])
                        kb.A("activation", out=lr_r[:, 0:n], in_=pb[0][:, 0:n], func=AF.Sigmoid, bias=C_LRU(cb, 5))
                        kb.A("activation", out=lr_i[:, 0:n], in_=pb[1][:, 0:n], func=AF.Sigmoid, bias=C_LRU(cb, 6))
                        kb.A("activation", out=lr_r[:, 0:n], in_=lr_r[:, 0:n], func=AF.Exp, scale=negc[:, cb:cb + 1])
                        kb.V("tensor_tensor", out=t_[0][:, 0:n], in0=lr_r[:, 0:n], in1=lr_r[:, 0:n], op=ALU.mult)
                        kb.V("tensor_scalar", out=t_[0][:, 0:n], in0=t_[0][:, 0:n], scalar1=-1.0, scalar2=1.0,
                             op0=ALU.mult, op1=ALU.add)
                        kb.A("activation", out=t_[0][:, 0:n], in_=t_[0][:, 0:n], func=AF.Sqrt)
                        kb.V("tensor_tensor", out=lr_i[:, 0:n], in0=lr_i[:, 0:n], in1=lr_xc[:, 0:n], op=ALU.mult)
                        kb.V("tensor_tensor", out=lr_i[:, 0:n], in0=lr_i[:, 0:n], in1=t_[0][:, 0:n], op=ALU.mult)
                        kb.V("tensor_tensor_scan", out=lr_h[:, 0:n], data0=lr_r[:, 0:n], data1=lr_i[:, 0:n],
                             initial=lru_h[cb][:, 0:1], op0=ALU.mult, op1=ALU.add)
                        kb.V("tensor_copy", out=lru_h[cb][:], in_=lr_h[:, n - 1:n])
                        kb.V("tensor_tensor", out=t_[1][:, 0:n], in0=lr_g[:, 0:n], in1=lr_g[:, 0:n], op=ALU.mult)
                        kb.V("tensor_scalar", out=t_[1][:, 0:n], in0=t_[1][:, 0:n], scalar1=0.044715, scalar2=1.0,
                             op0=ALU.mult, op1=ALU.add)
                        kb.V("tensor_tensor", out=t_[1][:, 0:n], in0=t_[1][:, 0:n], in1=lr_g[:, 0:n], op=ALU.mult)
                        kb.A("activation", out=t_[1][:, 0:n], in_=t_[1][:, 0:n], func=AF.Sigmoid, scale=1.5957691216057308)
                        kb.V("tensor_tensor", out=t_[1][:, 0:n], in0=t_[1][:, 0:n], in1=lr_g[:, 0:n], op=ALU.mult)
                        ys = yst[cb]
                        kb.V("tensor_tensor", out=ys[:, 0:n], in0=t_[1][:, 0:n], in1=lr_h[:, 0:n], op=ALU.mult)
                        store_ymt(256 + cb * 128, ys, n, col0, "b")
                        kb.V("tensor_copy", out=t_[2][:, 0:3], in_=xb_[:, n:n + 3])
                        kb.V("tensor_copy", out=xb_[:, 0:3], in_=t_[2][:, 0:3])
                    cut(20)
                    for ci in range(6):
                        kb.dma("sp", gdn_x[ci][:, 3:3 + n], CF[(4 + ci) * 128:(5 + ci) * 128, col0:col0 + n], rk=[("CF", "all")])
                    for p in range(2):
                        qc, kc, vc = t_[0], t_[1], Vv[p]
                        for dst, ci in ((qc, p), (kc, 2 + p), (vc, 4 + p)):
                            conv4(dst, gdn_x[ci], n, lambda j, ci=ci: C_GCW(ci, j))
                            kb.A("activation", out=dst[:, 0:n], in_=dst[:, 0:n], func=AF.Silu)
                        for src, dst, sc in ((qc, R_[p], 64 ** -0.5), (kc, t_[2], 1.0)):
                            kb.V("tensor_tensor", out=t_[3][:, 0:n], in0=src[:, 0:n], in1=src[:, 0:n], op=ALU.mult)
                            blocksum(pb[0], t_[3], n)
                            rsqrt_ps(t_[3][:, 0:n], pb[0][:, 0:n], 1.0, epsc)
                            kb.V("scalar_tensor_tensor", out=dst[:, 0:n], in0=src[:, 0:n], scalar=sc, in1=t_[3][:, 0:n],
                                 op0=ALU.mult, op1=ALU.mult)
                        kn = t_[2]
                        kb.dma("sp", t_[4][:, 0:n], CF[(12 + p) * 128:(13 + p) * 128, col0:col0 + n], rk=[("CF", "all")])
                        kb.dma("sp", t_[5][:, 0:n], CF[(14 + p) * 128:(15 + p) * 128, col0:col0 + n], rk=[("CF", "all")])
                        kb.dma("sp", zg[p][:, 0:n], CF[(10 + p) * 128:(11 + p) * 128, col0:col0 + n], rk=[("CF", "all")])
                        kb.A("activation", out=t_[4][:, 0:n], in_=t_[4][:, 0:n], func=AF.Exp, bias=C_GDT(p))
                        kb.A("activation", out=t_[4][:, 0:n], in_=t_[4][:, 0:n], func=AF.Ln, bias=ones[:, 0:1])
                        kb.V("tensor_scalar", out=logw[p][:, 0:n], in0=t_[4][:, 0:n], scalar1=negA[:, p:p + 1], scalar2=None,
                             op0=ALU.mult)
                        kb.A("activation", out=t_[5][:, 0:n], in_=t_[5][:, 0:n], func=AF.Sigmoid)
                        kb.V("tensor_tensor", out=Bv[p][:, 0:n], in0=t_[5][:, 0:n], in1=kn[:, 0:n], op=ALU.mult)
                        kb.V("tensor_scalar", out=A_[p][:, 0:n], in0=kn[:, 0:n], scalar1=-1.0, scalar2=None, op0=ALU.mult)
                    for ci in range(6):
                        kb.V("tensor_copy", out=t_[6][:, 0:3], in_=gdn_x[ci][:, n:n + 3])
                        kb.V("tensor_copy", out=gdn_x[ci][:, 0:3], in_=t_[6][:, 0:3])
                    cut(21)
                    for c in range(8):
                        kb.dma("sp", rw_x[c][:, 1:1 + n], CF[(16 + c) * 128:(17 + c) * 128, col0:col0 + n], rk=[("CF", "all")])
                        kb.V("tensor_tensor", out=xm[c][:, 0:n], in0=rw_x[c][:, 0:n], in1=rw_x[c][:, 1:1 + n], op=ALU.subtract)
                        kb.V("scalar_tensor_tensor", out=xm[c][:, 0:n], in0=xm[c][:, 0:n], scalar=C_MU(c), in1=rw_x[c][:, 1:1 + n],
                             op0=ALU.mult, op1=ALU.add)
                    kb.A("activation", out=t_[0][0:64, 0:n], in_=xm[6][0:64, 0:n], func=AF.Tanh)
                    kb.A("activation", out=t_[1][:, 0:n], in_=xm[7][:, 0:n], func=AF.Sigmoid)
                    for p in range(2):
                        P_ = 2 + p
                        kb.mm(pb[0][:, 0:n], wabr[0:64, p * 128:(p + 1) * 128], t_[0][0:64, 0:n])
                        kb.mm(pb[1][:, 0:n], wabr[64:128, p * 128:(p + 1) * 128], xm[6][64:128, 0:n])
                        kb.mm(pb[2][:, 0:n], gbr[:, p * 128:(p + 1) * 128], t_[1][:, 0:n])
                        kb.A("activation", out=logw[P_][:, 0:n], in_=pb[0][:, 0:n], func=AF.Sigmoid, bias=C_W0(p))
                        kb.V("tensor_scalar", out=logw[P_][:, 0:n], in0=logw[P_][:, 0:n], scalar1=-0.606531, scalar2=None,
                             op0=ALU.mult)
                        kb.A("activation", out=agate[:, 0:n], in_=pb[1][:, 0:n], func=AF.Sigmoid, bias=C_A0(p))
                        kb.cp(zg[P_][:, 0:n], pb[2][:, 0:n])
                        kb.V("tensor_scalar", out=t_[2][:, 0:n], in0=xm[2 + p][:, 0:n], scalar1=C_KK(p), scalar2=None, op0=ALU.mult)
                        kb.V("tensor_tensor", out=t_[3][:, 0:n], in0=t_[2][:, 0:n], in1=t_[2][:, 0:n], op=ALU.mult)
                        blocksum(pb[3], t_[3], n)
                        rsqrt_ps(t_[3][:, 0:n], pb[3][:, 0:n], 1.0, epsc)
                        kb.V("tensor_tensor", out=t_[2][:, 0:n], in0=t_[2][:, 0:n], in1=t_[3][:, 0:n], op=ALU.mult)
                        kb.V("tensor_scalar", out=A_[P_][:, 0:n], in0=t_[2][:, 0:n], scalar1=-1.0, scalar2=None, op0=ALU.mult)
                        kb.V("tensor_tensor", out=Bv[P_][:, 0:n], in0=t_[2][:, 0:n], in1=agate[:, 0:n], op=ALU.mult)
                        kb.V("tensor_scalar", out=t_[4][:, 0:n], in0=agate[:, 0:n], scalar1=-1.0, scalar2=C_KA(p),
                             op0=ALU.add, op1=ALU.mult)
                        kb.V("scalar_tensor_tensor", out=Kv[P_][:, 0:n], in0=t_[4][:, 0:n], scalar=1.0, in1=xm[2 + p][:, 0:n],
                             op0=ALU.add, op1=ALU.mult)
                        kb.V("tensor_copy", out=R_[P_][:, 0:n], in_=xm[p][:, 0:n])
                        kb.V("tensor_copy", out=Vv[P_][:, 0:n], in_=xm[4 + p][:, 0:n])
                    for c in range(8):
                        kb.V("tensor_copy", out=t_[6][:, 0:1], in_=rw_x[c][:, n:n + 1])
                        kb.V("tensor_copy", out=rw_x[c][:, 0:1], in_=t_[6][:, 0:1])
                    cut(22)
                    for P_ in range(NP):
                        lw = logw[P_]
                        kb.V("tensor_tensor_scan", out=cum[:, 0:n], data0=ones[:, 0:n], data1=lw[:, 0:n], initial=0.0,
                             op0=ALU.mult, op1=ALU.add)
                        c3 = cum[:, 0:n].rearrange("p (c i) -> p c i", i=16)
                        r3 = crel[:, 0:n].rearrange("p (c i) -> p c i", i=16)
                        kb.V("tensor_copy", out=r3[:, 0:1, :], in_=c3[:, 0:1, :])
                        if nch > 1:
                            kb.V("tensor_tensor", out=r3[:, 1:nch, :], in0=c3[:, 1:nch, :],
                                 in1=c3[:, 0:nch - 1, 15:16].to_broadcast([128, nch - 1, 16]), op=ALU.subtract)
                        kb.A("activation", out=t_[0][:, 0:n], in_=crel[:, 0:n], func=AF.Exp)
                        kb.V("tensor_tensor", out=Rt[P_][:, 0:n], in0=R_[P_][:, 0:n], in1=t_[0][:, 0:n], op=ALU.mult)
                        kb.V("tensor_copy", out=WC[P_][:, 0:nch], in_=t_[0][:, 0:n].rearrange("p (c i) -> p c i", i=16)[:, :, 15])
                        if P_ < 2:
                            kb.V("tensor_tensor", out=At[P_][:, 0:n], in0=A_[P_][:, 0:n], in1=t_[0][:, 0:n], op=ALU.mult)
                        else:
                            kb.V("tensor_tensor", out=t_[1][:, 0:n], in0=crel[:, 0:n], in1=lw[:, 0:n], op=ALU.subtract)
                            kb.A("activation", out=t_[1][:, 0:n], in_=t_[1][:, 0:n], func=AF.Exp)
                            kb.V("tensor_tensor", out=At[P_][:, 0:n], in0=A_[P_][:, 0:n], in1=t_[1][:, 0:n], op=ALU.mult)
                        kb.V("tensor_scalar", out=t_[2][:, 0:n], in0=crel[:, 0:n], scalar1=-1.0, scalar2=80.0,
                             op0=ALU.mult, op1=ALU.min)
                        kb.A("activation", out=t_[2][:, 0:n], in_=t_[2][:, 0:n], func=AF.Exp)
                        kb.V("tensor_tensor", out=Bt[P_][:, 0:n], in0=Bv[P_][:, 0:n], in1=t_[2][:, 0:n], op=ALU.mult)
                        if P_ >= 2:
                            kb.V("tensor_tensor", out=Kt[P_][:, 0:n], in0=Kv[P_][:, 0:n], in1=t_[2][:, 0:n], op=ALU.mult)
                        t33 = t_[3][:, 0:n].rearrange("p (c i) -> p c i", i=16)
                        kb.V("tensor_tensor", out=t33, in0=r3[:, :, 15:16].to_broadcast([128, nch, 16]), in1=r3, op=ALU.subtract)
                        kb.A("activation", out=t_[3][:, 0:n], in_=t_[3][:, 0:n], func=AF.Exp)
                        kb.V("tensor_tensor", out=Bp[P_][:, 0:n], in0=Bv[P_][:, 0:n], in1=t_[3][:, 0:n], op=ALU.mult)
                        if P_ >= 2:
                            kb.V("tensor_tensor", out=Kp[P_][:, 0:n], in0=Kv[P_][:, 0:n], in1=t_[3][:, 0:n], op=ALU.mult)
                    cut(23)
                    nsc = (n + 127) // 128
                    for sc in range(nsc):
                        o0 = sc * 128
                        m = min(128, n - o0)
                        ncs = m // 16
                        sl = slice(o0, o0 + m)
                        for P_ in range(NP):
                            rw = P_ >= 2
                            Ktp = Kt[P_] if rw else Bt[P_]
                            Kpp = Kp[P_] if rw else Bp[P_]
                            for hh in range(2):
                                ps_ = slice(64 * hh, 64 * hh + 64)
                                kb.mm(pb[0][0:m, 0:m], At[P_][ps_, sl], Bt[P_][ps_, sl])
                                kb.mm(pb[1][0:m, 0:m], Bt[P_][ps_, sl], At[P_][ps_, sl])
                                kb.V("tensor_tensor", out=Xs[0:m, 0:m], in0=pb[0][0:m, 0:m], in1=mstrict[0:m, 0:m], op=ALU.mult)
                                kb.V("tensor_tensor", out=XTs[0:m, 0:m], in0=pb[1][0:m, 0:m], in1=mstrictT[0:m, 0:m], op=ALU.mult)
                                if rw:
                                    kb.mm(pb[2][0:m, 0:m], At[P_][ps_, sl], Ktp[ps_, sl])
                                    kb.V("tensor_tensor", out=Aaks[0:m, 0:m], in0=pb[2][0:m, 0:m], in1=mstrict[0:m, 0:m], op=ALU.mult)
                                    aak = Aaks
                                else:
                                    aak = Xs
                                kb.V("tensor_tensor", out=PT[0][0:m, 0:m], in0=XTs[0:m, 0:m], in1=identf[0:m, 0:m], op=ALU.add)
                                yc, ytc, pc = Xs, XTs, 0
                                for lev in range(3):
                                    yn, ytn = Y[lev % 2], YT[lev % 2]
                                    kb.mm(pb[2][0:m, 0:m], ytc[0:m, 0:m], yc[0:m, 0:m])
                                    kb.cp(yn[0:m, 0:m], pb[2][0:m, 0:m])
                                    if lev < 2:
                                        kb.mm(pb[1][0:m, 0:m], yc[0:m, 0:m], ytc[0:m, 0:m])
                                        kb.cp(ytn[0:m, 0:m], pb[1][0:m, 0:m])
                                    kb.mm(pb[0][0:m, 0:m], yn[0:m, 0:m], PT[pc][0:m, 0:m])
                                    kb.V("tensor_tensor", out=PT[1 - pc][0:m, 0:m], in0=pb[0][0:m, 0:m], in1=PT[pc][0:m, 0:m], op=ALU.add)
                                    pc = 1 - pc
                                    yc, ytc = yn, ytn
                                TT = PT[pc]
                                kb.mm(pb[1][0:m, 0:m], aak[0:m, 0:m], TT[0:m, 0:m])
                                kb.cp(TAT[P_][hh][0:m, 0:m], pb[1][0:m, 0:m])
                                if hh == 0:
                                    kb.tr(pb[4][0:m, 0:128], At[P_][:, sl], identf[:])
                                    kb.cp(Atok[0:m, :], pb[4][0:m, 0:128])
                                    kb.tr(pb[4][0:m, 128:256], Vv[P_][:, sl], identf[:])
                                    kb.cp(Vtok[P_][0:m, :], pb[4][0:m, 128:256])
                                kb.mm(pb[2][ps_, 0:m], Atok[0:m, ps_], TT[0:m, 0:m])
                                kb.cp(WtT[P_][ps_, 0:m], pb[2][ps_, 0:m])
                                for c in range(ncs):
                                    cs_ = slice(o0 + c * 16, o0 + c * 16 + 16)
                                    kb.mm(pb[3][0:16, c * 16:(c + 1) * 16], Bt[P_][ps_, cs_], Rt[P_][ps_, cs_])
                                    if rw:
                                        kb.mm(pb[3][0:16, 128 + c * 16:128 + (c + 1) * 16], Ktp[ps_, cs_], Rt[P_][ps_, cs_])
                                kb.V("tensor_tensor", out=RBT[P_][hh][:, 0:m], in0=pb[3][0:16, 0:m], in1=mincl[:, 0:m], op=ALU.mult)
                                if rw:
                                    kb.V("tensor_tensor", out=RKT[P_][hh][:, 0:m], in0=pb[3][0:16, 128:128 + m], in1=mincl[:, 0:m], op=ALU.mult)
                            for c in range(ncs):
                                cs_ = slice(o0 + c * 16, o0 + c * 16 + 16)
                                kb.tr(pb[4][0:16, 0:128], Bp[P_][:, cs_], identf[:])
                                if rw:
                                    kb.tr(pb[4][0:16, 128:256], Kpp[:, cs_], identf[:])
                                kb.tr(pb[4][0:16, 256:384], Vv[P_][:, cs_], identf[:])
                                if rw:
                                    kb.cp(CM[P_][:, c, :], pb[4][0:16, 0:384])
                                else:
                                    kb.cp(CM[P_][:, c, 0:128], pb[4][0:16, 0:128])
                                    kb.cp(CM[P_][:, c, 256:384], pb[4][0:16, 256:384])
                        cut(24)
                        for c in range(ncs):
                            cg = sc * 8 + c
                            cs_ = slice(o0 + c * 16, o0 + c * 16 + 16)
                            cl = slice(c * 16, c * 16 + 16)
                            Hc, Hn = Hs[hcur], Hs[1 - hcur]
                            U = Usb[cg % 2]
                            uv = UV[cg % 2]
                            pu, pH, pO = pb[5], pb[6], pb[0]
                            for P_ in range(NP):
                                for hh in range(2):
                                    ps_ = slice(64 * hh, 64 * hh + 64)
                                    oc = slice(P_ * 128 + hh * 64, P_ * 128 + hh * 64 + 64)
                                    kb.mm(pu[0:16, oc], WtT[P_][ps_, cl], Hc[ps_, P_, :], start=True, stop=False)
                                    kb.mm(pu[0:16, oc], TAT[P_][hh][0:m, cl], Vtok[P_][0:m, ps_], start=False, stop=True)
                            cut(30)
                            kb.A("activation", out=U[:], in_=pu[0:16, :], func=AF.Copy)
                            for P_ in range(2):
                                kb.V("tensor_tensor", out=uv[:, P_ * 128:(P_ + 1) * 128], in0=pu[0:16, P_ * 128:(P_ + 1) * 128],
                                     in1=CM[P_][:, c, 256:384], op=ALU.add)
                            cut(31)
                            for P_ in range(NP):
                                rw = P_ >= 2
                                for hh in range(2):
                                    ps_ = slice(64 * hh, 64 * hh + 64)
                                    oc = slice(P_ * 128 + hh * 64, P_ * 128 + hh * 64 + 64)
                                    if rw:
                                        kb.mm(pH[ps_, P_ * 64:(P_ + 1) * 64], CM[P_][:, c, 128 + 64 * hh:128 + 64 * hh + 64],
                                              CM[P_][:, c, 256 + 64 * hh:256 + 64 * hh + 64], start=True, stop=False)
                                        kb.mm(pH[ps_, P_ * 64:(P_ + 1) * 64], CM[P_][:, c, 64 * hh:64 * hh + 64], U[:, oc],
                                              start=False, stop=True)
                                    else:
                                        kb.mm(pH[ps_, P_ * 64:(P_ + 1) * 64], CM[P_][:, c, 64 * hh:64 * hh + 64],
                                              uv[:, P_ * 128 + hh * 64:P_ * 128 + hh * 64 + 64], start=True, stop=True)
                                    kb.mm(pO[ps_, P_ * 16:(P_ + 1) * 16], Hc[ps_, P_, :], Rt[P_][ps_, cs_], start=True, stop=False)
                                    if rw:
                                        kb.mm(pO[ps_, P_ * 16:(P_ + 1) * 16], U[:, oc], RBT[P_][hh][:, cl], start=False, stop=False)
                                        kb.mm(pO[ps_, P_ * 16:(P_ + 1) * 16], CM[P_][:, c, 256 + 64 * hh:256 + 64 * hh + 64],
                                              RKT[P_][hh][:, cl], start=False, stop=True)
                                    else:
                                        kb.mm(pO[ps_, P_ * 16:(P_ + 1) * 16], uv[:, P_ * 128 + hh * 64:P_ * 128 + hh * 64 + 64],
                                              RBT[P_][hh][:, cl], start=False, stop=True)
                            cut(33)
                            wcb = None
                            for P_ in range(NP):
                                kb.V("scalar_tensor_tensor", out=Hn[:, P_, :], in0=Hc[:, P_, :], scalar=WC[P_][:, cg:cg + 1],
                                     in1=pH[:, P_ * 64:(P_ + 1) * 64], op0=ALU.mult, op1=ALU.add)
                            for P_ in range(NP):
                                kb.cp(OT[:, P_, o0 + c * 16:o0 + c * 16 + 16], pO[:, P_ * 16:(P_ + 1) * 16])
                            hcur = 1 - hcur
                    cut(25)
                    for p in range(2):
                        o_ = OT[:, p, :]
                        kb.V("tensor_tensor", out=t_[0][:, 0:n], in0=o_[:, 0:n], in1=o_[:, 0:n], op=ALU.mult)
                        blocksum(pb[1], t_[0], n)
                        rsqrt_ps(t_[0][:, 0:n], pb[1][:, 0:n], 1.0 / 64.0, epsc)
                        kb.V("scalar_tensor_tensor", out=t_[0][:, 0:n], in0=o_[:, 0:n], scalar=C_GON(p), in1=t_[0][:, 0:n],
                             op0=ALU.mult, op1=ALU.mult)
                        kb.A("activation", out=t_[1][:, 0:n], in_=zg[p][:, 0:n], func=AF.Silu)
                        ys = yst[p]
                        kb.V("tensor_tensor", out=ys[:, 0:n], in0=t_[0][:, 0:n], in1=t_[1][:, 0:n], op=ALU.mult)
                        store_ymt(512 + p * 128, ys, n, col0, "c")
                    for p in range(2):
                        P_ = 2 + p
                        o_ = OT[:, P_, :]
                        blocksum(pb[1], o_, n)
                        kb.V("scalar_tensor_tensor", out=t_[0][:, 0:n], in0=pb[1][:, 0:n], scalar=-1.0 / 64.0, in1=o_[:, 0:n],
                             op0=ALU.mult, op1=ALU.add)
                        kb.V("tensor_tensor", out=t_[1][:, 0:n], in0=t_[0][:, 0:n], in1=t_[0][:, 0:n], op=ALU.mult)
                        blocksum(pb[2], t_[1], n)
                        rsqrt_ps(t_[1][:, 0:n], pb[2][:, 0:n], 1.0 / 64.0, eps_gn)
                        kb.V("tensor_tensor", out=t_[0][:, 0:n], in0=t_[0][:, 0:n], in1=t_[1][:, 0:n], op=ALU.mult)
                        kb.V("tensor_scalar", out=t_[0][:, 0:n], in0=t_[0][:, 0:n], scalar1=C_LNW(p), scalar2=C_LNB(p),
                             op0=ALU.mult, op1=ALU.add)
                        kb.V("scalar_tensor_tensor", out=t_[2][:, 0:n], in0=R_[P_][:, 0:n], scalar=C_RK(p), in1=Kv[P_][:, 0:n],
                             op0=ALU.mult, op1=ALU.mult)
                        blocksum(pb[3], t_[2], n)
                        kb.V("tensor_tensor", out=t_[2][:, 0:n], in0=pb[3][:, 0:n], in1=Vv[P_][:, 0:n], op=ALU.mult)
                        kb.V("tensor_tensor", out=t_[0][:, 0:n], in0=t_[0][:, 0:n], in1=t_[2][:, 0:n], op=ALU.add)
                        ys = yst[p]
                        kb.V("tensor_tensor", out=ys[:, 0:n], in0=t_[0][:, 0:n], in1=zg[P_][:, 0:n], op=ALU.mult)
                        store_ymt(768 + p * 128, ys, n, col0, "d")
                Hf = Hs[hcur]
                for cb in range(2):
                    kb.dma("sp", o_lru_conv[l, si, cb * 128:(cb + 1) * 128, :], lru_x[cb][:, 0:3])
                    kb.dma("sp", o_lru_h[l, si, cb * 128:(cb + 1) * 128, :], lru_h[cb][:])
                for ci in range(6):
                    kb.dma("sp", o_gdn_conv[l, si, ci * 128:(ci + 1) * 128, :], gdn_x[ci][:, 0:3])
                for c in range(8):
                    kb.dma("sp", o_rwkv_shift[l, si, c * 128:(c + 1) * 128, :], rw_x[c][:, 0:1])
                for p in range(2):
                    kb.dma("sp", o_gdn_s[l, si, p * 128:(p + 1) * 128, :], Hf[:, p, :])
                    kb.dma("sp", o_rwkv_s[l, si, p * 128:(p + 1) * 128, :], Hf[:, 2 + p, :])
                if hcur == 1:
                    pass
                return hcur

            segs_p = [(0, 16)] + [(128 + j * SEG, min(SEG, NFT * 128 - j * SEG)) for j in range((NFT * 128 + SEG - 1) // SEG)]
            run_sequence(0, segs_p)
            for s_ in range(NS):
                run_sequence(1 + s_, [(16 + 16 * s_, 16)])

    only = cfg.get("ONLY", "AMRB")
    for l in range(NL):
        if "A" in only:
            phaseA(l)
            barrier()
        if "M" in only:
            phaseMLA(l)
            kb.stop = False
            barrier()
        if "R" in only:
            phaseREC(l)
            kb.stop = False
            barrier()
        if "B" in only:
            phaseB(l, l == NL - 1)
            barrier()
    kb.P.emit()
    glob.close()
    global _LAST_KB
    _LAST_KB = kb
    return nc


def _consts(T, NS, PAST):
    c = {}
    c["identf"] = np.eye(128, dtype=np.float32)
    i = np.arange(128)
    same = (i[:, None] // 16) == (i[None, :] // 16)
    c["mstrict"] = (same & (i[None, :] < i[:, None])).astype(np.float32)
    c["mstrictT"] = (same & (i[:, None] < i[None, :])).astype(np.float32)
    j = np.arange(16)
    m16 = (j[:, None] <= j[None, :]).astype(np.float32)
    c["mincl16"] = np.tile(m16, (1, 8))
    c["blockones"] = ((i[:, None] // 64) == (i[None, :] // 64)).astype(np.float32)
    c["attmask"] = (~((i[:, None] >= 64) & (i[None, :] < 64))).astype(np.float32)
    pos = np.zeros(T, np.float64)
    pos[0:16] = np.arange(16)
    for s in range(NS):
        pos[16 + 16 * s:32 + 16 * s] = PAST + np.arange(16)
    pos[128:] = 16 + np.arange(T - 128)
    inv = (10000.0 ** (-np.arange(0, 32, 2, dtype=np.float32) / 32)).astype(np.float32)
    ang = pos.astype(np.float32)[:, None] * inv[None, :]
    c["cs"] = np.concatenate([np.cos(ang), np.sin(ang)], axis=1).astype(np.float32)
    return c


def _layer_tables(inp):
    L = inp["norm_mix"].shape[0]
    colp = np.zeros((L, 128, NCOL), np.float32)
    for l in range(L):
        for cb in range(2):
            sl = slice(cb * 128, (cb + 1) * 128)
            for j in range(4):
                colp[l, :, cb * 8 + j] = inp["lru_conv_w"][l, j, sl]
            colp[l, :, cb * 8 + 4] = inp["lru_conv_b"][l, sl]
            colp[l, :, cb * 8 + 5] = inp["lru_ba"][l, sl]
            colp[l, :, cb * 8 + 6] = inp["lru_bx"][l, sl]
            colp[l, :, cb * 8 + 7] = inp["lru_lambda"][l, sl]
        for ci in range(6):
            for j in range(4):
                colp[l, :, 16 + ci * 4 + j] = inp["gdn_conv_w"][l, j, ci * 128:(ci + 1) * 128]
        for p in range(2):
            hsel = np.repeat(np.arange(2 * p, 2 * p + 2), 64)
            colp[l, :, 40 + p] = inp["gdn_a_log"][l, hsel]
            colp[l, :, 42 + p] = inp["gdn_dt_bias"][l, hsel]
            colp[l, :, 44 + p] = np.tile(inp["gdn_o_norm"][l], 2)
            sl = slice(p * 128, (p + 1) * 128)
            colp[l, :, 54 + p] = inp["rwkv_w0"][l, sl]
            colp[l, :, 56 + p] = inp["rwkv_a0"][l, sl]
            colp[l, :, 58 + p] = inp["rwkv_k_k"][l, sl]
            colp[l, :, 60 + p] = inp["rwkv_k_a"][l, sl]
            colp[l, :, 62 + p] = inp["rwkv_r_k"][l].reshape(-1)[sl]
            colp[l, :, 64 + p] = inp["rwkv_ln_w"][l, sl]
            colp[l, :, 66 + p] = inp["rwkv_ln_b"][l, sl]
        for c in range(8):
            colp[l, :, 46 + c] = inp["rwkv_mu"][l, c * 128:(c + 1) * 128]
    wbd = np.zeros((L, 4, 128, 128), np.float32)
    for l in range(L):
        for wi, nm in enumerate(("lru_wa", "lru_wx")):
            for cb in range(2):
                for hh in range(2):
                    wbd[l, wi * 2 + cb, hh * 64:(hh + 1) * 64, hh * 64:(hh + 1) * 64] = inp[nm][l, cb * 2 + hh]
    t = {}
    t["colp"] = colp
    t["lru_wbd"] = wbd
    t["rwkv_wab"] = np.ascontiguousarray(np.concatenate([inp["rwkv_w_b"], inp["rwkv_a_b"]], axis=1))
    t["rwkv_gb"] = np.ascontiguousarray(inp["rwkv_g_b"])
    t["gdn_ab"] = np.ascontiguousarray(inp["w_in"][:, :, 1952:1960])
    t["mla_rows"] = np.ascontiguousarray(np.concatenate(
        [inp["mla_q_a_norm"], inp["mla_kv_a_norm"], inp["mla_q_norm"], inp["mla_k_norm"]], axis=1))
    return t


def make_in_maps(inp, n_cores, cfg):
    NFT, NS, PAST = cfg["NFT"], cfg["NS"], cfg["PAST"]
    T = (NFT + 1) * 128
    L = inp["norm_mix"].shape[0]
    consts = _consts(T, NS, PAST)
    tabs = _layer_tables(inp)
    shared = {}
    for nm in ("norm_ffn1", "ffn1_w1", "ffn1_w2", "norm_mix", "w_in", "w_out", "norm_ffn2", "ffn2_w1", "ffn2_w2",
               "mla_w_uq", "mla_w_ukv"):
        shared[nm] = np.ascontiguousarray(inp[nm])
    shared["final_norm"] = np.ascontiguousarray(inp["final_norm"].reshape(1, D))
    shared.update(tabs)
    shared.update(consts)
    maps = []
    for c in range(n_cores):
        b = c // 2
        ss = slice(c * NS, (c + 1) * NS)
        m = dict(shared)
        x0 = np.zeros((T, D), np.float32)
        x0[0:16] = inp["meta_tokens"]
        x0[16:16 + 16 * NS] = inp["x_sample"][ss].reshape(NS * 16, D)
        x0[128:] = inp["x_prompt"][b]
        m["x0"] = x0
        m["cache_ckv"] = np.ascontiguousarray(inp["cache_mla_ckv"][:, ss])
        m["cache_kpe"] = np.ascontiguousarray(inp["cache_mla_kpe"][:, ss])
        m["st_lru_conv"] = np.ascontiguousarray(inp["state_lru_conv"][:, ss].transpose(0, 1, 3, 2))
        m["st_lru_h"] = np.ascontiguousarray(inp["state_lru_h"][:, ss].reshape(L, NS, 256, 1))
        m["st_gdn_conv"] = np.ascontiguousarray(inp["state_gdn_conv"][:, ss].transpose(0, 1, 3, 2))
        m["st_gdn_s"] = np.ascontiguousarray(inp["state_gdn_s"][:, ss].reshape(L, NS, 256, 64))
        m["st_rwkv_shift"] = np.ascontiguousarray(inp["state_rwkv_shift"][:, ss].reshape(L, NS, 1024, 1))
        m["st_rwkv_s"] = np.ascontiguousarray(inp["state_rwkv_s"][:, ss].transpose(0, 1, 2, 4, 3).reshape(L, NS, 256, 64))
        maps.append(m)
    return maps


def assemble(res, n_cores, cfg, L):
    NFT, NS = cfg["NFT"], cfg["NS"]
    NB = n_cores // 2
    SEQ = NFT * 128
    f32 = np.float32
    yp = np.zeros((NB, SEQ, D), f32)
    ys = np.zeros((n_cores * NS, 16, D), f32)
    p_ckv = np.zeros((L, NB, 16 + SEQ, 128), f32)
    p_kpe = np.zeros((L, NB, 16 + SEQ, 32), f32)
    s_ckv = np.zeros((L, n_cores * NS, 16, 128), f32)
    s_kpe = np.zeros((L, n_cores * NS, 16, 32), f32)

    def mk(nb):
        return [np.zeros((L, nb, 3, 256), f32), np.zeros((L, nb, 256), f32), np.zeros((L, nb, 3, 768), f32),
                np.zeros((L, nb, 4, 64, 64), f32), np.zeros((L, nb, 1024), f32), np.zeros((L, nb, 4, 64, 64), f32)]
    pst = mk(NB)
    sst = mk(n_cores * NS)

    def put(dst, bi, r, si):
        dst[0][:, bi] = r["o_lru_conv"][:L, si].transpose(0, 2, 1)
        dst[1][:, bi] = r["o_lru_h"][:L, si, :, 0]
        dst[2][:, bi] = r["o_gdn_conv"][:L, si].transpose(0, 2, 1)
        dst[3][:, bi] = r["o_gdn_s"][:L, si].reshape(L, 4, 64, 64)
        dst[4][:, bi] = r["o_rwkv_shift"][:L, si, :, 0]
        dst[5][:, bi] = r["o_rwkv_s"][:L, si].reshape(L, 4, 64, 64).transpose(0, 1, 3, 2)

    for c in range(n_cores):
        r = res[c]
        if c % 2 == 0:
            b = c // 2
            yp[b] = r["yout"][128:]
            p_ckv[:, b, 0:16] = r["o_ckv"][:L, 0:16]
            p_ckv[:, b, 16:] = r["o_ckv"][:L, 128:]
            p_kpe[:, b, 0:16] = r["o_kpe"][:L, 0:16]
            p_kpe[:, b, 16:] = r["o_kpe"][:L, 128:]
            put(pst, b, r, 0)
        for s in range(NS):
            g = c * NS + s
            ys[g] = r["yout"][16 + 16 * s:32 + 16 * s]
            s_ckv[:, g] = r["o_ckv"][:L, 16 + 16 * s:32 + 16 * s]
            s_kpe[:, g] = r["o_kpe"][:L, 16 + 16 * s:32 + 16 * s]
            put(sst, g, r, 1 + s)
    return (yp, ys, p_ckv, p_kpe, *pst, s_ckv, s_kpe, *sst)


_CFG = dict(NFT=32, NS=4, PAST=4096, NTS=11)
_NC_CACHE = {}


def kernel(**inputs):
    inp = {k: np.asarray(v) for k, v in inputs.items()}
    cfg = _CFG
    if "nc" not in _NC_CACHE:
        _NC_CACHE["nc"] = build(cfg)
    nc = _NC_CACHE["nc"]
    maps = make_in_maps(inp, 8, cfg)
    res = run_bass_kernel_spmd(nc, maps, core_ids=list(range(8)))
    return assemble(res.results, 8, cfg, DEPTH)
```
